# Optimizing a Trainium2 kernel written in Bass

```python
import math
import jax, jax.numpy as jnp
from jax import lax
import numpy as np

D_MODEL = 1024
BATCH = 32
SEQ = 256
DEPTH = 2
DEC_BATCH = 2
DEC_SEQ = 1024
PAST_LEN = 256

GRID_W = 64
MLA_HEADS = 4
MLA_NOPE = 64
MLA_ROPE = 32
MLA_V = 64
MLA_Q_LORA = 256
MLA_KV_LORA = 128
W_MLA = MLA_HEADS * MLA_V
W_CONV = 256
CONV_K = 3
DIFF_HEADS = 4
DIFF_QK = 64
DIFF_V = 2 * DIFF_QK
W_DIFF = DIFF_HEADS * DIFF_V
D_MIX = W_MLA + W_CONV + W_DIFF
IN_SIZES = (MLA_Q_LORA, MLA_KV_LORA, MLA_ROPE,
            W_CONV, W_CONV, W_CONV,
            DIFF_HEADS * 2 * DIFF_QK, DIFF_HEADS * 2 * DIFF_QK, W_DIFF,
            D_MIX)
D_IN = sum(IN_SIZES)
ROPE_THETA = 10000.0
Q_BLOCK = 128
EPS = 1e-6

kernel_name = "hybrid_mla_shortconv_diffattn_prefix_dit"


def _split_offsets():
    acc, offs = 0, []
    for s in IN_SIZES[:-1]:
        acc += s
        offs.append(acc)
    return offs


def rmsnorm(x, g):
    xf = x.astype(jnp.float32)
    y = xf * lax.rsqrt(jnp.mean(xf * xf, axis=-1, keepdims=True) + EPS)
    return (y * g.astype(jnp.float32)).astype(x.dtype)


def grid_positions(n):
    rows = n // GRID_W
    row = jnp.broadcast_to(jnp.arange(rows)[:, None], (rows, GRID_W)).reshape(n)
    col = jnp.broadcast_to(jnp.arange(GRID_W)[None, :], (rows, GRID_W)).reshape(n)
    return row, col


def axial_tables(n, rot_dim):
    half = rot_dim // 2
    inv = ROPE_THETA ** (-(jnp.arange(0, half, 2, dtype=jnp.float32) / half))
    row, col = grid_positions(n)
    ang_r = row.astype(jnp.float32)[:, None] * inv
    ang_c = col.astype(jnp.float32)[:, None] * inv
    ang = jnp.concatenate([ang_r, ang_r, ang_c, ang_c], axis=-1)
    return jnp.cos(ang), jnp.sin(ang)


def _rot_half_axial(x):
    half = x.shape[-1] // 2
    q = half // 2
    xr, xc = x[..., :half], x[..., half:]
    rh = lambda t: jnp.concatenate([-t[..., q:], t[..., :q]], axis=-1)
    return jnp.concatenate([rh(xr), rh(xc)], axis=-1)


def apply_rope(x, cos, sin):
    xf = x.astype(jnp.float32)
    return (xf * cos + _rot_half_axial(xf) * sin).astype(x.dtype)


def map_query_blocks(f, q):
    b, n = q.shape[:2]
    nb = n // Q_BLOCK
    qb = q.reshape((b, nb, Q_BLOCK) + q.shape[2:]).swapaxes(0, 1)
    out = lax.map(f, qb)
    return out.swapaxes(0, 1).reshape((b, n) + out.shape[3:])


def mla_branch(c_q, c_kv, k_rope, q_norm, w_uq, kv_norm, w_ukv, rope, ctx):
    b, n = c_q.shape[:2]
    q = (rmsnorm(c_q, q_norm) @ w_uq).reshape(b, n, MLA_HEADS, MLA_NOPE + MLA_ROPE)
    ckv = rmsnorm(c_kv, kv_norm)
    if rope is not None:
        cos, sin = rope
        q = jnp.concatenate([q[..., :MLA_NOPE],
                             apply_rope(q[..., MLA_NOPE:], cos[:, None, :], sin[:, None, :])], axis=-1)
        k_rope_pos = apply_rope(k_rope, cos, sin)
    else:
        k_rope_pos = k_rope
    ckv_all, krope_all = ckv, k_rope_pos
    if ctx is not None:
        ckv_ctx, krope_ctx = ctx
        ckv_all = jnp.concatenate([ckv_ctx.astype(ckv.dtype), ckv], axis=1)
        krope_all = jnp.concatenate([krope_ctx.astype(k_rope_pos.dtype), k_rope_pos], axis=1)
    m = ckv_all.shape[1]
    kv = (ckv_all @ w_ukv).reshape(b, m, MLA_HEADS, MLA_NOPE + MLA_V)
    k_nope, v = kv[..., :MLA_NOPE], kv[..., MLA_NOPE:]
    scale = (MLA_NOPE + MLA_ROPE) ** -0.5
    k_nope_f = k_nope.astype(jnp.float32)
    krope_f = krope_all.astype(jnp.float32)

    def blk(qb):
        qf = qb.astype(jnp.float32)
        s = (jnp.einsum('bqhd,bkhd->bhqk', qf[..., :MLA_NOPE], k_nope_f)
             + jnp.einsum('bqhr,bkr->bhqk', qf[..., MLA_NOPE:], krope_f))
        p = jax.nn.softmax(s * scale, axis=-1)
        return jnp.einsum('bhqk,bkhd->bqhd', p.astype(v.dtype), v)

    o = map_query_blocks(blk, q).reshape(b, n, W_MLA)
    return o, ckv, k_rope


def conv_branch(bg, cg, xc, conv_w):
    u = cg * xc
    n = u.shape[1]
    pad = CONV_K // 2
    up = jnp.pad(u, ((0, 0), (pad, pad), (0, 0)))
    y = sum(up[:, j:j + n] * conv_w[j] for j in range(CONV_K))
    return bg * y


def diff_branch(q, k, v, lam, subln, lambda_init, rope, ctx):
    b, n = q.shape[:2]
    q = q.reshape(b, n, DIFF_HEADS, 2, DIFF_QK)
    k = k.reshape(b, n, DIFF_HEADS, 2, DIFF_QK)
    v = v.reshape(b, n, DIFF_HEADS, DIFF_V)
    k_ctx_out = k
    if rope is not None:
        cos, sin = rope
        cq, sq = cos[:, None, None, :], sin[:, None, None, :]
        q = apply_rope(q, cq, sq)
        k = apply_rope(k, cq, sq)
    k_all, v_all = k, v
    if ctx is not None:
        k_ctx, v_ctx = ctx
        k_all = jnp.concatenate([k_ctx.astype(k.dtype), k], axis=1)
        v_all = jnp.concatenate([v_ctx.astype(v.dtype), v], axis=1)
    lf = lam.astype(jnp.float32)
    lam_full = jnp.exp(jnp.sum(lf[0] * lf[1])) - jnp.exp(jnp.sum(lf[2] * lf[3])) + lambda_init
    k_f = k_all.astype(jnp.float32)
    scale = DIFF_QK ** -0.5

    def blk(qb):
        s = jnp.einsum('bqhad,bkhad->bhaqk', qb.astype(jnp.float32), k_f) * scale
        p = jax.nn.softmax(s, axis=-1)
        a = p[:, :, 0] - lam_full * p[:, :, 1]
        return jnp.einsum('bhqk,bkhe->bqhe', a.astype(v_all.dtype), v_all)

    o = map_query_blocks(blk, q)
    o = rmsnorm(o, subln) * (1.0 - lambda_init)
    return o.reshape(b, n, W_DIFF), k_ctx_out, v


def trunk_layer(x, cond, layer_idx, p, rope_mla, rope_diff, ctx):
    mod = jax.nn.silu(cond) @ p['ada_w'] + p['ada_b']
    shift, scale, gate = jnp.split(mod, 3, axis=-1)
    h = rmsnorm(x, p['norm_g']) * (1.0 + scale) + shift
    parts = jnp.split(h @ p['w_in'], _split_offsets(), axis=-1)
    c_q, c_kv, k_rope, cb, cc, cx, dq, dk, dv, z = parts
    if ctx is None:
        ctx_mla, ctx_diff = None, None
    else:
        ctx_mla, ctx_diff = (ctx[0], ctx[1]), (ctx[2], ctx[3])
    o_mla, ckv, kr = mla_branch(c_q, c_kv, k_rope, p['q_norm'], p['w_uq'], p['kv_norm'], p['w_ukv'],
                                rope_mla, ctx_mla)
    o_conv = conv_branch(cb, cc, cx, p['conv_w'])
    lambda_init = 0.8 - 0.6 * math.exp(-0.3 * layer_idx)
    o_diff, k_d, v_d = diff_branch(dq, dk, dv, p['lam'], p['subln'], lambda_init, rope_diff, ctx_diff)
    mix = jnp.concatenate([o_mla, o_conv, o_diff], axis=-1) * jax.nn.silu(z)
    x = x + gate * (mix @ p['w_out'])
    return x, (ckv, kr, k_d, v_d)


def setup_inputs(seed: int = 0) -> dict:
    key = jax.random.key(seed)
    ks = jax.random.split(key, 24)
    f32 = jnp.float32
    nrm = lambda k, shape, s: jax.random.normal(k, shape, f32) * s
    gain = lambda k, shape: 1.0 + 0.1 * jax.random.normal(k, shape, f32)
    return {
        "x_prompt": nrm(ks[0], (BATCH, SEQ, D_MODEL), 1.0),
        "x_sample": nrm(ks[1], (DEC_BATCH, DEC_SEQ, D_MODEL), 1.0),
        "c": nrm(ks[2], (DEC_BATCH, D_MODEL), 1.0),
        "cache_mla_ckv": nrm(ks[3], (DEC_BATCH, DEPTH, PAST_LEN, MLA_KV_LORA), 1.0),
        "cache_mla_krope": nrm(ks[4], (DEC_BATCH, DEPTH, PAST_LEN, MLA_ROPE), 1.0),
        "cache_diff_k": nrm(ks[5], (DEC_BATCH, DEPTH, PAST_LEN, DIFF_HEADS, 2, DIFF_QK), 1.0),
        "cache_diff_v": nrm(ks[6], (DEC_BATCH, DEPTH, PAST_LEN, DIFF_HEADS, DIFF_V), 1.0),
        "c_ctx": nrm(ks[7], (D_MODEL,), 1.0),
        "norm_g": gain(ks[8], (DEPTH, D_MODEL)),
        "ada_w": nrm(ks[9], (DEPTH, D_MODEL, 3 * D_MODEL), D_MODEL ** -0.5),
        "ada_b": nrm(ks[10], (DEPTH, 3 * D_MODEL), 0.02),
        "w_in": nrm(ks[11], (DEPTH, D_MODEL, D_IN), D_MODEL ** -0.5),
        "mla_q_norm": gain(ks[12], (DEPTH, MLA_Q_LORA)),
        "w_uq": nrm(ks[13], (DEPTH, MLA_Q_LORA, MLA_HEADS * (MLA_NOPE + MLA_ROPE)), MLA_Q_LORA ** -0.5),
        "mla_kv_norm": gain(ks[14], (DEPTH, MLA_KV_LORA)),
        "w_ukv": nrm(ks[15], (DEPTH, MLA_KV_LORA, MLA_HEADS * (MLA_NOPE + MLA_V)), MLA_KV_LORA ** -0.5),
        "conv_w": nrm(ks[16], (DEPTH, CONV_K, W_CONV), CONV_K ** -0.5),
        "diff_lambda": nrm(ks[17], (DEPTH, 4, DIFF_QK), 0.1),
        "diff_subln": gain(ks[18], (DEPTH, DIFF_V)),
        "w_out": nrm(ks[19], (DEPTH, D_MIX, D_MODEL), D_MIX ** -0.5),
        "final_norm": gain(ks[20], (D_MODEL,)),
    }


def reference(x_prompt, x_sample, c, cache_mla_ckv, cache_mla_krope, cache_diff_k, cache_diff_v,
              c_ctx, norm_g, ada_w, ada_b, w_in, mla_q_norm, w_uq, mla_kv_norm, w_ukv,
              conv_w, diff_lambda, diff_subln, w_out, final_norm):
    n_lat = x_sample.shape[1]
    rope_mla = axial_tables(n_lat, MLA_ROPE)
    rope_diff = axial_tables(n_lat, DIFF_QK)
    cond_ctx = c_ctx[None, None, :]
    cond_lat = c[:, None, :]
    h_ctx, h_lat = x_prompt, x_sample
    ckvs, krs, dks, dvs = [], [], [], []
    for l in range(DEPTH):
        p = dict(norm_g=norm_g[l], ada_w=ada_w[l], ada_b=ada_b[l], w_in=w_in[l],
                 q_norm=mla_q_norm[l], w_uq=w_uq[l], kv_norm=mla_kv_norm[l], w_ukv=w_ukv[l],
                 conv_w=conv_w[l], lam=diff_lambda[l], subln=diff_subln[l], w_out=w_out[l])
        h_ctx, (ckv, kr, dk, dv) = trunk_layer(h_ctx, cond_ctx, l, p, None, None, None)
        ckvs.append(ckv)
        krs.append(kr)
        dks.append(dk)
        dvs.append(dv)
        ctx = (cache_mla_ckv[:, l], cache_mla_krope[:, l], cache_diff_k[:, l], cache_diff_v[:, l])
        h_lat, _ = trunk_layer(h_lat, cond_lat, l, p, rope_mla, rope_diff, ctx)
    y_prompt = rmsnorm(h_ctx, final_norm)
    y_sample = rmsnorm(h_lat, final_norm)
    return (y_prompt, y_sample, jnp.stack(ckvs, axis=1), jnp.stack(krs, axis=1),
            jnp.stack(dks, axis=1), jnp.stack(dvs, axis=1))
```

```python
import contextlib
import math
import numpy as np
import concourse.bass as bass
import concourse.mybir as mybir
from concourse.bass_utils import run_bass_kernel_spmd

F32 = mybir.dt.float32
BF16 = mybir.dt.bfloat16
AF = mybir.ActivationFunctionType
ALU = mybir.AluOpType
AX = mybir.AxisListType

EPS = 1e-6
NCORES = 8
DEPTH = 2
D = 1024
CH = [(0, 512), (512, 512), (1024, 160), (1184, 512), (1696, 512), (2208, 512), (2720, 512), (3232, 512)]
CPAD = 3604
O_KT, O_KD, O_VM, O_VD, O_UB = 0, 1024, 2048, 2568, 3600
_DBG = {}
MLA_SCALE = 96 ** -0.5
DIFF_SCALE = 64 ** -0.5


class Res:
    __slots__ = ("name", "lw", "rd", "excl", "sem")

    def __init__(self, name, excl=False):
        self.name = name
        self.lw = None
        self.rd = {}
        self.excl = excl
        self.sem = None


class FW:
    def __init__(self, nc):
        self.nc = nc
        self.es = contextlib.ExitStack()
        self.engs = {"pe": nc.tensor, "act": nc.scalar, "dve": nc.vector, "pool": nc.gpsimd, "sp": nc.sync}
        self.sems = {}
        self.cnt = {}
        self.known = {k: {} for k in self.engs}
        for k in ("pe", "act", "dve", "pool"):
            self.sems[k] = self.es.enter_context(nc.semaphore("sem_" + k))
            self.cnt[k] = 0
        self.ndma = 0

    def sbuf(self, name, shape, dt):
        return self.es.enter_context(self.nc.sbuf_tensor(name, list(shape), dt))

    def psum(self, name, shape, dt):
        return self.es.enter_context(self.nc.psum_tensor(name, list(shape), dt))

    def dma_sem(self):
        self.ndma += 1
        key = "dma%d" % self.ndma
        self.sems[key] = self.es.enter_context(self.nc.semaphore("sem_" + key))
        self.cnt[key] = 0
        return key

    def _waits(self, e, reads, writes, same_ok=False):
        deps = {}

        def need(kv):
            if kv is not None and deps.get(kv[0], 0) < kv[1]:
                deps[kv[0]] = kv[1]
        for r in reads:
            need(r.lw)
            if r.excl:
                for kv in r.rd.items():
                    need(kv)
        for w in writes:
            need(w.lw)
            for kv in w.rd.items():
                need(kv)
        eng = self.engs[e]
        for key, val in deps.items():
            if key == e and same_ok:
                continue
            if self.known[e].get(key, 0) >= val:
                continue
            eng.wait_ge(self.sems[key], val)
            self.known[e][key] = val

    def _record(self, reads, writes, key, ticket):
        for r in reads:
            if r.excl:
                r.lw = (key, ticket)
                r.rd = {}
            elif r.rd.get(key, 0) < ticket:
                r.rd[key] = ticket
        for w in writes:
            w.lw = (key, ticket)
            w.rd = {}

    def op(self, e, fn, reads=(), writes=(), signal=True):
        self._waits(e, reads, writes, same_ok=(e == "pe"))
        ins = fn(self.engs[e])
        if signal:
            self.cnt[e] += 1
            ins.then_inc(self.sems[e], 1)
            self._record(reads, writes, e, self.cnt[e])
        else:
            self._record(reads, writes, e, self.cnt[e] + 1)
        return ins

    def dma(self, q, out, in_, reads=(), writes=(), sem=None):
        self._waits(q, reads, writes)
        ins = self.engs[q].dma_start(out=out, in_=in_)
        self.cnt[sem] += 16
        ins.then_inc(self.sems[sem], 16)
        self._record(reads, writes, sem, self.cnt[sem])
        return ins

    def wait_all(self, e, keys):
        for k in keys:
            if self.cnt[k] > 0 and self.known[e].get(k, 0) < self.cnt[k]:
                self.engs[e].wait_ge(self.sems[k], self.cnt[k])
                self.known[e][k] = self.cnt[k]


class RR:
    def __init__(self, items):
        self.items = items
        self.i = 0

    def next(self):
        x = self.items[self.i % len(self.items)]
        self.i += 1
        return x


def build_nc():
    nc = bass.Bass("TRN2", target_bir_lowering=False)

    def din(name, shape, dt=F32):
        return nc.dram_tensor(name, list(shape), dt, kind="ExternalInput").ap()

    def dout(name, shape):
        return nc.dram_tensor(name, list(shape), F32, kind="ExternalOutput").ap()

    xp = din("xp", [4, 256, D])
    xs = din("xs", [256, D])
    condT = din("condT", [128, 8, 2])
    ckv_c = din("ckv_c", [2, 256, 128])
    kr_c = din("kr_c", [2, 256, 32])
    dk_c = din("dk_c", [2, 256, 512])
    dv_c = din("dv_c", [2, 256, 512])
    gT_d = din("normgT", [2, 128, 8])
    fn_d = din("final_norm", [1, D])
    ada_w = din("ada_w_s", [2, D, 768])
    ada_bT = din("ada_bT_s", [2, 128, 6])
    w_in = din("w_in_p", [2, D, 3744])
    qnT = din("qnT", [2, 128, 2])
    w_uq = din("w_uq", [2, 256, 384])
    kvn = din("kv_norm", [2, 1, 128])
    w_ukv = din("w_ukv", [2, 128, 512])
    cwT = din("conv_wT", [2, 128, 6])
    lam_d = din("diff_lambda", [2, 1, 256])
    sub_d = din("diff_subln", [2, 1, 128])
    w_out = din("w_out", [2, D, D])
    ident_d = din("ident", [128, 128])
    ropd_d = din("rope_d", [128, 2, 2, 64])
    ropm_d = din("rope_m", [128, 2, 2, 32])
    sel_d = din("halo_sel", [128, 8])

    y_p = dout("y_p", [4, 256, D])
    y_s = dout("y_s", [256, D])
    o_ckv = dout("o_ckv", [4, 2, 256, 128])
    o_kr = dout("o_kr", [4, 2, 256, 32])
    o_dk = dout("o_dk", [4, 2, 256, 512])
    o_dv = dout("o_dv", [4, 2, 256, 512])

    mod_loc = nc.dram_tensor("mod_loc", [128, 24], F32)
    mod_all = nc.dram_tensor("mod_all", [512, 24], F32)
    kv_loc = [nc.dram_tensor("kv_loc%d" % l, [128, CPAD], BF16) for l in range(2)]
    kv_all = [nc.dram_tensor("kv_all%d" % l, [512, CPAD], BF16) for l in range(2)]

    fw = FW(nc)
    with fw.es:
        S = fw.sbuf
        X = [S("X%d" % i, (128, D), F32) for i in range(6)]
        rX = [Res("X%d" % i) for i in range(6)]
        for r in rX:
            r.sem = fw.dma_sem()
        hT = S("hT", (128, 6, 8, 128), BF16)
        rhT = [[Res("hT%d_%d" % (i, k)) for k in range(8)] for i in range(6)]
        Wc = [S("Wc%d" % i, (128, 8, 512), BF16) for i in range(2)]
        rWc = [Res("Wc%d" % i) for i in range(2)]
        for r in rWc:
            r.sem = fw.dma_sem()
        Wo = [S("Wo%d" % l, (128, 8, D), BF16) for l in range(2)]
        rWo = [Res("Wo%d" % l) for l in range(2)]
        for r in rWo:
            r.sem = fw.dma_sem()
        Abc = S("Abc", (128, D), F32); rAbc = Res("Abc")
        gateB = [S("gateB%d" % i, (128, D), F32) for i in range(2)]; rgate = [Res("gateB%d" % i) for i in range(2)]
        ATt = S("ATt", (128, 2, 2, 8), F32); rAT = Res("ATt")
        rAbc.sem = fw.dma_sem()
        ph = [S("ph%d" % i, (128, 256), F32) for i in range(4)]
        rph = [Res("ph%d" % i) for i in range(4)]
        phRR = RR([0, 1, 2, 3])
        hb = [S("hb%d" % i, (128, D), BF16) for i in range(1)]
        rhb = [Res("hb%d" % i) for i in range(1)]
        hbRR = RR([0])
        junk = S("junk", (128, 256), BF16); rjunk = Res("junk")
        NS = 10
        KT = S("KT", (128, 4, NS * 128), BF16); rKT = [Res("KT%d" % i) for i in range(NS)]
        KdT = S("KdT", (128, 4, NS * 128), BF16); rKdT = [Res("KdT%d" % i) for i in range(NS)]
        Vm = S("Vm", (128, NS, 4, 65), BF16); rVm = [Res("Vm%d" % i) for i in range(NS)]
        Vd = S("Vd", (128, NS, 4, 129), BF16); rVd = [Res("Vd%d" % i) for i in range(NS)]
        for lst in (rKT, rKdT, rVm, rVd):
            for r in lst:
                r.sem = fw.dma_sem()
        QT = S("QT", (128, 4, 512), BF16); rQT = [Res("QT%d" % i) for i in range(4)]
        QdT = S("QdT", (128, 4, 512), BF16); rQdT = [Res("QdT%d" % i) for i in range(4)]
        E = [S("E%d" % i, (128, NS, 256), BF16) for i in range(2)]
        rE = [Res("E%d" % i) for i in range(2)]
        ERR = RR([0, 1])
        E.append(S("E3", (128, 2, 256), BF16)); rE.append(Res("E3"))
        ERR3 = RR([0, 1, 2])
        sRR3 = RR([4, 5, 3])
        uT = S("uT", (128, 3, 2, 258), BF16)
        ruT = [Res("uT%d" % i) for i in range(6)]
        ruH = [Res("uH%d" % i) for i in range(3)]
        gT = S("gT", (128, 2, 512), BF16)
        rgT = [Res("gT%d" % i) for i in range(4)]
        SZ = S("SZ", (128, 4, 768), BF16); rSZ = [Res("SZ%d" % i) for i in range(4)]
        mix = [S("mix%d" % i, (128, 768), BF16) for i in range(4)]
        rmix = [[Res("mix%d_%d" % (i, j)) for j in range(5)] for i in range(4)]
        mixT = [S("mixT%d" % i, (128, 6, 128), BF16) for i in range(2)]
        rmixT = [Res("mixT%d" % i) for i in range(2)]
        scr = [S("scr%d" % i, (128, 512), F32) for i in range(4)]
        rscr = [Res("scr%d" % i) for i in range(4)]
        for r in rscr:
            r.sem = fw.dma_sem()
        scrRR = RR([0, 1, 2, 3])
        stgc = [S("stgc%d" % i, (128, 160), F32) for i in range(2)]
        rstgc = [Res("stgc%d" % i) for i in range(2)]
        for r in rstgc:
            r.sem = fw.dma_sem()
        stgcRR = RR([0, 1])
        b512 = [S("b512_%d" % i, (128, 512), BF16) for i in range(2)]
        rb512 = [Res("b512_%d" % i) for i in range(2)]
        b512RR = RR([0, 1])
        b256 = [S("b256_%d" % i, (128, 256), BF16) for i in range(2)]
        rb256 = [Res("b256_%d" % i) for i in range(2)]
        b256RR = RR([0, 1])
        ckvb = [S("ckvb%d" % i, (128, 128), BF16) for i in range(2)]
        rckvb = [Res("ckvb%d" % i) for i in range(2)]
        ckvT = [S("ckvT%d" % i, (128, 128), BF16) for i in range(2)]
        rckvT = [Res("ckvT%d" % i) for i in range(2)]
        ckvRR = RR([0, 1])
        krb = [S("krb%d" % i, (128, 32), BF16) for i in range(2)]
        rkrb = [Res("krb%d" % i) for i in range(2)]
        cqT = [S("cqT%d" % i, (128, 2, 128), BF16) for i in range(2)]
        rcqT = [Res("cqT%d" % i) for i in range(2)]
        cqRR = RR([0, 1])
        Kf = [S("Kf%d" % i, (128, 4, 96), BF16) for i in range(2)]
        rKf = [Res("Kf%d" % i) for i in range(2)]
        KfRR = RR([0, 1])
        Qf = [S("Qf%d" % i, (128, 4, 96), BF16) for i in range(2)]
        rQf = [Res("Qf%d" % i) for i in range(2)]
        QfRR = RR([0, 1])
        ST = S("ST", (128, 512), F32)
        identf = S("identf", (128, 128), F32); ridf = Res("identf"); ridf.sem = fw.dma_sem()
        identb = S("identb", (128, 128), BF16); ridb = Res("identb")
        onesf = S("onesf", (128, 128), F32); rones = Res("onesf")
        cst = S("cst", (128, 4), F32); rcst = Res("cst")
        ropd = S("ropd", (128, 2, 2, 64), F32); rropd = Res("ropd"); rropd.sem = fw.dma_sem()
        ropm = S("ropm", (128, 2, 2, 32), F32); rropm = Res("ropm"); rropm.sem = fw.dma_sem()
        selh = S("selh", (128, 8), F32); rsel = Res("selh"); rsel.sem = fw.dma_sem()
        cT = S("cT", (128, 8, 2), F32); rcT = Res("cT"); rcT.sem = fw.dma_sem()
        scT = S("scT", (128, 8, 2), BF16); rscT = Res("scT")
        ctmp = S("ctmp", (128, 8, 2), F32); rctmp = Res("ctmp")
        modT = [S("modT%d" % l, (128, 24, 2), F32) for l in range(2)]; rmodT = [Res("modT%d" % l) for l in range(2)]
        abT = [S("abT%d" % l, (128, 6), F32) for l in range(2)]; rabT = [Res("abT%d" % l) for l in range(2)]
        gTn = [S("gTn%d" % l, (128, 8), F32) for l in range(2)]; rgTn = [Res("gTn%d" % l) for l in range(2)]
        qn = [S("qn%d" % l, (128, 2), F32) for l in range(2)]; rqn = [Res("qn%d" % l) for l in range(2)]
        wuqf = gateB[0][:, 0:768].rearrange("p (k n) -> p k n", k=2); rwuqf = rgate[0]; rgate[0].sem = fw.dma_sem()
        Wuq = [S("Wuq%d" % l, (128, 2, 384), BF16) for l in range(2)]; rWuq = [Res("Wuq%d" % l) for l in range(2)]
        Wukv = [S("Wukv%d" % l, (128, 512), BF16) for l in range(2)]; rWukv = [Res("Wukv%d" % l) for l in range(2)]
        kvnb = [S("kvnb%d" % l, (128, 128), F32) for l in range(2)]; rkvnb = [Res("kvnb%d" % l) for l in range(2)]
        subb = [S("subb%d" % l, (128, 128), F32) for l in range(2)]; rsubb = [Res("subb%d" % l) for l in range(2)]
        cw = [S("cw%d" % l, (128, 6), F32) for l in range(2)]; rcw = [Res("cw%d" % l) for l in range(2)]
        lamt = [S("lamt%d" % l, (128, 8), F32) for l in range(2)]; rlamt = [Res("lamt%d" % l) for l in range(2)]
        for lst in (rabT, rgTn, rqn, rWukv, rkvnb, rsubb, rcw):
            for r in lst:
                r.sem = fw.dma_sem()
        vT8 = S("vT8", (128, 8), F32); rvT8 = Res("vT8")
        modp = S("modp", (128, 2, 6, 2), F32); rmodp = Res("modp")
        moda = S("moda", (128, 4, 24), F32); rmoda = Res("moda"); rmoda.sem = fw.dma_sem()
        rmodloc = Res("modloc"); rmodloc.sem = fw.dma_sem()
        cc_mod = fw.es.enter_context(nc.semaphore("cc_mod"))
        uhal = S("uhal", (128, 4, 2, 2), BF16); ruhal = Res("uhal"); ruhal.sem = fw.dma_sem()
        uhf = S("uhf", (128, 4, 2, 2), F32); ruhf = Res("uhf")
        uacc = S("uacc", (128, 2, 2), F32); ruacc = Res("uacc")
        ubnd = S("ubnd", (128, 2, 2), BF16); rubnd = Res("ubnd")

        PS = [fw.psum("ps%d" % i, (128, 512), F32) for i in range(8)]
        rPS = [Res("ps%d" % i, excl=True) for i in range(8)]
        inRR = RR([0, 1])
        ipRR = RR([0, 1, 6, 7])
        trRR = RR([2, 3])
        sRR = RR([4, 5])

        def pst(b):
            return PS[b][:].bitcast(BF16).rearrange("p (a c) -> p a c", c=128)

        NBLK = 128
        rST = [Res("st%d" % i) for i in range(NBLK)]
        st_state = {"n": 0}

        def st_reset():
            pass

        def st_alloc(w=1):
            assert w <= 4
            i = st_state["n"] % NBLK
            st_state["n"] += 1
            return ST[:, 4 * i:4 * i + w], rST[i]

        def act(fn, reads, writes):
            return fw.op("act", fn, reads, writes)

        def dve(fn, reads, writes):
            return fw.op("dve", fn, reads, writes)

        def pe(fn, reads, writes, signal=True):
            return fw.op("pe", fn, reads, writes, signal)

        def rstd_from_ss(ss, rss, n, w=1):
            la, rla = st_alloc(w)
            rs, rrs = st_alloc(w)
            act(lambda e: e.activation(out=la, in_=ss, func=AF.Ln, scale=1.0 / n, bias=cst[:, 0:1]), [rss, rcst], [rla])
            act(lambda e: e.activation(out=rs, in_=la, func=AF.Exp, scale=-0.5), [rla], [rrs])
            return rs, rrs

        def sigmoid_from(ps_ap, rps, out_ap, rout, w):
            act(lambda e: e.activation(out=out_ap, in_=ps_ap, func=AF.Exp, scale=-1.0), [rps], [rout])
            act(lambda e: e.activation(out=out_ap, in_=out_ap, func=AF.Ln, bias=cst[:, 1:2]), [rout, rcst], [rout])
            act(lambda e: e.activation(out=out_ap, in_=out_ap, func=AF.Exp, scale=-1.0), [rout], [rout])

        def transposes(srcs, rsrc, nparts_out=128):
            b = trRR.next()
            v = pst(b)
            n = len(srcs)
            for i, s in enumerate(srcs):
                F = s.shape[-1]
                pe(lambda e, i=i, s=s, F=F: e.transpose(out=v[0:F, i, :], in_=s, identity=identb[:]),
                   list(rsrc) + [ridb], [rPS[b]], signal=(i == n - 1))
            return b, v

        def load_wo(l):
            fw.dma("pool", Wo[l][:], w_out[l].rearrange("(k p) n -> p k n", p=128), writes=[rWo[l]], sem=rWo[l].sem)

        fw.dma("sp", identf[:], ident_d, writes=[ridf], sem=ridf.sem)
        dve(lambda e: e.tensor_copy(out=identb[:], in_=identf[:]), [ridf], [ridb])
        dve(lambda e: e.memset(onesf[:], 1.0), [], [rones])
        dve(lambda e: e.memset(cst[:, 0:1], EPS), [], [rcst])
        dve(lambda e: e.memset(cst[:, 1:2], 1.0), [], [rcst])
        fw.dma("sp", ropd[:], ropd_d, writes=[rropd], sem=rropd.sem)
        fw.dma("sp", ropm[:], ropm_d, writes=[rropm], sem=rropm.sem)
        fw.dma("sp", selh[:], sel_d, writes=[rsel], sem=rsel.sem)
        fw.dma("sp", cT[:], condT, writes=[rcT], sem=rcT.sem)
        fw.op("pool", lambda e: e.memset(Vm[:], 1.0), [], rVm)
        fw.op("pool", lambda e: e.memset(Vd[:], 1.0), [], rVd)
        fw.op("pool", lambda e: e.memset(KT[:], 0.0), [], rKT)
        fw.op("pool", lambda e: e.memset(KdT[:], 0.0), [], rKdT)
        fw.op("pool", lambda e: e.memset(QT[:], 0.0), [], rQT)
        fw.op("pool", lambda e: e.memset(uT[:], 0.0), [], ruT + ruH)
        dve(lambda e: e.memset(ubnd[:], 0.0), [], [rubnd])
        st_reset()
        sigmoid_from(cT[:], rcT, ctmp[:], rctmp, 16)
        dve(lambda e: e.tensor_tensor(out=scT[:], in0=cT[:], in1=ctmp[:], op=ALU.mult), [rcT, rctmp], [rscT])

        for l in range(2):
            fw.dma("sp", abT[l][:], ada_bT[l], writes=[rabT[l]], sem=rabT[l].sem)
            fw.dma("sp", gTn[l][:], gT_d[l], writes=[rgTn[l]], sem=rgTn[l].sem)
            fw.dma("sp", qn[l][:], qnT[l], writes=[rqn[l]], sem=rqn[l].sem)
            fw.dma("sp", cw[l][:], cwT[l], writes=[rcw[l]], sem=rcw[l].sem)
            fw.dma("sp", kvnb[l][:], kvn[l].partition_broadcast(128).rearrange("p o n -> p (o n)"),
                   writes=[rkvnb[l]], sem=rkvnb[l].sem)
            fw.dma("sp", subb[l][:], sub_d[l].partition_broadcast(128).rearrange("p o n -> p (o n)"),
                   writes=[rsubb[l]], sem=rsubb[l].sem)
            li = 0.8 - 0.6 * math.exp(-0.3 * l)
            dve(lambda e, l=l, li=li: e.tensor_scalar(out=subb[l][:], in0=subb[l][:], scalar1=1.0 - li, scalar2=None,
                                                     op0=ALU.mult), [rsubb[l]], [rsubb[l]])
            fw.dma("pool", Wukv[l][:], w_ukv[l], writes=[rWukv[l]], sem=rWukv[l].sem)
            fw.dma("sp", wuqf, w_uq[l].rearrange("(k p) n -> p k n", p=128), writes=[rwuqf], sem=rwuqf.sem)
            for kc in range(2):
                dve(lambda e, l=l, kc=kc: e.tensor_scalar(out=Wuq[l][:, kc, :], in0=wuqf[:, kc, :],
                                                         scalar1=qn[l][:, kc:kc + 1], scalar2=None, op0=ALU.mult),
                    [rwuqf, rqn[l]], [rWuq[l]])
            fw.dma("sp", scr[1][:, 0:256], lam_d[l].partition_broadcast(128).rearrange("p o n -> p (o n)"),
                   writes=[rscr[1]], sem=rscr[1].sem)
            rlamb = rscr[1]
            lv = scr[1][:, 0:256].rearrange("p (a b d) -> p a b d", a=2, b=2)
            sc0 = scr[0]
            dve(lambda e: e.tensor_tensor(out=sc0[:, 0:128].rearrange("p (a d) -> p a d", a=2), in0=lv[:, :, 0, :],
                                          in1=lv[:, :, 1, :], op=ALU.mult), [rlamb], [rscr[0]])
            dve(lambda e, l=l: e.reduce_sum(out=lamt[l][:, 0:2], in_=sc0[:, 0:128].rearrange("p (a d) -> p a d", a=2),
                                            axis=AX.X), [rscr[0]], [rlamt[l]])
            act(lambda e, l=l: e.activation(out=lamt[l][:, 0:2], in_=lamt[l][:, 0:2], func=AF.Exp), [rlamt[l]], [rlamt[l]])
            dve(lambda e, l=l: e.tensor_tensor(out=lamt[l][:, 2:3], in0=lamt[l][:, 1:2], in1=lamt[l][:, 0:1],
                                               op=ALU.subtract), [rlamt[l]], [rlamt[l]])
            dve(lambda e, l=l, li=li: e.tensor_scalar(out=lamt[l][:, 3:4], in0=lamt[l][:, 2:3], scalar1=-li,
                                                     scalar2=None, op0=ALU.add), [rlamt[l]], [rlamt[l]])
            if l == 0:
                bmod = inRR.next()
            mps = PS[bmod][:, l * 12:(l + 1) * 12].rearrange("p (j c) -> p j c", c=2)
            for ch in range(2):
                wb = ch % 2
                fw.dma("pool", Wc[wb][:, :, 0:384], ada_w[l][:, ch * 384:(ch + 1) * 384].rearrange("(k p) n -> p k n", p=128),
                       writes=[rWc[wb]], sem=rWc[wb].sem)
                for jj in range(3):
                    j = ch * 3 + jj
                    for k in range(8):
                        pe(lambda e, k=k, jj=jj, j=j, wb=wb, mps=mps: e.matmul(
                            mps[:, j, :], lhsT=Wc[wb][:, k, jj * 128:(jj + 1) * 128], rhs=scT[:, k, :],
                            start=(k == 0), stop=(k == 7)), [rWc[wb], rscT], [rPS[bmod]], signal=(k == 7))
            dve(lambda e, l=l, mps=mps: e.tensor_tensor(out=modp[:, l, :, :], in0=mps,
                                                        in1=abT[l][:].unsqueeze(2).to_broadcast([128, 6, 2]), op=ALU.add),
                [rPS[bmod], rabT[l]], [rmodp])
        fw.dma("sp", mod_loc.ap(), modp[:].rearrange("p l j c -> p (l j c)"), reads=[rmodp], writes=[rmodloc], sem=rmodloc.sem)
        fw._waits("pool", [rmodloc], [])
        nc.gpsimd.collective_compute("AllGather", ALU.bypass, replica_groups=[[0, 1, 2, 3], [4, 5, 6, 7]],
                                     ins=[mod_loc.ap().opt()], outs=[mod_all.ap().opt()]).then_inc(cc_mod)
        load_wo(0)
        load_wo(1)
        def finish_mod_a():
            nc.sync.wait_ge(cc_mod, 1)
            fw.dma("sp", moda[:], mod_all.ap().rearrange("(r p) c -> p r c", p=128), writes=[rmoda], sem=rmoda.sem)
            for l in range(2):
                dve(lambda e, l=l: e.tensor_copy(out=modT[l][:].rearrange("p (r j) c -> p r j c", r=4),
                                                 in_=moda[:, :, l * 12:(l + 1) * 12].rearrange("p r (j c) -> p r j c", c=2)),
                    [rmoda], [rmodT[l]])


        def bcast_rows(vT_ap, rv, out_t, rout):
            for half in range(2):
                di = scrRR.next()
                dve(lambda e, di=di, half=half: e.tensor_tensor(
                    out=scr[di][:].rearrange("p (a c) -> p a c", a=4), in0=identf[:].unsqueeze(1).to_broadcast([128, 4, 128]),
                    in1=vT_ap[:, half * 4:half * 4 + 4].unsqueeze(2).to_broadcast([128, 4, 128]), op=ALU.mult),
                    [ridf, rv], [rscr[di]])
                b = inRR.next()
                pe(lambda e, di=di, b=b: e.matmul(PS[b][:], lhsT=onesf[:], rhs=scr[di][:],
                                                  start=True, stop=True), [rones, rscr[di]], [rPS[b]])
                act(lambda e, b=b, half=half: e.copy(out=out_t[:, half * 512:(half + 1) * 512], in_=PS[b][:]),
                    [rPS[b]], [rout])

        def finish_mod_b():
            fw.dma("sp", Abc[:], fn_d.partition_broadcast(128).rearrange("p o n -> p (o n)"), writes=[rAbc], sem=rAbc.sem)
            for l in range(2):
                for c in range(2):
                    dve(lambda e, l=l, c=c: e.scalar_tensor_tensor(out=ATt[:, l, c, :], in0=modT[l][:, 8:16, c], scalar=1.0,
                                                                   in1=gTn[l][:], op0=ALU.add, op1=ALU.mult),
                        [rmodT[l], rgTn[l]], [rAT])


        def phase_gate(l, c):
            bcast_rows(modT[l][:, 16:24, c], rmodT[l], gateB[c], rgate[c])

        wstate = {"i": 0}

        PASSES = [(0, list(range(8))), (0, [0, 1, 2, 3]), (1, list(range(8))), (0, [4, 5, 6, 7]),
                  (0, list(range(8))), (1, [0, 1, 2, 3]), (1, list(range(8))), (1, [4, 5, 6, 7])]
        pstate = {"i": 0, "pre": {}}

        def load_chunk(l, ci):
            wb = wstate["i"] % 2
            wstate["i"] += 1
            c0, cwid = CH[ci]
            fw.dma("pool", Wc[wb][:, :, 0:cwid], w_in[l][:, c0:c0 + cwid].rearrange("(k p) n -> p k n", p=128),
                   writes=[rWc[wb]], sem=rWc[wb].sem)
            return wb

        def inproj(wb, ci, ht, brr=None):
            cwid = CH[ci][1]
            b = (brr or ipRR).next()
            for k in range(8):
                pe(lambda e, k=k: e.matmul(PS[b][:, 0:cwid], lhsT=hT[:, ht, k, :], rhs=Wc[wb][:, k, 0:cwid],
                                           start=(k == 0), stop=(k == 7)), [rhT[ht][k], rWc[wb]], [rPS[b]], signal=(k == 7))
            return b

        def rope_apply(src, rsrc, G, dim, tab, rtab, tl, out_ap, rout, pre_scale=None):
            hq = dim // 4
            rd = list(rsrc) + [rtab]
            if pre_scale is not None:
                s0 = scrRR.next()
                sc_ = scr[s0][:, 0:G * dim].rearrange("p (g d) -> p g d", g=G)
                dve(lambda e: e.tensor_scalar(out=sc_, in0=src, scalar1=pre_scale[0], scalar2=None, op0=ALU.mult),
                    list(rsrc) + [pre_scale[1]], [rscr[s0]])
                src = sc_
                rd = [rscr[s0], rtab]
            s1 = scrRR.next(); s2 = scrRR.next()
            t1 = scr[s1][:, 0:G * dim].rearrange("p (g d) -> p g d", g=G)
            t2 = scr[s2][:, 0:G * dim].rearrange("p (g d) -> p g d", g=G)
            cosb = tab[:, tl, 0, :].unsqueeze(1).to_broadcast([128, G, dim])
            dve(lambda e: e.tensor_tensor(out=t1, in0=src, in1=cosb, op=ALU.mult), rd, [rscr[s1]])
            sv = src.rearrange("p g (a two q) -> p g a two q", two=2, q=hq)
            tv = t2.rearrange("p g (a two q) -> p g a two q", two=2, q=hq)
            sn = tab[:, tl, 1, :].rearrange("p (a two q) -> p a two q", two=2, q=hq)
            for two in range(2):
                o_ = tv[:, :, :, two, :]
                i_ = sv[:, :, :, 1 - two, :]
                s_ = sn[:, :, two, :].unsqueeze(1).to_broadcast([128, G, 2, hq])
                dve(lambda e, o_=o_, i_=i_, s_=s_: e.tensor_tensor(out=o_, in0=i_, in1=s_, op=ALU.mult), rd, [rscr[s2]])
            dve(lambda e: e.tensor_tensor(out=out_ap, in0=t1, in1=t2, op=ALU.add), [rscr[s1], rscr[s2]], [rout])

        def k_diff(src, rsrc, slot, rope_tl, out_dst):
            if out_dst is not None:
                si = scrRR.next()
                act(lambda e: e.copy(out=scr[si][:], in_=src), rsrc, [rscr[si]])
                fw.dma("sp", out_dst, scr[si][:], reads=[rscr[si]], sem=rscr[si].sem)
            bi = b512RR.next()
            if rope_tl is None:
                dve(lambda e: e.tensor_copy(out=b512[bi][:], in_=src), rsrc, [rb512[bi]])
            else:
                rope_apply(src.rearrange("p (g d) -> p g d", g=8), rsrc, 8, 64, ropd, rropd, rope_tl,
                           b512[bi][:].rearrange("p (g d) -> p g d", g=8), rb512[bi])
            yield
            b, v = transposes([b512[bi][:, h * 128:(h + 1) * 128] for h in range(4)], [rb512[bi]])
            act(lambda e: e.copy(out=KdT[:, :, slot * 128:(slot + 1) * 128], in_=v[:, 0:4, :]), [rPS[b]], [rKdT[slot]])

        def v_diff(src, rsrc, slot, out_dst):
            if out_dst is not None:
                si = scrRR.next()
                act(lambda e: e.copy(out=scr[si][:], in_=src), rsrc, [rscr[si]])
                fw.dma("sp", out_dst, scr[si][:], reads=[rscr[si]], sem=rscr[si].sem)
            dve(lambda e: e.tensor_copy(out=Vd[:, slot, :, 0:128], in_=src.rearrange("p (h d) -> p h d", h=4)),
                rsrc, [rVd[slot]])
            return
            yield

        def k_mla(l, ckv_src, kr_src, rsrc, slot, rope_tl, normalize, out_ckv, out_kr):
            ci = ckvRR.next()
            if normalize:
                ss, rss = st_alloc()
                act(lambda e: e.activation(out=junk[:, 0:128], in_=ckv_src, func=AF.Square, accum_out=ss),
                    list(rsrc) + [rss], [rjunk, rss])
                rs, rrs = rstd_from_ss(ss, rss, 128)
                gi = stgcRR.next()
                dve(lambda e: e.scalar_tensor_tensor(out=stgc[gi][:, 0:128], in0=ckv_src, scalar=rs, in1=kvnb[l][:],
                                                     op0=ALU.mult, op1=ALU.mult), list(rsrc) + [rrs, rkvnb[l]], [rstgc[gi]])
                if out_ckv is not None:
                    act(lambda e: e.copy(out=stgc[gi][:, 128:160], in_=kr_src), rsrc, [rstgc[gi]])
                    fw.dma("sp", out_ckv, stgc[gi][:, 0:128], reads=[rstgc[gi]], sem=rstgc[gi].sem)
                    fw.dma("sp", out_kr, stgc[gi][:, 128:160], reads=[rstgc[gi]], sem=rstgc[gi].sem)
                dve(lambda e: e.tensor_copy(out=ckvb[ci][:], in_=stgc[gi][:, 0:128]), [rstgc[gi]], [rckvb[ci]])
            else:
                dve(lambda e: e.tensor_copy(out=ckvb[ci][:], in_=ckv_src), rsrc, [rckvb[ci]])
            if rope_tl is None:
                dve(lambda e: e.tensor_copy(out=krb[ci][:], in_=kr_src), rsrc, [rkrb[ci]])
            else:
                rope_apply(kr_src.rearrange("p (g d) -> p g d", g=1), rsrc, 1, 32, ropm, rropm, rope_tl,
                           krb[ci][:].rearrange("p (g d) -> p g d", g=1), rkrb[ci])
            yield
            b, v = transposes([ckvb[ci][:]], [rckvb[ci]])
            act(lambda e: e.copy(out=ckvT[ci][:], in_=v[:, 0, :]), [rPS[b]], [rckvT[ci]])
            yield
            b2 = 4
            pe(lambda e: e.matmul(PS[b2][:], lhsT=ckvT[ci][:], rhs=Wukv[l][:], start=True, stop=True),
               [rckvT[ci], rWukv[l]], [rPS[b2]])
            pv = PS[b2][:].rearrange("p (h d) -> p h d", h=4)
            ki = KfRR.next()
            dve(lambda e: e.tensor_copy(out=Kf[ki][:, :, 0:64], in_=pv[:, :, 0:64]), [rPS[b2]], [rKf[ki]])
            act(lambda e: e.copy(out=Vm[:, slot, :, 0:64], in_=pv[:, :, 64:128]), [rPS[b2]], [rVm[slot]])
            dve(lambda e: e.tensor_copy(out=Kf[ki][:, :, 64:96], in_=krb[ci][:].unsqueeze(1).to_broadcast([128, 4, 32])),
                [rkrb[ci]], [rKf[ki]])
            yield
            b3, v3 = transposes([Kf[ki][:, h, :] for h in range(4)], [rKf[ki]])
            act(lambda e: e.copy(out=KT[0:96, :, slot * 128:(slot + 1) * 128], in_=v3[0:96, 0:4, :]), [rPS[b3]], [rKT[slot]])

        def stage_norm(G, l, t):
            for _ in norm_gen(G, l, t):
                pass

        def norm_gen(G, l, t):
            xi = G["X"][t]
            c = G["c"]
            if l == 0 and not G.get("xpre"):
                fw.dma("sp", X[xi][:], G["xsrc"][t], writes=[rX[xi]], sem=rX[xi].sem)
            ss, rss = st_alloc()
            act(lambda e: e.activation(out=E[0][:, 0:4, :].rearrange("p a c -> p (a c)"), in_=X[xi][:], func=AF.Square,
                                       accum_out=ss), [rX[xi], rss], [rE[0], rss])
            rs, rrs = rstd_from_ss(ss, rss, D)
            yield
            hi = hbRR.next()
            dve(lambda e: e.tensor_scalar(out=hb[hi][:], in0=X[xi][:], scalar1=rs, scalar2=None, op0=ALU.mult),
                [rX[xi], rrs], [rhb[hi]])
            b, v = transposes([hb[hi][:, k * 128:(k + 1) * 128] for k in range(8)], [rhb[hi]])
            ht = G["ht"][t]
            for k in range(8):
                if k % 2 == 0:
                    act(lambda e, k=k: e.activation(out=hT[:, ht, k, :], in_=v[:, k, :], func=AF.Identity,
                                                    scale=ATt[:, l, c, k:k + 1], bias=modT[l][:, k, c:c + 1]),
                        [rPS[b], rAT, rmodT[l]], [rhT[ht][k]])
                else:
                    dve(lambda e, k=k: e.tensor_scalar(out=hT[:, ht, k, :], in0=v[:, k, :], scalar1=ATt[:, l, c, k:k + 1],
                                                       scalar2=modT[l][:, k, c:c + 1], op0=ALU.mult, op1=ALU.add),
                        [rPS[b], rAT, rmodT[l]], [rhT[ht][k]])

        def consume_K(G, l, ci, t, b):
            rope = G["rope"]
            slot = G["kslot"][t]
            rps = [rPS[b]]
            if ci == 0:
                dst = None if rope else G["o_dk"](l, t)
                yield from k_diff(PS[b][:], rps, slot, (t if rope else None), dst)
            elif ci == 1:
                dst = None if rope else G["o_dv"](l, t)
                yield from v_diff(PS[b][:], rps, slot, dst)
            elif ci == 2:
                yield from k_mla(l, PS[b][:, 0:128], PS[b][:, 128:160], rps, slot, (t if rope else None), True,
                      None if rope else G["o_ckv"](l, t), None if rope else G["o_kr"](l, t))
            else:
                xi = scrRR.next()
                act(lambda e: e.copy(out=scr[xi][:, 0:256], in_=PS[b][:, 0:256]), rps, [rscr[xi]])
                ui = b256RR.next()
                dve(lambda e: e.tensor_tensor(out=b256[ui][:], in0=PS[b][:, 256:512], in1=scr[xi][:, 0:256], op=ALU.mult),
                    rps + [rscr[xi]], [rb256[ui]])
                yield
                bt, v = transposes([b256[ui][:, k * 128:(k + 1) * 128] for k in range(2)], [rb256[ui]])
                sq, hf = G["us"] + t // 2, t % 2
                act(lambda e: e.copy(out=uT[:, sq, :, 1 + hf * 128:1 + (hf + 1) * 128], in_=v[:, 0:2, :]),
                    [rPS[bt]], [ruT[2 * G["us"] + t]])

        def consume_Q(G, l, ci, t, b):
            rope = G["rope"]
            rps = [rPS[b]]
            if False:
                yield
            if ci == 4:
                ss, rss = st_alloc()
                act(lambda e: e.activation(out=junk[:, 0:256], in_=PS[b][:, 0:256], func=AF.Square, accum_out=ss),
                    rps + [rss], [rjunk, rss])
                la_, rla_ = st_alloc()
                rs, rrs = st_alloc()
                qi = b256RR.next()
                dve(lambda e: e.tensor_copy(out=b256[qi][:], in_=PS[b][:, 0:256]), rps, [rb256[qi]])
                si = scrRR.next()
                sigmoid_from(PS[b][:, 256:512], rPS[b], scr[si][:, 0:256], rscr[si], 256)
                dve(lambda e: e.tensor_tensor(out=SZ[:, t, 0:256], in0=PS[b][:, 256:512], in1=scr[si][:, 0:256],
                                              op=ALU.mult), rps + [rscr[si]], [rSZ[t]])
                yield
                act(lambda e: e.activation(out=la_, in_=ss, func=AF.Ln, scale=1.0 / 256, bias=cst[:, 0:1]), [rss, rcst], [rla_])
                bt, v = transposes([b256[qi][:, k * 128:(k + 1) * 128] for k in range(2)], [rb256[qi]])
                ci_ = cqRR.next()
                act(lambda e: e.copy(out=cqT[ci_][:], in_=v[:, 0:2, :]), [rPS[bt]], [rcqT[ci_]])
                yield
                act(lambda e: e.activation(out=rs, in_=la_, func=AF.Exp, scale=-0.5), [rla_], [rrs])
                b5 = 5
                for kc in range(2):
                    pe(lambda e, kc=kc: e.matmul(PS[b5][:, 0:384], lhsT=cqT[ci_][:, kc, :], rhs=Wuq[l][:, kc, :],
                                                 start=(kc == 0), stop=(kc == 1)), [rcqT[ci_], rWuq[l]], [rPS[b5]],
                       signal=(kc == 1))
                qv = PS[b5][:, 0:384].rearrange("p (h d) -> p h d", h=4)
                fi = QfRR.next()
                if not rope:
                    dve(lambda e: e.tensor_scalar(out=Qf[fi][:], in0=qv, scalar1=rs, scalar2=None, op0=ALU.mult),
                        [rPS[b5], rrs], [rQf[fi]])
                else:
                    dve(lambda e: e.tensor_scalar(out=Qf[fi][:, :, 0:64], in0=qv[:, :, 0:64], scalar1=rs,
                                                  scalar2=None, op0=ALU.mult), [rPS[b5], rrs], [rQf[fi]])
                    rope_apply(qv[:, :, 64:96], [rPS[b5]], 4, 32, ropm, rropm, t, Qf[fi][:, :, 64:96], rQf[fi],
                               pre_scale=(rs, rrs))
                yield
                b6, v6 = transposes([Qf[fi][:, h, :] for h in range(4)], [rQf[fi]])
                act(lambda e: e.copy(out=QT[0:96, :, t * 128:(t + 1) * 128], in_=v6[0:96, 0:4, :]), [rPS[b6]], [rQT[t]])
            elif ci == 5:
                bi = b512RR.next()
                if not rope:
                    dve(lambda e: e.tensor_copy(out=b512[bi][:], in_=PS[b][:]), rps, [rb512[bi]])
                else:
                    rope_apply(PS[b][:].rearrange("p (g d) -> p g d", g=8), rps, 8, 64, ropd, rropd, t,
                               b512[bi][:].rearrange("p (g d) -> p g d", g=8), rb512[bi])
                yield
                bt, v = transposes([b512[bi][:, h * 128:(h + 1) * 128] for h in range(4)], [rb512[bi]])
                act(lambda e: e.copy(out=QdT[:, :, t * 128:(t + 1) * 128], in_=v[:, 0:4, :]), [rPS[bt]], [rQdT[t]])
            elif ci == 6:
                si = scrRR.next()
                sigmoid_from(PS[b][:, 256:512], rPS[b], scr[si][:, 0:256], rscr[si], 256)
                dve(lambda e: e.tensor_tensor(out=scr[si][:, 0:256], in0=PS[b][:, 256:512], in1=scr[si][:, 0:256],
                                              op=ALU.mult), rps + [rscr[si]], [rscr[si]])
                gi = b256RR.next()
                dve(lambda e: e.tensor_tensor(out=b256[gi][:], in0=PS[b][:, 0:256], in1=scr[si][:, 0:256], op=ALU.mult),
                    rps + [rscr[si]], [rb256[gi]])
                yield
                bt, v = transposes([b256[gi][:, k * 128:(k + 1) * 128] for k in range(2)], [rb256[gi]])
                act(lambda e: e.copy(out=gT[:, :, t * 128:(t + 1) * 128], in_=v[:, 0:2, :]), [rPS[bt]], [rgT[t]])
            else:
                si = scrRR.next()
                sigmoid_from(PS[b][:], rPS[b], scr[si][:], rscr[si], 512)
                dve(lambda e: e.tensor_tensor(out=scr[si][:], in0=PS[b][:], in1=scr[si][:], op=ALU.mult),
                    rps + [rscr[si]], [rscr[si]])
                dve(lambda e: e.tensor_tensor(out=SZ[:, t, 256:768].rearrange("p (h d) -> p h d", h=4),
                                              in0=scr[si][:].rearrange("p (h d) -> p h d", h=4),
                                              in1=subb[l][:].unsqueeze(1).to_broadcast([128, 4, 128]), op=ALU.mult),
                    [rscr[si], rsubb[l]], [rSZ[t]])

        def step_all(lst):
            for g_ in list(lst):
                try:
                    next(g_)
                except StopIteration:
                    lst.remove(g_)

        def stage_inproj(G, l, chunks, extra=None, extra_delay=0):
            for _ in inproj_gen(G, l, chunks, extra=extra, extra_delay=extra_delay):
                pass

        def inproj_gen(G, l, chunks, extra=None, brr=None, extra_delay=0):
            nt = G["nt"]
            items = [(ci, t) for ci in chunks for t in range(nt)]
            assert PASSES[pstate["i"]] == (l, chunks), (pstate["i"], l, chunks)
            wbuf = dict(pstate["pre"])
            pstate["pre"] = {}
            if chunks[0] not in wbuf:
                wbuf[chunks[0]] = load_chunk(l, chunks[0])

            def issue(j):
                ci, t = items[j]
                if t == 0:
                    idx = chunks.index(ci)
                    if idx + 1 < len(chunks) and chunks[idx + 1] not in wbuf:
                        wbuf[chunks[idx + 1]] = load_chunk(l, chunks[idx + 1])
                return inproj(wbuf[ci], ci, G["ht"][t], brr)

            active = list(extra) if (extra and extra_delay == 0) else []
            for j in range(len(items)):
                b = issue(j)
                if extra and extra_delay > 0 and j >= extra_delay and (j - extra_delay) < len(extra):
                    active.append(extra[j - extra_delay])
                step_all(active)
                ci, t = items[j]
                gen = consume_K(G, l, ci, t, b) if ci < 4 else consume_Q(G, l, ci, t, b)
                try:
                    next(gen)
                    active.append(gen)
                except StopIteration:
                    pass
                yield
            while active:
                step_all(active)
                yield
            pstate["i"] += 1
            if pstate["i"] < len(PASSES):
                ln, chn = PASSES[pstate["i"]]
                for cj in chn[0:2]:
                    pstate["pre"][cj] = load_chunk(ln, cj)

        def stage_K(G, l):
            stage_inproj(G, l, [0, 1, 2, 3])

        def stage_Q(G, l):
            stage_inproj(G, l, [4, 5, 6, 7])

        def conv_seq(G, l, sq):
            t0 = 2 * sq
            for blk in range(2):
                yi = scrRR.next()
                us = G["us"] + sq
                u = uT[:, us, blk, :]
                rd = [ruT[2 * us], ruT[2 * us + 1], ruH[us], rcw[l]]
                dve(lambda e: e.tensor_scalar(out=scr[yi][:, 0:256], in0=u[:, 0:256], scalar1=cw[l][:, blk * 3:blk * 3 + 1],
                                              scalar2=None, op0=ALU.mult), rd, [rscr[yi]])
                for j in (1, 2):
                    dve(lambda e, j=j: e.scalar_tensor_tensor(out=scr[yi][:, 0:256], in0=u[:, j:j + 256],
                                                              scalar=cw[l][:, blk * 3 + j:blk * 3 + j + 1], in1=scr[yi][:, 0:256],
                                                              op0=ALU.mult, op1=ALU.add), rd + [rscr[yi]], [rscr[yi]])
                g_ = gT[:, blk, t0 * 128:(t0 + 2) * 128]
                dve(lambda e: e.tensor_tensor(out=g_, in0=scr[yi][:, 0:256], in1=g_, op=ALU.mult),
                    [rscr[yi], rgT[t0], rgT[t0 + 1]], [rgT[t0], rgT[t0 + 1]])

        def attention_seq(G, l, sq, slots, extra=None, filler=None):
            t0 = 2 * sq
            qc = slice(t0 * 128, t0 * 128 + 256)
            ns = len(slots)
            units = [("m", h, 0) for h in range(4)] + [("d", h, a) for h in range(4) for a in range(2)]
            ebuf = {}

            def accbanks(u):
                if u[0] == "m" or filler is not None:
                    return (6, 7)
                return (0, 1) if u[1] % 2 == 0 else (6, 7)

            deep = (filler is None and ns == 2)

            def s_phase(u):
                kind, h, a = u
                ei = (ERR3 if deep else ERR).next()
                ebuf[u] = ei
                for p0 in range(0, ns, 2):
                    b = (sRR3 if deep else sRR).next()
                    pair = slots[p0:p0 + 2]
                    for j, slot in enumerate(pair):
                        if kind == "m":
                            pe(lambda e, b=b, slot=slot, j=j: e.matmul(
                                PS[b][:, j * 256:(j + 1) * 256], lhsT=KT[0:96, h, slot * 128:(slot + 1) * 128],
                                rhs=QT[0:96, h, qc], start=True, stop=True),
                               [rKT[slot], rQT[t0], rQT[t0 + 1]], [rPS[b]], signal=(j == len(pair) - 1))
                            sc = MLA_SCALE
                        else:
                            pe(lambda e, b=b, slot=slot, j=j: e.matmul(
                                PS[b][:, j * 256:(j + 1) * 256], lhsT=KdT[a * 64:(a + 1) * 64, h, slot * 128:(slot + 1) * 128],
                                rhs=QdT[a * 64:(a + 1) * 64, h, qc], start=True, stop=True),
                               [rKdT[slot], rQdT[t0], rQdT[t0 + 1]], [rPS[b]], signal=(j == len(pair) - 1))
                            sc = DIFF_SCALE
                    w = 256 * len(pair)
                    act(lambda e, b=b, p0=p0, sc=sc, w=w: e.activation(
                        out=E[ei][:, p0:p0 + len(pair), :].rearrange("p s q -> p (s q)"), in_=PS[b][:, 0:w], func=AF.Exp,
                        scale=sc), [rPS[b]], [rE[ei]])

            def av_phase(u):
                kind, h, a = u
                ei = ebuf[u]
                ab = accbanks(u)
                for qt in range(2):
                    for si, slot in enumerate(slots):
                        if kind == "m":
                            pe(lambda e, qt=qt, si=si, slot=slot: e.matmul(
                                PS[ab[qt]][:, h * 65:(h + 1) * 65], lhsT=E[ei][:, si, qt * 128:(qt + 1) * 128],
                                rhs=Vm[:, slot, h, :], start=(si == 0), stop=(si == ns - 1)),
                               [rE[ei], rVm[slot]], [rPS[ab[qt]]], signal=(si == ns - 1))
                        else:
                            pe(lambda e, qt=qt, si=si, slot=slot: e.matmul(
                                PS[ab[qt]][:, a * 129:(a + 1) * 129], lhsT=E[ei][:, si, qt * 128:(qt + 1) * 128],
                                rhs=Vd[:, slot, h, :], start=(si == 0), stop=(si == ns - 1)),
                               [rE[ei], rVd[slot]], [rPS[ab[qt]]], signal=(si == ns - 1))

            def post(u):
                kind, h, a = u
                ab = accbanks(u)
                if False:
                    yield
                if kind == "m" and h == 3:
                    avs = [PS[ab[qt]][:, 0:260].rearrange("p (h d) -> p h d", h=4) for qt in range(2)]
                    rzs = [st_alloc(4) for qt in range(2)]
                    sis = [scrRR.next() for qt in range(2)]
                    for qt in range(2):
                        dve(lambda e, qt=qt: e.reciprocal(out=rzs[qt][0].unsqueeze(2), in_=avs[qt][:, :, 64:65]),
                            [rPS[ab[qt]]], [rzs[qt][1]])
                    for qt in range(2):
                        ov = scr[sis[qt]][:, 0:256].rearrange("p (h d) -> p h d", h=4)
                        dve(lambda e, qt=qt, ov=ov: e.tensor_tensor(out=ov, in0=avs[qt][:, :, 0:64],
                                                                   in1=rzs[qt][0].unsqueeze(2).to_broadcast([128, 4, 64]),
                                                                   op=ALU.mult), [rPS[ab[qt]], rzs[qt][1]], [rscr[sis[qt]]])
                    for qt in range(2):
                        t = t0 + qt
                        dve(lambda e, t=t, qt=qt: e.tensor_tensor(out=mix[t][:, 0:256], in0=scr[sis[qt]][:, 0:256],
                                                                 in1=SZ[:, t, 0:256], op=ALU.mult),
                            [rscr[sis[qt]], rSZ[t]], [rmix[t][0]])
                if kind == "d" and a == 1:
                    st_ = []
                    avs = [PS[ab[qt]][:, 0:258].rearrange("p (a d) -> p a d", a=2) for qt in range(2)]
                    rzs = [st_alloc(2) for qt in range(2)]
                    pis = [phRR.next() for qt in range(2)]
                    for qt in range(2):
                        dve(lambda e, qt=qt: e.reciprocal(out=rzs[qt][0].unsqueeze(2), in_=avs[qt][:, :, 128:129]),
                            [rPS[ab[qt]]], [rzs[qt][1]])
                    for qt in range(2):
                        dve(lambda e, qt=qt: e.tensor_scalar(out=ph[pis[qt]][:, 0:128], in0=avs[qt][:, 0, 0:128],
                                                            scalar1=rzs[qt][0][:, 0:1], scalar2=None, op0=ALU.mult),
                            [rPS[ab[qt]], rzs[qt][1]], [rph[pis[qt]]])
                    for qt in range(2):
                        dve(lambda e, qt=qt: e.tensor_tensor(out=rzs[qt][0][:, 1:2], in0=rzs[qt][0][:, 1:2], in1=lamt[l][:, 3:4],
                                                            op=ALU.mult), [rzs[qt][1], rlamt[l]], [rzs[qt][1]])
                    for qt in range(2):
                        dve(lambda e, qt=qt: e.scalar_tensor_tensor(out=ph[pis[qt]][:, 128:256], in0=avs[qt][:, 1, 0:128],
                                                                   scalar=rzs[qt][0][:, 1:2], in1=ph[pis[qt]][:, 0:128],
                                                                   op0=ALU.mult, op1=ALU.add),
                            [rPS[ab[qt]], rzs[qt][1], rph[pis[qt]]], [rph[pis[qt]]])
                    for qt in range(2):
                        st_.append((t0 + qt, pis[qt]))
                    yield
                    ss, rss = st_alloc(2)
                    for i_, (t, pi) in enumerate(st_):
                        act(lambda e, pi=pi, i_=i_: e.activation(out=junk[:, 0:128], in_=ph[pi][:, 128:256], func=AF.Square,
                                                                 accum_out=ss[:, i_:i_ + 1]), [rph[pi], rss], [rjunk, rss])
                    yield
                    la_, rla_ = st_alloc(2)
                    rs, rrs = st_alloc(2)
                    act(lambda e: e.activation(out=la_, in_=ss, func=AF.Ln, scale=1.0 / 128, bias=cst[:, 0:1]), [rss, rcst], [rla_])
                    yield
                    act(lambda e: e.activation(out=rs, in_=la_, func=AF.Exp, scale=-0.5), [rla_], [rrs])
                    yield
                    c0 = 256 + h * 128
                    for i_, (t, pi) in enumerate(st_):
                        dve(lambda e, t=t, pi=pi, i_=i_: e.scalar_tensor_tensor(out=mix[t][:, c0:c0 + 128], in0=ph[pi][:, 128:256],
                                                                               scalar=rs[:, i_:i_ + 1], in1=SZ[:, t, c0:c0 + 128],
                                                                               op0=ALU.mult, op1=ALU.mult),
                            [rph[pi], rrs, rSZ[t]], [rmix[t][1 + h]])

            la = 2 if deep else 1
            if deep:
                trRR.items = [2]
            for j in range(la):
                s_phase(units[j])
            pend = list(extra) if extra else []
            for i, u in enumerate(units):
                if i + la < len(units):
                    s_phase(units[i + la])
                av_phase(u)
                step_all(pend)
                if filler:
                    step_all(filler)
                gen = post(u)
                try:
                    next(gen)
                    pend.append(gen)
                except StopIteration:
                    pass
            while pend:
                for g_ in list(pend):
                    try:
                        next(g_)
                    except StopIteration:
                        pend.remove(g_)
            trRR.items = [2, 3]
            for qt in range(2):
                t = t0 + qt
                mi = qt
                b, v = transposes([mix[t][:, k * 128:(k + 1) * 128] for k in range(6)], rmix[t])
                act(lambda e: e.copy(out=mixT[mi][:], in_=v[:, 0:6, :]), [rPS[b]], [rmixT[mi]])

        def outproj_tile(G, l, t, mi):
            xi = G["X"][t]
            banks = [inRR.next(), inRR.next()]
            for half in range(2):
                b = banks[half]
                for kc in range(8):
                    if kc < 2:
                        lt, rl = mixT[mi][:, kc, :], rmixT[mi]
                    elif kc < 4:
                        lt, rl = gT[:, kc - 2, t * 128:(t + 1) * 128], rgT[t]
                    else:
                        lt, rl = mixT[mi][:, kc - 2, :], rmixT[mi]
                    pe(lambda e, lt=lt, kc=kc: e.matmul(PS[b][:], lhsT=lt, rhs=Wo[l][:, kc, half * 512:(half + 1) * 512],
                                                        start=(kc == 0), stop=(kc == 7)), [rl, rWo[l]], [rPS[b]],
                       signal=(kc == 7))
                gi = scrRR.next()
                dve(lambda e, b=b, half=half, gi=gi: e.tensor_tensor(out=scr[gi][:], in0=PS[b][:],
                                                                    in1=gateB[G["c"]][:, half * 512:(half + 1) * 512], op=ALU.mult),
                    [rPS[b], rgate[G["c"]]], [rscr[gi]])
                dve(lambda e, half=half, gi=gi: e.tensor_tensor(out=X[xi][:, half * 512:(half + 1) * 512], in0=scr[gi][:],
                                                               in1=X[xi][:, half * 512:(half + 1) * 512], op=ALU.add),
                    [rscr[gi], rX[xi]], [rX[xi]])
            if l == DEPTH - 1:
                ss, rss = st_alloc()
                act(lambda e: e.activation(out=E[0][:, 0:4, :].rearrange("p a c -> p (a c)"), in_=X[xi][:], func=AF.Square, accum_out=ss), [rX[xi], rss], [rE[0], rss])
                rs, rrs = rstd_from_ss(ss, rss, D)
                for half in range(2):
                    yi = scrRR.next()
                    cs = slice(half * 512, (half + 1) * 512)
                    dve(lambda e, yi=yi, cs=cs: e.scalar_tensor_tensor(out=scr[yi][:], in0=X[xi][:, cs], scalar=rs, in1=Abc[:, cs],
                                                                      op0=ALU.mult, op1=ALU.mult), [rX[xi], rrs, rAbc], [rscr[yi]])
                    fw.dma("sp", G["ydst"][t][:, cs], scr[yi][:], reads=[rscr[yi]], sem=rscr[yi].sem)
                return None
            gen = norm_gen(G, l + 1, t)
            next(gen)
            return gen

        def prompt_group(g):
            G = {"nt": 4, "rope": False, "X": [0, 1, 2, 3], "kslot": [0, 1, 2, 3], "ht": [0, 1, 2, 3], "us": 0, "c": 0}
            G["xsrc"] = [xp[2 * g + t // 2, (t % 2) * 128:(t % 2 + 1) * 128, :] for t in range(4)]
            G["ydst"] = [y_p[2 * g + t // 2, (t % 2) * 128:(t % 2 + 1) * 128, :] for t in range(4)]
            rows = lambda t: slice((t % 2) * 128, (t % 2 + 1) * 128)
            G["o_dk"] = lambda l, t: o_dk[2 * g + t // 2, l, rows(t), :]
            G["o_dv"] = lambda l, t: o_dv[2 * g + t // 2, l, rows(t), :]
            G["o_ckv"] = lambda l, t: o_ckv[2 * g + t // 2, l, rows(t), :]
            G["o_kr"] = lambda l, t: o_kr[2 * g + t // 2, l, rows(t), :]
            return G

        PG = [prompt_group(0), prompt_group(1)]

        xsem2 = {}

        def prefetch_x(G, q="sp"):
            for t in range(G["nt"]):
                xi = G["X"][t]
                sm = rX[xi].sem
                if q == "pool":
                    if xi not in xsem2:
                        xsem2[xi] = fw.dma_sem()
                    sm = xsem2[xi]
                fw.dma(q, X[xi][:], G["xsrc"][t], writes=[rX[xi]], sem=sm)
            G["xpre"] = True

        def group_norm_gens(g):
            G = PG[g]
            gens = []
            for t in range(4):
                gen = norm_gen(G, 0, t)
                next(gen)
                gens.append(gen)
            return gens

        def run_prompt_group(g, fill_gen=None, first_extra=None):
            G = PG[g]
            carry = list(first_extra) if first_extra else []
            for l in range(DEPTH):
                st_reset()
                phase_gate(l, 0)
                stage_inproj(G, l, [0, 1, 2, 3, 4, 5, 6, 7], extra=carry, extra_delay=1)
                carry = []
                trigger_ag()
                filler = [fill_gen] if (l == 0 and fill_gen is not None) else None
                for sq in range(2):
                    conv_seq(G, l, sq)
                    attention_seq(G, l, sq, [2 * sq, 2 * sq + 1], extra=carry, filler=filler)
                    carry = []
                    if sq == 1 and filler:
                        while filler:
                            step_all(filler)
                    for qt in range(2):
                        gen = outproj_tile(G, l, 2 * sq + qt, qt)
                        if gen is not None:
                            carry.append(gen)
            assert not carry

        GS = {"nt": 2, "rope": True, "X": [4, 5], "kslot": [8, 9], "ht": [4, 5], "us": 2, "c": 1}
        GS["xsrc"] = [xs[t * 128:(t + 1) * 128, :] for t in range(2)]
        GS["ydst"] = [y_s[t * 128:(t + 1) * 128, :] for t in range(2)]
        rloc = [Res("kvloc%d" % l) for l in range(2)]
        rall = [Res("kvall%d" % l) for l in range(2)]
        for r in rloc:
            r.sem = fw.dma_sem()
        cc_sem = [fw.es.enter_context(nc.semaphore("cc_sem%d" % l)) for l in range(2)]

        fillRR = RR([0, 1])

        def sample_front_gen(l):
            yield from inproj_gen(GS, l, [0, 1, 2, 3], brr=fillRR)
            dve(lambda e: e.tensor_copy(out=ubnd[:, :, 0:1], in_=uT[:, 2, :, 1:2]), [ruT[4]], [rubnd])
            dve(lambda e: e.tensor_copy(out=ubnd[:, :, 1:2], in_=uT[:, 2, :, 256:257]), [ruT[5]], [rubnd])
            loc = kv_loc[l].ap()
            s = rloc[l].sem
            fw.dma("sp", loc[:, O_KT:O_KT + 1024].rearrange("p (h k) -> p h k", h=4), KT[:, :, 1024:1280],
                   reads=[rKT[8], rKT[9]], writes=[], sem=s)
            fw.dma("sp", loc[:, O_KD:O_KD + 1024].rearrange("p (h k) -> p h k", h=4), KdT[:, :, 1024:1280],
                   reads=[rKdT[8], rKdT[9]], writes=[], sem=s)
            fw.dma("sp", loc[:, O_VM:O_VM + 520], Vm[:, 8:10, :, :].rearrange("p s h d -> p (s h d)"),
                   reads=[rVm[8], rVm[9]], writes=[], sem=s)
            fw.dma("sp", loc[:, O_VD:O_VD + 1032], Vd[:, 8:10, :, :].rearrange("p s h d -> p (s h d)"),
                   reads=[rVd[8], rVd[9]], writes=[], sem=s)
            fw.dma("sp", loc[:, O_UB:O_UB + 4], ubnd[:].rearrange("p b c -> p (b c)"), reads=[rubnd], writes=[rloc[l]], sem=s)
            for r_ in (rKT[8], rKT[9], rKdT[8], rKdT[9], rVm[8], rVm[9], rVd[8], rVd[9], rubnd):
                r_.rd[s] = fw.cnt[s]
            pending_ag.append(l)

        pending_ag = []

        def trigger_ag():
            while pending_ag:
                l = pending_ag.pop(0)
                fw._waits("pool", [rloc[l]], [rall[l]])
                nc.gpsimd.collective_compute("AllGather", ALU.bypass, replica_groups=[[0, 1, 2, 3], [4, 5, 6, 7]],
                                             ins=[kv_loc[l].ap().opt()], outs=[kv_all[l].ap().opt()]).then_inc(cc_sem[l])

        def sample_ctx(l):
            for kt in range(2):
                rows = slice(kt * 128, (kt + 1) * 128)
                ci = scrRR.next()
                fw.dma("sp", scr[ci][:], dk_c[l, rows, :], writes=[rscr[ci]], sem=rscr[ci].sem)
                for _ in k_diff(scr[ci][:], [rscr[ci]], 4 + kt, None, None):
                    pass
                ci = scrRR.next()
                fw.dma("sp", scr[ci][:], dv_c[l, rows, :], writes=[rscr[ci]], sem=rscr[ci].sem)
                for _ in v_diff(scr[ci][:], [rscr[ci]], 4 + kt, None):
                    pass
                ci = scrRR.next()
                fw.dma("sp", scr[ci][:, 0:128], ckv_c[l, rows, :], writes=[rscr[ci]], sem=rscr[ci].sem)
                fw.dma("sp", scr[ci][:, 128:160], kr_c[l, rows, :], writes=[rscr[ci]], sem=rscr[ci].sem)
                for _ in k_mla(l, scr[ci][:, 0:128], scr[ci][:, 128:160], [rscr[ci]], 4 + kt, None, False, None, None):
                    pass

        def sample_back(l, extra=None):
            st_reset()
            phase_gate(l, 1)
            nc.sync.wait_ge(cc_sem[l], 1)
            al = kv_all[l].ap()
            for r in range(4):
                rw = al[r * 128:(r + 1) * 128, :]
                s0 = (0, 2, 6, 8)[r]
                fw.dma("sp", KT[:, :, s0 * 128:(s0 + 2) * 128], rw[:, O_KT:O_KT + 1024].rearrange("p (h k) -> p h k", h=4),
                       writes=[rKT[s0], rKT[s0 + 1]], sem=rKT[s0].sem)
                fw.dma("sp", KdT[:, :, s0 * 128:(s0 + 2) * 128], rw[:, O_KD:O_KD + 1024].rearrange("p (h k) -> p h k", h=4),
                       writes=[rKdT[s0], rKdT[s0 + 1]], sem=rKdT[s0].sem)
                fw.dma("sp", Vm[:, s0:s0 + 2, :, :].rearrange("p s h d -> p (s h d)"), rw[:, O_VM:O_VM + 520],
                       writes=[rVm[s0], rVm[s0 + 1]], sem=rVm[s0].sem)
                fw.dma("sp", Vd[:, s0:s0 + 2, :, :].rearrange("p s h d -> p (s h d)"), rw[:, O_VD:O_VD + 1032],
                       writes=[rVd[s0], rVd[s0 + 1]], sem=rVd[s0].sem)
                fw.dma("sp", uhal[:, r, :, :].rearrange("p b c -> p (b c)"), rw[:, O_UB:O_UB + 4], writes=[ruhal], sem=ruhal.sem)
            dve(lambda e: e.tensor_copy(out=uhf[:], in_=uhal[:]), [ruhal], [ruhf])
            for side in range(2):
                col = 1 - side
                dve(lambda e: e.tensor_scalar(out=uacc[:, :, side:side + 1], in0=uhf[:, 0, :, col:col + 1],
                                              scalar1=selh[:, side * 4:side * 4 + 1], scalar2=None, op0=ALU.mult),
                    [ruhf, rsel], [ruacc])
                for r in range(1, 4):
                    dve(lambda e, r=r: e.scalar_tensor_tensor(out=uacc[:, :, side:side + 1], in0=uhf[:, r, :, col:col + 1],
                                                              scalar=selh[:, side * 4 + r:side * 4 + r + 1],
                                                              in1=uacc[:, :, side:side + 1], op0=ALU.mult, op1=ALU.add),
                        [ruhf, rsel, ruacc], [ruacc])
            dve(lambda e: e.tensor_copy(out=uT[:, 2, :, 0:1], in_=uacc[:, :, 0:1]), [ruacc], [ruH[2]])
            dve(lambda e: e.tensor_copy(out=uT[:, 2, :, 257:258], in_=uacc[:, :, 1:2]), [ruacc], [ruH[2]])
            stage_inproj(GS, l, [4, 5, 6, 7], extra=extra, extra_delay=3)
            conv_seq(GS, l, 0)
            attention_seq(GS, l, 0, list(range(10)))
            gens = []
            for qt in range(2):
                gen = outproj_tile(GS, l, qt, qt)
                if gen is not None:
                    gens.append(gen)
            return gens

        def zero_prompt_halos():
            dve(lambda e: e.memset(uT[:, 0, :, 0:1], 0.0), [], [ruH[0]])
            dve(lambda e: e.memset(uT[:, 0, :, 257:258], 0.0), [], [ruH[0]])

        prefetch_x(GS)
        prefetch_x(PG[0])
        sample_ctx(0)
        gA = group_norm_gens(0)
        gS0 = []
        for t in range(2):
            g_ = norm_gen(GS, 0, t)
            next(g_)
            gS0.append(g_)
        finish_mod_a()
        finish_mod_b()
        for g_ in gA[0:2]:
            for _ in g_:
                pass
        run_prompt_group(0, fill_gen=sample_front_gen(0), first_extra=gA[2:4] + gS0)
        prefetch_x(PG[1], q="pool")
        gB = group_norm_gens(1)
        gS = sample_back(0, extra=gB)
        step_all(gS)
        sample_ctx(1)
        while gS:
            step_all(gS)
        run_prompt_group(1, fill_gen=sample_front_gen(1))
        sample_back(1)

        _DBG["sbuf_free"] = nc.sbuf_bytes_remaining
        fw.wait_all("sp", [k for k in fw.sems if k.startswith("dma")])
    return nc


def _rope_tables(pos, rot_dim):
    half = rot_dim // 2
    inv = (10000.0 ** (-(np.arange(0, half, 2, dtype=np.float32) / np.float32(half)))).astype(np.float32)
    row = (pos // 64).astype(np.float32)
    col = (pos % 64).astype(np.float32)
    ang_r = row[:, None] * inv
    ang_c = col[:, None] * inv
    ang = np.concatenate([ang_r, ang_r, ang_c, ang_c], axis=-1).astype(np.float32)
    cos = np.cos(ang).astype(np.float32)
    sin = np.sin(ang).astype(np.float32)
    q = half // 2
    sign = np.ones(rot_dim, np.float32)
    sign[0:q] = -1.0
    sign[half:half + q] = -1.0
    return cos, sin * sign


_COLS = None


def _perm_cols():
    offs = np.cumsum([0, 256, 128, 32, 256, 256, 256, 512, 512, 512, 1024])
    c_q, c_kv, k_r, cb, cc, cx, dq, dk, dv, z = [np.arange(offs[i], offs[i + 1]) for i in range(10)]
    return np.concatenate([dk, dv, c_kv, k_r, cc, cx, c_q, z[0:256], dq, cb, z[256:512], z[512:1024]])


_NC_CACHE = {}


def kernel(x_prompt, x_sample, c, cache_mla_ckv, cache_mla_krope, cache_diff_k, cache_diff_v,
           c_ctx, norm_g, ada_w, ada_b, w_in, mla_q_norm, w_uq, mla_kv_norm, w_ukv,
           conv_w, diff_lambda, diff_subln, w_out, final_norm):
    f = lambda a: np.ascontiguousarray(np.asarray(a, dtype=np.float32))
    x_prompt, x_sample, c, c_ctx = f(x_prompt), f(x_sample), f(c), f(c_ctx)
    cols = _perm_cols()
    shared = {
        "normgT": f(np.asarray(norm_g).reshape(2, 8, 128).transpose(0, 2, 1)),
        "final_norm": f(np.asarray(final_norm).reshape(1, D)),
        "w_in_p": f(np.asarray(w_in)[:, :, cols]),
        "qnT": f(np.asarray(mla_q_norm).reshape(2, 2, 128).transpose(0, 2, 1)),
        "w_uq": f(w_uq),
        "kv_norm": f(np.asarray(mla_kv_norm).reshape(2, 1, 128)),
        "w_ukv": f(w_ukv),
        "conv_wT": f(np.asarray(conv_w).reshape(2, 3, 2, 128).transpose(0, 3, 2, 1).reshape(2, 128, 6)),
        "diff_lambda": f(np.asarray(diff_lambda).reshape(2, 1, 256)),
        "diff_subln": f(np.asarray(diff_subln).reshape(2, 1, 128)),
        "w_out": f(w_out),
        "ident": np.eye(128, dtype=np.float32),
    }
    in_maps = []
    for i in range(NCORES):
        b, j = i // 4, i % 4
        pos = 256 * j + np.arange(256)
        cd, sd = _rope_tables(pos, 64)
        cm, sm = _rope_tables(pos, 32)
        rope_d = np.stack([cd, sd], axis=1).reshape(2, 128, 2, 64).transpose(1, 0, 2, 3)
        rope_m = np.stack([cm, sm], axis=1).reshape(2, 128, 2, 32).transpose(1, 0, 2, 3)
        sel = np.zeros((128, 8), np.float32)
        if j > 0:
            sel[:, j - 1] = 1.0
        if j < 3:
            sel[:, 4 + j + 1] = 1.0
        condT = np.stack([c_ctx.reshape(8, 128).T, c[b].reshape(8, 128).T], axis=-1)
        m = dict(shared)
        m.update({
            "xp": f(x_prompt[4 * i:4 * i + 4]),
            "xs": f(x_sample[b, 256 * j:256 * j + 256]),
            "condT": f(condT),
            "ckv_c": f(np.asarray(cache_mla_ckv)[b]),
            "kr_c": f(np.asarray(cache_mla_krope)[b]),
            "dk_c": f(np.asarray(cache_diff_k)[b].reshape(2, 256, 512)),
            "dv_c": f(np.asarray(cache_diff_v)[b].reshape(2, 256, 512)),
            "rope_d": f(rope_d), "rope_m": f(rope_m), "halo_sel": sel,
            "ada_w_s": f(np.asarray(ada_w)[:, :, 768 * j:768 * (j + 1)]),
            "ada_bT_s": f(np.asarray(ada_b)[:, 768 * j:768 * (j + 1)].reshape(2, 6, 128).transpose(0, 2, 1)),
        })
        in_maps.append(m)
    if "nc" not in _NC_CACHE:
        _NC_CACHE["nc"] = build_nc()
    res = run_bass_kernel_spmd(_NC_CACHE["nc"], in_maps, core_ids=list(range(NCORES)))
    R = res.results
    y_prompt = np.concatenate([R[i]["y_p"] for i in range(NCORES)], axis=0)
    y_sample = np.stack([np.concatenate([R[4 * b + j]["y_s"] for j in range(4)], axis=0) for b in range(2)], axis=0)
    s_ckv = np.concatenate([R[i]["o_ckv"] for i in range(NCORES)], axis=0)
    s_kr = np.concatenate([R[i]["o_kr"] for i in range(NCORES)], axis=0)
    s_dk = np.concatenate([R[i]["o_dk"] for i in range(NCORES)], axis=0).reshape(32, 2, 256, 4, 2, 64)
    s_dv = np.concatenate([R[i]["o_dv"] for i in range(NCORES)], axis=0).reshape(32, 2, 256, 4, 128)
    out = (y_prompt, y_sample, s_ckv, s_kr, s_dk, s_dv)
    return tuple(np.ascontiguousarray(o, dtype=np.float32) for o in out)
```

```python
import contextlib
import math
import numpy as np
import concourse.bass as bass
import concourse.mybir as mybir
from concourse.bass_utils import run_bass_kernel_spmd

F32 = mybir.dt.float32
BF16 = mybir.dt.bfloat16
AF = mybir.ActivationFunctionType
ALU = mybir.AluOpType
AX = mybir.AxisListType

EPS = 1e-6
NCORES = 8
DEPTH = 2
D = 1024
CH = [(0, 512), (512, 512), (1024, 160), (1184, 512), (1696, 512), (2208, 512), (2720, 512), (3232, 512)]
CPAD = 3604
O_KT, O_KD, O_VM, O_VD, O_UB = 0, 1024, 2048, 2568, 3600
_DBG = {}
MLA_SCALE = 96 ** -0.5
DIFF_SCALE = 64 ** -0.5


class Res:
    __slots__ = ("name", "lw", "rd", "excl", "sem")

    def __init__(self, name, excl=False):
        self.name = name
        self.lw = None
        self.rd = {}
        self.excl = excl
        self.sem = None


class FW:
    def __init__(self, nc):
        self.nc = nc
        self.es = contextlib.ExitStack()
        self.engs = {"pe": nc.tensor, "act": nc.scalar, "dve": nc.vector, "pool": nc.gpsimd, "sp": nc.sync}
        self.sems = {}
        self.cnt = {}
        self.known = {k: {} for k in self.engs}
        for k in ("pe", "act", "dve", "pool"):
            self.sems[k] = self.es.enter_context(nc.semaphore("sem_" + k))
            self.cnt[k] = 0
        self.ndma = 0

    def sbuf(self, name, shape, dt):
        return self.es.enter_context(self.nc.sbuf_tensor(name, list(shape), dt))

    def psum(self, name, shape, dt):
        return self.es.enter_context(self.nc.psum_tensor(name, list(shape), dt))

    def dma_sem(self):
        self.ndma += 1
        key = "dma%d" % self.ndma
        self.sems[key] = self.es.enter_context(self.nc.semaphore("sem_" + key))
        self.cnt[key] = 0
        return key

    def _waits(self, e, reads, writes, same_ok=False):
        deps = {}

        def need(kv):
            if kv is not None and deps.get(kv[0], 0) < kv[1]:
                deps[kv[0]] = kv[1]
        for r in reads:
            need(r.lw)
            if r.excl:
                for kv in r.rd.items():
                    need(kv)
        for w in writes:
            need(w.lw)
            for kv in w.rd.items():
                need(kv)
        eng = self.engs[e]
        for key, val in deps.items():
            if key == e and same_ok:
                continue
            if self.known[e].get(key, 0) >= val:
                continue
            eng.wait_ge(self.sems[key], val)
            self.known[e][key] = val

    def _record(self, reads, writes, key, ticket):
        for r in reads:
            if r.excl:
                r.lw = (key, ticket)
                r.rd = {}
            elif r.rd.get(key, 0) < ticket:
                r.rd[key] = ticket
        for w in writes:
            w.lw = (key, ticket)
            w.rd = {}

    def op(self, e, fn, reads=(), writes=(), signal=True):
        self._waits(e, reads, writes, same_ok=(e == "pe"))
        ins = fn(self.engs[e])
        if signal:
            self.cnt[e] += 1
            ins.then_inc(self.sems[e], 1)
            self._record(reads, writes, e, self.cnt[e])
        else:
            self._record(reads, writes, e, self.cnt[e] + 1)
        return ins

    def dma(self, q, out, in_, reads=(), writes=(), sem=None):
        self._waits(q, reads, writes)
        ins = self.engs[q].dma_start(out=out, in_=in_)
        self.cnt[sem] += 16
        ins.then_inc(self.sems[sem], 16)
        self._record(reads, writes, sem, self.cnt[sem])
        return ins

    def wait_all(self, e, keys):
        for k in keys:
            if self.cnt[k] > 0 and self.known[e].get(k, 0) < self.cnt[k]:
                self.engs[e].wait_ge(self.sems[k], self.cnt[k])
                self.known[e][k] = self.cnt[k]


class RR:
    def __init__(self, items):
        self.items = items
        self.i = 0

    def next(self):
        x = self.items[self.i % len(self.items)]
        self.i += 1
        return x


def build_nc():
    nc = bass.Bass("TRN2", target_bir_lowering=False)

    def din(name, shape, dt=F32):
        return nc.dram_tensor(name, list(shape), dt, kind="ExternalInput").ap()

    def dout(name, shape):
        return nc.dram_tensor(name, list(shape), F32, kind="ExternalOutput").ap()

    xp = din("xp", [4, 256, D])
    xs = din("xs", [256, D])
    condT = din("condT", [128, 8, 2])
    ckv_c = din("ckv_c", [2, 256, 128])
    kr_c = din("kr_c", [2, 256, 32])
    dk_c = din("dk_c", [2, 256, 512])
    dv_c = din("dv_c", [2, 256, 512])
    gT_d = din("normgT", [2, 128, 8])
    fn_d = din("final_norm", [1, D])
    ada_w = din("ada_w_s", [2, D, 768])
    ada_bT = din("ada_bT_s", [2, 128, 6])
    w_in = din("w_in_p", [2, D, 3744])
    qnT = din("qnT", [2, 128, 2])
    w_uq = din("w_uq", [2, 256, 384])
    kvn = din("kv_norm", [2, 1, 128])
    w_ukv = din("w_ukv", [2, 128, 512])
    cwT = din("conv_wT", [2, 128, 6])
    lam_d = din("diff_lambda", [2, 1, 256])
    sub_d = din("diff_subln", [2, 1, 128])
    w_out = din("w_out", [2, D, D])
    ident_d = din("ident", [128, 128])
    ropd_d = din("rope_d", [128, 2, 2, 64])
    ropm_d = din("rope_m", [128, 2, 2, 32])
    sel_d = din("halo_sel", [128, 8])

    y_p = dout("y_p", [4, 256, D])
    y_s = dout("y_s", [256, D])
    o_ckv = dout("o_ckv", [4, 2, 256, 128])
    o_kr = dout("o_kr", [4, 2, 256, 32])
    o_dk = dout("o_dk", [4, 2, 256, 512])
    o_dv = dout("o_dv", [4, 2, 256, 512])

    mod_loc = nc.dram_tensor("mod_loc", [128, 24], F32)
    mod_all = nc.dram_tensor("mod_all", [512, 24], F32)
    kv_loc = [nc.dram_tensor("kv_loc%d" % l, [128, CPAD], BF16) for l in range(2)]
    kv_all = [nc.dram_tensor("kv_all%d" % l, [512, CPAD], BF16) for l in range(2)]

    fw = FW(nc)
    with fw.es:
        S = fw.sbuf
        X = [S("X%d" % i, (128, D), F32) for i in range(6)]
        rX = [Res("X%d" % i) for i in range(6)]
        for r in rX:
            r.sem = fw.dma_sem()
        hT = S("hT", (128, 6, 8, 128), BF16)
        rhT = [[Res("hT%d_%d" % (i, k)) for k in range(8)] for i in range(6)]
        Wc = [S("Wc%d" % i, (128, 8, 512), BF16) for i in range(2)]
        rWc = [Res("Wc%d" % i) for i in range(2)]
        for r in rWc:
            r.sem = fw.dma_sem()
        Wo = [S("Wo%d" % l, (128, 8, D), BF16) for l in range(2)]
        rWo = [Res("Wo%d" % l) for l in range(2)]
        for r in rWo:
            r.sem = fw.dma_sem()
        Abc = S("Abc", (128, D), F32); rAbc = Res("Abc")
        gateB = [S("gateB%d" % i, (128, D), F32) for i in range(2)]; rgate = [Res("gateB%d" % i) for i in range(2)]
        ATt = S("ATt", (128, 2, 2, 8), F32); rAT = Res("ATt")
        rAbc.sem = fw.dma_sem()
        ph = [S("ph%d" % i, (128, 256), F32) for i in range(4)]
        rph = [Res("ph%d" % i) for i in range(4)]
        phRR = RR([0, 1, 2, 3])
        hb = [S("hb%d" % i, (128, D), BF16) for i in range(1)]
        rhb = [Res("hb%d" % i) for i in range(1)]
        hbRR = RR([0])
        junk = S("junk", (128, 256), BF16); rjunk = Res("junk")
        NS = 10
        KT = S("KT", (128, 4, NS * 128), BF16); rKT = [Res("KT%d" % i) for i in range(NS)]
        KdT = S("KdT", (128, 4, NS * 128), BF16); rKdT = [Res("KdT%d" % i) for i in range(NS)]
        Vm = S("Vm", (128, NS, 4, 65), BF16); rVm = [Res("Vm%d" % i) for i in range(NS)]
        Vd = S("Vd", (128, NS, 4, 129), BF16); rVd = [Res("Vd%d" % i) for i in range(NS)]
        for lst in (rKT, rKdT, rVm, rVd):
            for r in lst:
                r.sem = fw.dma_sem()
        QT = S("QT", (128, 4, 512), BF16); rQT = [Res("QT%d" % i) for i in range(4)]
        QdT = S("QdT", (128, 4, 512), BF16); rQdT = [Res("QdT%d" % i) for i in range(4)]
        E = [S("E%d" % i, (128, NS, 256), BF16) for i in range(2)]
        rE = [Res("E%d" % i) for i in range(2)]
        ERR = RR([0, 1])
        E.append(S("E3", (128, 2, 256), BF16)); rE.append(Res("E3"))
        ERR3 = RR([0, 1, 2])
        sRR3 = RR([4, 5, 3])
        uT = S("uT", (128, 3, 2, 258), BF16)
        ruT = [Res("uT%d" % i) for i in range(6)]
        ruH = [Res("uH%d" % i) for i in range(3)]
        gT = S("gT", (128, 2, 512), BF16)
        rgT = [Res("gT%d" % i) for i in range(4)]
        SZ = S("SZ", (128, 4, 768), BF16); rSZ = [Res("SZ%d" % i) for i in range(4)]
        mix = [S("mix%d" % i, (128, 768), BF16) for i in range(4)]
        rmix = [[Res("mix%d_%d" % (i, j)) for j in range(5)] for i in range(4)]
        mixT = [S("mixT%d" % i, (128, 6, 128), BF16) for i in range(2)]
        rmixT = [Res("mixT%d" % i) for i in range(2)]
        scr = [S("scr%d" % i, (128, 512), F32) for i in range(4)]
        rscr = [Res("scr%d" % i) for i in range(4)]
        for r in rscr:
            r.sem = fw.dma_sem()
        scrRR = RR([0, 1, 2, 3])
        stgc = [S("stgc%d" % i, (128, 160), F32) for i in range(2)]
        rstgc = [Res("stgc%d" % i) for i in range(2)]
        for r in rstgc:
            r.sem = fw.dma_sem()
        stgcRR = RR([0, 1])
        b512 = [S("b512_%d" % i, (128, 512), BF16) for i in range(2)]
        rb512 = [Res("b512_%d" % i) for i in range(2)]
        b512RR = RR([0, 1])
        b256 = [S("b256_%d" % i, (128, 256), BF16) for i in range(2)]
        rb256 = [Res("b256_%d" % i) for i in range(2)]
        b256RR = RR([0, 1])
        ckvb = [S("ckvb%d" % i, (128, 128), BF16) for i in range(2)]
        rckvb = [Res("ckvb%d" % i) for i in range(2)]
        ckvT = [S("ckvT%d" % i, (128, 128), BF16) for i in range(2)]
        rckvT = [Res("ckvT%d" % i) for i in range(2)]
        ckvRR = RR([0, 1])
        krb = [S("krb%d" % i, (128, 32), BF16) for i in range(2)]
        rkrb = [Res("krb%d" % i) for i in range(2)]
        cqT = [S("cqT%d" % i, (128, 2, 128), BF16) for i in range(2)]
        rcqT = [Res("cqT%d" % i) for i in range(2)]
        cqRR = RR([0, 1])
        Kf = [S("Kf%d" % i, (128, 4, 96), BF16) for i in range(2)]
        rKf = [Res("Kf%d" % i) for i in range(2)]
        KfRR = RR([0, 1])
        Qf = [S("Qf%d" % i, (128, 4, 96), BF16) for i in range(2)]
        rQf = [Res("Qf%d" % i) for i in range(2)]
        QfRR = RR([0, 1])
        ST = S("ST", (128, 512), F32)
        identf = S("identf", (128, 128), F32); ridf = Res("identf"); ridf.sem = fw.dma_sem()
        identb = S("identb", (128, 128), BF16); ridb = Res("identb")
        onesf = S("onesf", (128, 128), F32); rones = Res("onesf")
        cst = S("cst", (128, 4), F32); rcst = Res("cst")
        ropd = S("ropd", (128, 2, 2, 64), F32); rropd = Res("ropd"); rropd.sem = fw.dma_sem()
        ropm = S("ropm", (128, 2, 2, 32), F32); rropm = Res("ropm"); rropm.sem = fw.dma_sem()
        selh = S("selh", (128, 8), F32); rsel = Res("selh"); rsel.sem = fw.dma_sem()
        cT = S("cT", (128, 8, 2), F32); rcT = Res("cT"); rcT.sem = fw.dma_sem()
        scT = S("scT", (128, 8, 2), BF16); rscT = Res("scT")
        ctmp = S("ctmp", (128, 8, 2), F32); rctmp = Res("ctmp")
        modT = [S("modT%d" % l, (128, 24, 2), F32) for l in range(2)]; rmodT = [Res("modT%d" % l) for l in range(2)]
        abT = [S("abT%d" % l, (128, 6), F32) for l in range(2)]; rabT = [Res("abT%d" % l) for l in range(2)]
        gTn = [S("gTn%d" % l, (128, 8), F32) for l in range(2)]; rgTn = [Res("gTn%d" % l) for l in range(2)]
        qn = [S("qn%d" % l, (128, 2), F32) for l in range(2)]; rqn = [Res("qn%d" % l) for l in range(2)]
        wuqf = gateB[0][:, 0:768].rearrange("p (k n) -> p k n", k=2); rwuqf = rgate[0]; rgate[0].sem = fw.dma_sem()
        Wuq = [S("Wuq%d" % l, (128, 2, 384), BF16) for l in range(2)]; rWuq = [Res("Wuq%d" % l) for l in range(2)]
        Wukv = [S("Wukv%d" % l, (128, 512), BF16) for l in range(2)]; rWukv = [Res("Wukv%d" % l) for l in range(2)]
        kvnb = [S("kvnb%d" % l, (128, 128), F32) for l in range(2)]; rkvnb = [Res("kvnb%d" % l) for l in range(2)]
        subb = [S("subb%d" % l, (128, 128), F32) for l in range(2)]; rsubb = [Res("subb%d" % l) for l in range(2)]
        cw = [S("cw%d" % l, (128, 6), F32) for l in range(2)]; rcw = [Res("cw%d" % l) for l in range(2)]
        lamt = [S("lamt%d" % l, (128, 8), F32) for l in range(2)]; rlamt = [Res("lamt%d" % l) for l in range(2)]
        for lst in (rabT, rgTn, rqn, rWukv, rkvnb, rsubb, rcw):
            for r in lst:
                r.sem = fw.dma_sem()
        vT8 = S("vT8", (128, 8), F32); rvT8 = Res("vT8")
        modp = S("modp", (128, 2, 6, 2), F32); rmodp = Res("modp")
        moda = S("moda", (128, 4, 24), F32); rmoda = Res("moda"); rmoda.sem = fw.dma_sem()
        rmodloc = Res("modloc"); rmodloc.sem = fw.dma_sem()
        cc_mod = fw.es.enter_context(nc.semaphore("cc_mod"))
        uhal = S("uhal", (128, 4, 2, 2), BF16); ruhal = Res("uhal"); ruhal.sem = fw.dma_sem()
        uhf = S("uhf", (128, 4, 2, 2), F32); ruhf = Res("uhf")
        uacc = S("uacc", (128, 2, 2), F32); ruacc = Res("uacc")
        ubnd = S("ubnd", (128, 2, 2), BF16); rubnd = Res("ubnd")

        PS = [fw.psum("ps%d" % i, (128, 512), F32) for i in range(8)]
        rPS = [Res("ps%d" % i, excl=True) for i in range(8)]
        inRR = RR([0, 1])
        ipRR = RR([0, 1, 6, 7])
        trRR = RR([2, 3])
        sRR = RR([4, 5])

        def pst(b):
            return PS[b][:].bitcast(BF16).rearrange("p (a c) -> p a c", c=128)

        NBLK = 128
        rST = [Res("st%d" % i) for i in range(NBLK)]
        st_state = {"n": 0}

        def st_reset():
            pass

        def st_alloc(w=1):
            assert w <= 4
            i = st_state["n"] % NBLK
            st_state["n"] += 1
            return ST[:, 4 * i:4 * i + w], rST[i]

        def act(fn, reads, writes):
            return fw.op("act", fn, reads, writes)

        def dve(fn, reads, writes):
            return fw.op("dve", fn, reads, writes)

        def pe(fn, reads, writes, signal=True):
            return fw.op("pe", fn, reads, writes, signal)

        def rstd_from_ss(ss, rss, n, w=1):
            la, rla = st_alloc(w)
            rs, rrs = st_alloc(w)
            act(lambda e: e.activation(out=la, in_=ss, func=AF.Ln, scale=1.0 / n, bias=cst[:, 0:1]), [rss, rcst], [rla])
            act(lambda e: e.activation(out=rs, in_=la, func=AF.Exp, scale=-0.5), [rla], [rrs])
            return rs, rrs

        def sigmoid_from(ps_ap, rps, out_ap, rout, w):
            act(lambda e: e.activation(out=out_ap, in_=ps_ap, func=AF.Exp, scale=-1.0), [rps], [rout])
            act(lambda e: e.activation(out=out_ap, in_=out_ap, func=AF.Ln, bias=cst[:, 1:2]), [rout, rcst], [rout])
            act(lambda e: e.activation(out=out_ap, in_=out_ap, func=AF.Exp, scale=-1.0), [rout], [rout])

        def transposes(srcs, rsrc, nparts_out=128):
            b = trRR.next()
            v = pst(b)
            n = len(srcs)
            for i, s in enumerate(srcs):
                F = s.shape[-1]
                pe(lambda e, i=i, s=s, F=F: e.transpose(out=v[0:F, i, :], in_=s, identity=identb[:]),
                   list(rsrc) + [ridb], [rPS[b]], signal=(i == n - 1))
            return b, v

        def load_wo(l):
            fw.dma("pool", Wo[l][:], w_out[l].rearrange("(k p) n -> p k n", p=128), writes=[rWo[l]], sem=rWo[l].sem)

        fw.dma("sp", identf[:], ident_d, writes=[ridf], sem=ridf.sem)
        dve(lambda e: e.tensor_copy(out=identb[:], in_=identf[:]), [ridf], [ridb])
        dve(lambda e: e.memset(onesf[:], 1.0), [], [rones])
        dve(lambda e: e.memset(cst[:, 0:1], EPS), [], [rcst])
        dve(lambda e: e.memset(cst[:, 1:2], 1.0), [], [rcst])
        fw.dma("sp", ropd[:], ropd_d, writes=[rropd], sem=rropd.sem)
        fw.dma("sp", ropm[:], ropm_d, writes=[rropm], sem=rropm.sem)
        fw.dma("sp", selh[:], sel_d, writes=[rsel], sem=rsel.sem)
        fw.dma("sp", cT[:], condT, writes=[rcT], sem=rcT.sem)
        fw.op("pool", lambda e: e.memset(Vm[:], 1.0), [], rVm)
        fw.op("pool", lambda e: e.memset(Vd[:], 1.0), [], rVd)
        fw.op("pool", lambda e: e.memset(KT[:], 0.0), [], rKT)
        fw.op("pool", lambda e: e.memset(KdT[:], 0.0), [], rKdT)
        fw.op("pool", lambda e: e.memset(QT[:], 0.0), [], rQT)
        fw.op("pool", lambda e: e.memset(uT[:], 0.0), [], ruT + ruH)
        dve(lambda e: e.memset(ubnd[:], 0.0), [], [rubnd])
        st_reset()
        sigmoid_from(cT[:], rcT, ctmp[:], rctmp, 16)
        dve(lambda e: e.tensor_tensor(out=scT[:], in0=cT[:], in1=ctmp[:], op=ALU.mult), [rcT, rctmp], [rscT])

        for l in range(2):
            fw.dma("sp", abT[l][:], ada_bT[l], writes=[rabT[l]], sem=rabT[l].sem)
            fw.dma("sp", gTn[l][:], gT_d[l], writes=[rgTn[l]], sem=rgTn[l].sem)
            fw.dma("sp", qn[l][:], qnT[l], writes=[rqn[l]], sem=rqn[l].sem)
            fw.dma("sp", cw[l][:], cwT[l], writes=[rcw[l]], sem=rcw[l].sem)
            fw.dma("sp", kvnb[l][:], kvn[l].partition_broadcast(128).rearrange("p o n -> p (o n)"),
                   writes=[rkvnb[l]], sem=rkvnb[l].sem)
            fw.dma("sp", subb[l][:], sub_d[l].partition_broadcast(128).rearrange("p o n -> p (o n)"),
                   writes=[rsubb[l]], sem=rsubb[l].sem)
            li = 0.8 - 0.6 * math.exp(-0.3 * l)
            dve(lambda e, l=l, li=li: e.tensor_scalar(out=subb[l][:], in0=subb[l][:], scalar1=1.0 - li, scalar2=None,
                                                     op0=ALU.mult), [rsubb[l]], [rsubb[l]])
            fw.dma("pool", Wukv[l][:], w_ukv[l], writes=[rWukv[l]], sem=rWukv[l].sem)
            fw.dma("sp", wuqf, w_uq[l].rearrange("(k p) n -> p k n", p=128), writes=[rwuqf], sem=rwuqf.sem)
            for kc in range(2):
                dve(lambda e, l=l, kc=kc: e.tensor_scalar(out=Wuq[l][:, kc, :], in0=wuqf[:, kc, :],
                                                         scalar1=qn[l][:, kc:kc + 1], scalar2=None, op0=ALU.mult),
                    [rwuqf, rqn[l]], [rWuq[l]])
            fw.dma("sp", scr[1][:, 0:256], lam_d[l].partition_broadcast(128).rearrange("p o n -> p (o n)"),
                   writes=[rscr[1]], sem=rscr[1].sem)
            rlamb = rscr[1]
            lv = scr[1][:, 0:256].rearrange("p (a b d) -> p a b d", a=2, b=2)
            sc0 = scr[0]
            dve(lambda e: e.tensor_tensor(out=sc0[:, 0:128].rearrange("p (a d) -> p a d", a=2), in0=lv[:, :, 0, :],
                                          in1=lv[:, :, 1, :], op=ALU.mult), [rlamb], [rscr[0]])
            dve(lambda e, l=l: e.reduce_sum(out=lamt[l][:, 0:2], in_=sc0[:, 0:128].rearrange("p (a d) -> p a d", a=2),
                                            axis=AX.X), [rscr[0]], [rlamt[l]])
            act(lambda e, l=l: e.activation(out=lamt[l][:, 0:2], in_=lamt[l][:, 0:2], func=AF.Exp), [rlamt[l]], [rlamt[l]])
            dve(lambda e, l=l: e.tensor_tensor(out=lamt[l][:, 2:3], in0=lamt[l][:, 1:2], in1=lamt[l][:, 0:1],
                                               op=ALU.subtract), [rlamt[l]], [rlamt[l]])
            dve(lambda e, l=l, li=li: e.tensor_scalar(out=lamt[l][:, 3:4], in0=lamt[l][:, 2:3], scalar1=-li,
                                                     scalar2=None, op0=ALU.add), [rlamt[l]], [rlamt[l]])
            if l == 0:
                bmod = inRR.next()
            mps = PS[bmod][:, l * 12:(l + 1) * 12].rearrange("p (j c) -> p j c", c=2)
            for ch in range(2):
                wb = ch % 2
                fw.dma("pool", Wc[wb][:, :, 0:384], ada_w[l][:, ch * 384:(ch + 1) * 384].rearrange("(k p) n -> p k n", p=128),
                       writes=[rWc[wb]], sem=rWc[wb].sem)
                for jj in range(3):
                    j = ch * 3 + jj
                    for k in range(8):
                        pe(lambda e, k=k, jj=jj, j=j, wb=wb, mps=mps: e.matmul(
                            mps[:, j, :], lhsT=Wc[wb][:, k, jj * 128:(jj + 1) * 128], rhs=scT[:, k, :],
                            start=(k == 0), stop=(k == 7)), [rWc[wb], rscT], [rPS[bmod]], signal=(k == 7))
            dve(lambda e, l=l, mps=mps: e.tensor_tensor(out=modp[:, l, :, :], in0=mps,
                                                        in1=abT[l][:].unsqueeze(2).to_broadcast([128, 6, 2]), op=ALU.add),
                [rPS[bmod], rabT[l]], [rmodp])
        fw.dma("sp", mod_loc.ap(), modp[:].rearrange("p l j c -> p (l j c)"), reads=[rmodp], writes=[rmodloc], sem=rmodloc.sem)
        fw._waits("pool", [rmodloc], [])
        nc.gpsimd.collective_compute("AllGather", ALU.bypass, replica_groups=[[0, 1, 2, 3], [4, 5, 6, 7]],
                                     ins=[mod_loc.ap().opt()], outs=[mod_all.ap().opt()]).then_inc(cc_mod)
        load_wo(0)
        load_wo(1)
        def finish_mod_a():
            nc.sync.wait_ge(cc_mod, 1)
            fw.dma("sp", moda[:], mod_all.ap().rearrange("(r p) c -> p r c", p=128), writes=[rmoda], sem=rmoda.sem)
            for l in range(2):
                dve(lambda e, l=l: e.tensor_copy(out=modT[l][:].rearrange("p (r j) c -> p r j c", r=4),
                                                 in_=moda[:, :, l * 12:(l + 1) * 12].rearrange("p r (j c) -> p r j c", c=2)),
                    [rmoda], [rmodT[l]])


        def bcast_rows(vT_ap, rv, out_t, rout):
            for half in range(2):
                di = scrRR.next()
                dve(lambda e, di=di, half=half: e.tensor_tensor(
                    out=scr[di][:].rearrange("p (a c) -> p a c", a=4), in0=identf[:].unsqueeze(1).to_broadcast([128, 4, 128]),
                    in1=vT_ap[:, half * 4:half * 4 + 4].unsqueeze(2).to_broadcast([128, 4, 128]), op=ALU.mult),
                    [ridf, rv], [rscr[di]])
                b = inRR.next()
                pe(lambda e, di=di, b=b: e.matmul(PS[b][:], lhsT=onesf[:], rhs=scr[di][:],
                                                  start=True, stop=True), [rones, rscr[di]], [rPS[b]])
                act(lambda e, b=b, half=half: e.copy(out=out_t[:, half * 512:(half + 1) * 512], in_=PS[b][:]),
                    [rPS[b]], [rout])

        def finish_mod_b():
            fw.dma("sp", Abc[:], fn_d.partition_broadcast(128).rearrange("p o n -> p (o n)"), writes=[rAbc], sem=rAbc.sem)
            for l in range(2):
                for c in range(2):
                    dve(lambda e, l=l, c=c: e.scalar_tensor_tensor(out=ATt[:, l, c, :], in0=modT[l][:, 8:16, c], scalar=1.0,
                                                                   in1=gTn[l][:], op0=ALU.add, op1=ALU.mult),
                        [rmodT[l], rgTn[l]], [rAT])


        def phase_gate(l, c):
            bcast_rows(modT[l][:, 16:24, c], rmodT[l], gateB[c], rgate[c])

        wstate = {"i": 0}

        PASSES = [(0, list(range(8))), (0, [0, 1, 2, 3]), (1, list(range(8))), (0, [4, 5, 6, 7]),
                  (0, list(range(8))), (1, [0, 1, 2, 3]), (1, list(range(8))), (1, [4, 5, 6, 7])]
        pstate = {"i": 0, "pre": {}}

        def load_chunk(l, ci):
            wb = wstate["i"] % 2
            wstate["i"] += 1
            c0, cwid = CH[ci]
            fw.dma("pool", Wc[wb][:, :, 0:cwid], w_in[l][:, c0:c0 + cwid].rearrange("(k p) n -> p k n", p=128),
                   writes=[rWc[wb]], sem=rWc[wb].sem)
            return wb

        def inproj(wb, ci, ht, brr=None):
            cwid = CH[ci][1]
            b = (brr or ipRR).next()
            for k in range(8):
                pe(lambda e, k=k: e.matmul(PS[b][:, 0:cwid], lhsT=hT[:, ht, k, :], rhs=Wc[wb][:, k, 0:cwid],
                                           start=(k == 0), stop=(k == 7)), [rhT[ht][k], rWc[wb]], [rPS[b]], signal=(k == 7))
            return b

        def rope_apply(src, rsrc, G, dim, tab, rtab, tl, out_ap, rout, pre_scale=None):
            hq = dim // 4
            rd = list(rsrc) + [rtab]
            if pre_scale is not None:
                s0 = scrRR.next()
                sc_ = scr[s0][:, 0:G * dim].rearrange("p (g d) -> p g d", g=G)
                dve(lambda e: e.tensor_scalar(out=sc_, in0=src, scalar1=pre_scale[0], scalar2=None, op0=ALU.mult),
                    list(rsrc) + [pre_scale[1]], [rscr[s0]])
                src = sc_
                rd = [rscr[s0], rtab]
            s1 = scrRR.next(); s2 = scrRR.next()
            t1 = scr[s1][:, 0:G * dim].rearrange("p (g d) -> p g d", g=G)
            t2 = scr[s2][:, 0:G * dim].rearrange("p (g d) -> p g d", g=G)
            cosb = tab[:, tl, 0, :].unsqueeze(1).to_broadcast([128, G, dim])
            dve(lambda e: e.tensor_tensor(out=t1, in0=src, in1=cosb, op=ALU.mult), rd, [rscr[s1]])
            sv = src.rearrange("p g (a two q) -> p g a two q", two=2, q=hq)
            tv = t2.rearrange("p g (a two q) -> p g a two q", two=2, q=hq)
            sn = tab[:, tl, 1, :].rearrange("p (a two q) -> p a two q", two=2, q=hq)
            for two in range(2):
                o_ = tv[:, :, :, two, :]
                i_ = sv[:, :, :, 1 - two, :]
                s_ = sn[:, :, two, :].unsqueeze(1).to_broadcast([128, G, 2, hq])
                dve(lambda e, o_=o_, i_=i_, s_=s_: e.tensor_tensor(out=o_, in0=i_, in1=s_, op=ALU.mult), rd, [rscr[s2]])
            dve(lambda e: e.tensor_tensor(out=out_ap, in0=t1, in1=t2, op=ALU.add), [rscr[s1], rscr[s2]], [rout])

        def k_diff(src, rsrc, slot, rope_tl, out_dst):
            if out_dst is not None:
                si = scrRR.next()
                act(lambda e: e.copy(out=scr[si][:], in_=src), rsrc, [rscr[si]])
                fw.dma("sp", out_dst, scr[si][:], reads=[rscr[si]], sem=rscr[si].sem)
            bi = b512RR.next()
            if rope_tl is None:
                dve(lambda e: e.tensor_copy(out=b512[bi][:], in_=src), rsrc, [rb512[bi]])
            else:
                rope_apply(src.rearrange("p (g d) -> p g d", g=8), rsrc, 8, 64, ropd, rropd, rope_tl,
                           b512[bi][:].rearrange("p (g d) -> p g d", g=8), rb512[bi])
            yield
            b, v = transposes([b512[bi][:, h * 128:(h + 1) * 128] for h in range(4)], [rb512[bi]])
            act(lambda e: e.copy(out=KdT[:, :, slot * 128:(slot + 1) * 128], in_=v[:, 0:4, :]), [rPS[b]], [rKdT[slot]])

        def v_diff(src, rsrc, slot, out_dst):
            if out_dst is not None:
                si = scrRR.next()
                act(lambda e: e.copy(out=scr[si][:], in_=src), rsrc, [rscr[si]])
                fw.dma("sp", out_dst, scr[si][:], reads=[rscr[si]], sem=rscr[si].sem)
            dve(lambda e: e.tensor_copy(out=Vd[:, slot, :, 0:128], in_=src.rearrange("p (h d) -> p h d", h=4)),
                rsrc, [rVd[slot]])
            return
            yield

        def k_mla(l, ckv_src, kr_src, rsrc, slot, rope_tl, normalize, out_ckv, out_kr):
            ci = ckvRR.next()
            if normalize:
                ss, rss = st_alloc()
                act(lambda e: e.activation(out=junk[:, 0:128], in_=ckv_src, func=AF.Square, accum_out=ss),
                    list(rsrc) + [rss], [rjunk, rss])
                rs, rrs = rstd_from_ss(ss, rss, 128)
                gi = stgcRR.next()
                dve(lambda e: e.scalar_tensor_tensor(out=stgc[gi][:, 0:128], in0=ckv_src, scalar=rs, in1=kvnb[l][:],
                                                     op0=ALU.mult, op1=ALU.mult), list(rsrc) + [rrs, rkvnb[l]], [rstgc[gi]])
                if out_ckv is not None:
                    act(lambda e: e.copy(out=stgc[gi][:, 128:160], in_=kr_src), rsrc, [rstgc[gi]])
                    fw.dma("sp", out_ckv, stgc[gi][:, 0:128], reads=[rstgc[gi]], sem=rstgc[gi].sem)
                    fw.dma("sp", out_kr, stgc[gi][:, 128:160], reads=[rstgc[gi]], sem=rstgc[gi].sem)
                dve(lambda e: e.tensor_copy(out=ckvb[ci][:], in_=stgc[gi][:, 0:128]), [rstgc[gi]], [rckvb[ci]])
            else:
                dve(lambda e: e.tensor_copy(out=ckvb[ci][:], in_=ckv_src), rsrc, [rckvb[ci]])
            if rope_tl is None:
                dve(lambda e: e.tensor_copy(out=krb[ci][:], in_=kr_src), rsrc, [rkrb[ci]])
            else:
                rope_apply(kr_src.rearrange("p (g d) -> p g d", g=1), rsrc, 1, 32, ropm, rropm, rope_tl,
                           krb[ci][:].rearrange("p (g d) -> p g d", g=1), rkrb[ci])
            yield
            b, v = transposes([ckvb[ci][:]], [rckvb[ci]])
            act(lambda e: e.copy(out=ckvT[ci][:], in_=v[:, 0, :]), [rPS[b]], [rckvT[ci]])
            yield
            b2 = 4
            pe(lambda e: e.matmul(PS[b2][:], lhsT=ckvT[ci][:], rhs=Wukv[l][:], start=True, stop=True),
               [rckvT[ci], rWukv[l]], [rPS[b2]])
            pv = PS[b2][:].rearrange("p (h d) -> p h d", h=4)
            ki = KfRR.next()
            dve(lambda e: e.tensor_copy(out=Kf[ki][:, :, 0:64], in_=pv[:, :, 0:64]), [rPS[b2]], [rKf[ki]])
            act(lambda e: e.copy(out=Vm[:, slot, :, 0:64], in_=pv[:, :, 64:128]), [rPS[b2]], [rVm[slot]])
            dve(lambda e: e.tensor_copy(out=Kf[ki][:, :, 64:96], in_=krb[ci][:].unsqueeze(1).to_broadcast([128, 4, 32])),
                [rkrb[ci]], [rKf[ki]])
            yield
            b3, v3 = transposes([Kf[ki][:, h, :] for h in range(4)], [rKf[ki]])
            act(lambda e: e.copy(out=KT[0:96, :, slot * 128:(slot + 1) * 128], in_=v3[0:96, 0:4, :]), [rPS[b3]], [rKT[slot]])

        def stage_norm(G, l, t):
            for _ in norm_gen(G, l, t):
                pass

        def norm_gen(G, l, t):
            xi = G["X"][t]
            c = G["c"]
            if l == 0 and not G.get("xpre"):
                fw.dma("sp", X[xi][:], G["xsrc"][t], writes=[rX[xi]], sem=rX[xi].sem)
            ss, rss = st_alloc()
            act(lambda e: e.activation(out=E[0][:, 0:4, :].rearrange("p a c -> p (a c)"), in_=X[xi][:], func=AF.Square,
                                       accum_out=ss), [rX[xi], rss], [rE[0], rss])
            rs, rrs = rstd_from_ss(ss, rss, D)
            yield
            hi = hbRR.next()
            dve(lambda e: e.tensor_scalar(out=hb[hi][:], in0=X[xi][:], scalar1=rs, scalar2=None, op0=ALU.mult),
                [rX[xi], rrs], [rhb[hi]])
            b, v = transposes([hb[hi][:, k * 128:(k + 1) * 128] for k in range(8)], [rhb[hi]])
            ht = G["ht"][t]
            for k in range(8):
                if k % 2 == 0:
                    act(lambda e, k=k: e.activation(out=hT[:, ht, k, :], in_=v[:, k, :], func=AF.Identity,
                                                    scale=ATt[:, l, c, k:k + 1], bias=modT[l][:, k, c:c + 1]),
                        [rPS[b], rAT, rmodT[l]], [rhT[ht][k]])
                else:
                    dve(lambda e, k=k: e.tensor_scalar(out=hT[:, ht, k, :], in0=v[:, k, :], scalar1=ATt[:, l, c, k:k + 1],
                                                       scalar2=modT[l][:, k, c:c + 1], op0=ALU.mult, op1=ALU.add),
                        [rPS[b], rAT, rmodT[l]], [rhT[ht][k]])

        def consume_K(G, l, ci, t, b):
            rope = G["rope"]
            slot = G["kslot"][t]
            rps = [rPS[b]]
            if ci == 0:
                dst = None if rope else G["o_dk"](l, t)
                yield from k_diff(PS[b][:], rps, slot, (t if rope else None), dst)
            elif ci == 1:
                dst = None if rope else G["o_dv"](l, t)
                yield from v_diff(PS[b][:], rps, slot, dst)
            elif ci == 2:
                yield from k_mla(l, PS[b][:, 0:128], PS[b][:, 128:160], rps, slot, (t if rope else None), True,
                      None if rope else G["o_ckv"](l, t), None if rope else G["o_kr"](l, t))
            else:
                xi = scrRR.next()
                act(lambda e: e.copy(out=scr[xi][:, 0:256], in_=PS[b][:, 0:256]), rps, [rscr[xi]])
                ui = b256RR.next()
                dve(lambda e: e.tensor_tensor(out=b256[ui][:], in0=PS[b][:, 256:512], in1=scr[xi][:, 0:256], op=ALU.mult),
                    rps + [rscr[xi]], [rb256[ui]])
                yield
                bt, v = transposes([b256[ui][:, k * 128:(k + 1) * 128] for k in range(2)], [rb256[ui]])
                sq, hf = G["us"] + t // 2, t % 2
                act(lambda e: e.copy(out=uT[:, sq, :, 1 + hf * 128:1 + (hf + 1) * 128], in_=v[:, 0:2, :]),
                    [rPS[bt]], [ruT[2 * G["us"] + t]])

        def consume_Q(G, l, ci, t, b):
            rope = G["rope"]
            rps = [rPS[b]]
            if False:
                yield
            if ci == 4:
                ss, rss = st_alloc()
                act(lambda e: e.activation(out=junk[:, 0:256], in_=PS[b][:, 0:256], func=AF.Square, accum_out=ss),
                    rps + [rss], [rjunk, rss])
                rs, rrs = rstd_from_ss(ss, rss, 256)
                qi = b256RR.next()
                dve(lambda e: e.tensor_copy(out=b256[qi][:], in_=PS[b][:, 0:256]), rps, [rb256[qi]])
                si = scrRR.next()
                sigmoid_from(PS[b][:, 256:512], rPS[b], scr[si][:, 0:256], rscr[si], 256)
                dve(lambda e: e.tensor_tensor(out=SZ[:, t, 0:256], in0=PS[b][:, 256:512], in1=scr[si][:, 0:256],
                                              op=ALU.mult), rps + [rscr[si]], [rSZ[t]])
                yield
                bt, v = transposes([b256[qi][:, k * 128:(k + 1) * 128] for k in range(2)], [rb256[qi]])
                ci_ = cqRR.next()
                act(lambda e: e.copy(out=cqT[ci_][:], in_=v[:, 0:2, :]), [rPS[bt]], [rcqT[ci_]])
                yield
                b5 = 5
                for kc in range(2):
                    pe(lambda e, kc=kc: e.matmul(PS[b5][:, 0:384], lhsT=cqT[ci_][:, kc, :], rhs=Wuq[l][:, kc, :],
                                                 start=(kc == 0), stop=(kc == 1)), [rcqT[ci_], rWuq[l]], [rPS[b5]],
                       signal=(kc == 1))
                qv = PS[b5][:, 0:384].rearrange("p (h d) -> p h d", h=4)
                fi = QfRR.next()
                if not rope:
                    dve(lambda e: e.tensor_scalar(out=Qf[fi][:], in0=qv, scalar1=rs, scalar2=None, op0=ALU.mult),
                        [rPS[b5], rrs], [rQf[fi]])
                else:
                    dve(lambda e: e.tensor_scalar(out=Qf[fi][:, :, 0:64], in0=qv[:, :, 0:64], scalar1=rs,
                                                  scalar2=None, op0=ALU.mult), [rPS[b5], rrs], [rQf[fi]])
                    rope_apply(qv[:, :, 64:96], [rPS[b5]], 4, 32, ropm, rropm, t, Qf[fi][:, :, 64:96], rQf[fi],
                               pre_scale=(rs, rrs))
                yield
                b6, v6 = transposes([Qf[fi][:, h, :] for h in range(4)], [rQf[fi]])
                act(lambda e: e.copy(out=QT[0:96, :, t * 128:(t + 1) * 128], in_=v6[0:96, 0:4, :]), [rPS[b6]], [rQT[t]])
            elif ci == 5:
                bi = b512RR.next()
                if not rope:
                    dve(lambda e: e.tensor_copy(out=b512[bi][:], in_=PS[b][:]), rps, [rb512[bi]])
                else:
                    rope_apply(PS[b][:].rearrange("p (g d) -> p g d", g=8), rps, 8, 64, ropd, rropd, t,
                               b512[bi][:].rearrange("p (g d) -> p g d", g=8), rb512[bi])
                yield
                bt, v = transposes([b512[bi][:, h * 128:(h + 1) * 128] for h in range(4)], [rb512[bi]])
                act(lambda e: e.copy(out=QdT[:, :, t * 128:(t + 1) * 128], in_=v[:, 0:4, :]), [rPS[bt]], [rQdT[t]])
            elif ci == 6:
                si = scrRR.next()
                sigmoid_from(PS[b][:, 256:512], rPS[b], scr[si][:, 0:256], rscr[si], 256)
                dve(lambda e: e.tensor_tensor(out=scr[si][:, 0:256], in0=PS[b][:, 256:512], in1=scr[si][:, 0:256],
                                              op=ALU.mult), rps + [rscr[si]], [rscr[si]])
                gi = b256RR.next()
                dve(lambda e: e.tensor_tensor(out=b256[gi][:], in0=PS[b][:, 0:256], in1=scr[si][:, 0:256], op=ALU.mult),
                    rps + [rscr[si]], [rb256[gi]])
                yield
                bt, v = transposes([b256[gi][:, k * 128:(k + 1) * 128] for k in range(2)], [rb256[gi]])
                act(lambda e: e.copy(out=gT[:, :, t * 128:(t + 1) * 128], in_=v[:, 0:2, :]), [rPS[bt]], [rgT[t]])
            else:
                si = scrRR.next()
                sigmoid_from(PS[b][:], rPS[b], scr[si][:], rscr[si], 512)
                dve(lambda e: e.tensor_tensor(out=scr[si][:], in0=PS[b][:], in1=scr[si][:], op=ALU.mult),
                    rps + [rscr[si]], [rscr[si]])
                dve(lambda e: e.tensor_tensor(out=SZ[:, t, 256:768].rearrange("p (h d) -> p h d", h=4),
                                              in0=scr[si][:].rearrange("p (h d) -> p h d", h=4),
                                              in1=subb[l][:].unsqueeze(1).to_broadcast([128, 4, 128]), op=ALU.mult),
                    [rscr[si], rsubb[l]], [rSZ[t]])

        def step_all(lst):
            for g_ in list(lst):
                try:
                    next(g_)
                except StopIteration:
                    lst.remove(g_)

        def stage_inproj(G, l, chunks, extra=None, extra_delay=0):
            for _ in inproj_gen(G, l, chunks, extra=extra, extra_delay=extra_delay):
                pass

        def inproj_gen(G, l, chunks, extra=None, brr=None, extra_delay=0):
            nt = G["nt"]
            items = [(ci, t) for ci in chunks for t in range(nt)]
            assert PASSES[pstate["i"]] == (l, chunks), (pstate["i"], l, chunks)
            wbuf = dict(pstate["pre"])
            pstate["pre"] = {}
            if chunks[0] not in wbuf:
                wbuf[chunks[0]] = load_chunk(l, chunks[0])

            def issue(j):
                ci, t = items[j]
                if t == 0:
                    idx = chunks.index(ci)
                    if idx + 1 < len(chunks) and chunks[idx + 1] not in wbuf:
                        wbuf[chunks[idx + 1]] = load_chunk(l, chunks[idx + 1])
                return inproj(wbuf[ci], ci, G["ht"][t], brr)

            active = list(extra) if (extra and extra_delay == 0) else []
            for j in range(len(items)):
                b = issue(j)
                if extra and extra_delay > 0 and j >= extra_delay and (j - extra_delay) < len(extra):
                    active.append(extra[j - extra_delay])
                step_all(active)
                ci, t = items[j]
                gen = consume_K(G, l, ci, t, b) if ci < 4 else consume_Q(G, l, ci, t, b)
                try:
                    next(gen)
                    active.append(gen)
                except StopIteration:
                    pass
                yield
            while active:
                step_all(active)
                yield
            pstate["i"] += 1
            if pstate["i"] < len(PASSES):
                ln, chn = PASSES[pstate["i"]]
                for cj in chn[0:2]:
                    pstate["pre"][cj] = load_chunk(ln, cj)

        def stage_K(G, l):
            stage_inproj(G, l, [0, 1, 2, 3])

        def stage_Q(G, l):
            stage_inproj(G, l, [4, 5, 6, 7])

        def conv_seq(G, l, sq):
            t0 = 2 * sq
            for blk in range(2):
                yi = scrRR.next()
                us = G["us"] + sq
                u = uT[:, us, blk, :]
                rd = [ruT[2 * us], ruT[2 * us + 1], ruH[us], rcw[l]]
                dve(lambda e: e.tensor_scalar(out=scr[yi][:, 0:256], in0=u[:, 0:256], scalar1=cw[l][:, blk * 3:blk * 3 + 1],
                                              scalar2=None, op0=ALU.mult), rd, [rscr[yi]])
                for j in (1, 2):
                    dve(lambda e, j=j: e.scalar_tensor_tensor(out=scr[yi][:, 0:256], in0=u[:, j:j + 256],
                                                              scalar=cw[l][:, blk * 3 + j:blk * 3 + j + 1], in1=scr[yi][:, 0:256],
                                                              op0=ALU.mult, op1=ALU.add), rd + [rscr[yi]], [rscr[yi]])
                g_ = gT[:, blk, t0 * 128:(t0 + 2) * 128]
                dve(lambda e: e.tensor_tensor(out=g_, in0=scr[yi][:, 0:256], in1=g_, op=ALU.mult),
                    [rscr[yi], rgT[t0], rgT[t0 + 1]], [rgT[t0], rgT[t0 + 1]])

        def attention_seq(G, l, sq, slots, extra=None, filler=None):
            t0 = 2 * sq
            qc = slice(t0 * 128, t0 * 128 + 256)
            ns = len(slots)
            units = [("d", h, a) for h in range(4) for a in range(2)] + [("m", h, 0) for h in range(4)]
            ebuf = {}

            def accbanks(u):
                if u[0] == "m" or filler is not None:
                    return (6, 7)
                return (0, 1) if u[1] % 2 == 0 else (6, 7)

            deep = (filler is None and ns == 2)

            def s_phase(u):
                kind, h, a = u
                ei = (ERR3 if deep else ERR).next()
                ebuf[u] = ei
                for p0 in range(0, ns, 2):
                    b = (sRR3 if deep else sRR).next()
                    pair = slots[p0:p0 + 2]
                    for j, slot in enumerate(pair):
                        if kind == "m":
                            pe(lambda e, b=b, slot=slot, j=j: e.matmul(
                                PS[b][:, j * 256:(j + 1) * 256], lhsT=KT[0:96, h, slot * 128:(slot + 1) * 128],
                                rhs=QT[0:96, h, qc], start=True, stop=True),
                               [rKT[slot], rQT[t0], rQT[t0 + 1]], [rPS[b]], signal=(j == len(pair) - 1))
                            sc = MLA_SCALE
                        else:
                            pe(lambda e, b=b, slot=slot, j=j: e.matmul(
                                PS[b][:, j * 256:(j + 1) * 256], lhsT=KdT[a * 64:(a + 1) * 64, h, slot * 128:(slot + 1) * 128],
                                rhs=QdT[a * 64:(a + 1) * 64, h, qc], start=True, stop=True),
                               [rKdT[slot], rQdT[t0], rQdT[t0 + 1]], [rPS[b]], signal=(j == len(pair) - 1))
                            sc = DIFF_SCALE
                    w = 256 * len(pair)
                    act(lambda e, b=b, p0=p0, sc=sc, w=w: e.activation(
                        out=E[ei][:, p0:p0 + len(pair), :].rearrange("p s q -> p (s q)"), in_=PS[b][:, 0:w], func=AF.Exp,
                        scale=sc), [rPS[b]], [rE[ei]])

            def av_phase(u):
                kind, h, a = u
                ei = ebuf[u]
                ab = accbanks(u)
                for qt in range(2):
                    for si, slot in enumerate(slots):
                        if kind == "m":
                            pe(lambda e, qt=qt, si=si, slot=slot: e.matmul(
                                PS[ab[qt]][:, h * 65:(h + 1) * 65], lhsT=E[ei][:, si, qt * 128:(qt + 1) * 128],
                                rhs=Vm[:, slot, h, :], start=(si == 0), stop=(si == ns - 1)),
                               [rE[ei], rVm[slot]], [rPS[ab[qt]]], signal=(si == ns - 1))
                        else:
                            pe(lambda e, qt=qt, si=si, slot=slot: e.matmul(
                                PS[ab[qt]][:, a * 129:(a + 1) * 129], lhsT=E[ei][:, si, qt * 128:(qt + 1) * 128],
                                rhs=Vd[:, slot, h, :], start=(si == 0), stop=(si == ns - 1)),
                               [rE[ei], rVd[slot]], [rPS[ab[qt]]], signal=(si == ns - 1))

            def post(u):
                kind, h, a = u
                ab = accbanks(u)
                if False:
                    yield
                if kind == "m" and h == 3:
                    avs = [PS[ab[qt]][:, 0:260].rearrange("p (h d) -> p h d", h=4) for qt in range(2)]
                    rzs = [st_alloc(4) for qt in range(2)]
                    sis = [scrRR.next() for qt in range(2)]
                    for qt in range(2):
                        dve(lambda e, qt=qt: e.reciprocal(out=rzs[qt][0].unsqueeze(2), in_=avs[qt][:, :, 64:65]),
                            [rPS[ab[qt]]], [rzs[qt][1]])
                    for qt in range(2):
                        ov = scr[sis[qt]][:, 0:256].rearrange("p (h d) -> p h d", h=4)
                        dve(lambda e, qt=qt, ov=ov: e.tensor_tensor(out=ov, in0=avs[qt][:, :, 0:64],
                                                                   in1=rzs[qt][0].unsqueeze(2).to_broadcast([128, 4, 64]),
                                                                   op=ALU.mult), [rPS[ab[qt]], rzs[qt][1]], [rscr[sis[qt]]])
                    for qt in range(2):
                        t = t0 + qt
                        dve(lambda e, t=t, qt=qt: e.tensor_tensor(out=mix[t][:, 0:256], in0=scr[sis[qt]][:, 0:256],
                                                                 in1=SZ[:, t, 0:256], op=ALU.mult),
                            [rscr[sis[qt]], rSZ[t]], [rmix[t][0]])
                if kind == "d" and a == 1:
                    st_ = []
                    avs = [PS[ab[qt]][:, 0:258].rearrange("p (a d) -> p a d", a=2) for qt in range(2)]
                    rzs = [st_alloc(2) for qt in range(2)]
                    pis = [phRR.next() for qt in range(2)]
                    for qt in range(2):
                        dve(lambda e, qt=qt: e.reciprocal(out=rzs[qt][0].unsqueeze(2), in_=avs[qt][:, :, 128:129]),
                            [rPS[ab[qt]]], [rzs[qt][1]])
                    for qt in range(2):
                        dve(lambda e, qt=qt: e.tensor_tensor(out=rzs[qt][0][:, 1:2], in0=rzs[qt][0][:, 1:2], in1=lamt[l][:, 3:4],
                                                            op=ALU.mult), [rzs[qt][1], rlamt[l]], [rzs[qt][1]])
                    for qt in range(2):
                        dve(lambda e, qt=qt: e.tensor_scalar(out=ph[pis[qt]][:, 0:128], in0=avs[qt][:, 0, 0:128],
                                                            scalar1=rzs[qt][0][:, 0:1], scalar2=None, op0=ALU.mult),
                            [rPS[ab[qt]], rzs[qt][1]], [rph[pis[qt]]])
                    for qt in range(2):
                        dve(lambda e, qt=qt: e.scalar_tensor_tensor(out=ph[pis[qt]][:, 128:256], in0=avs[qt][:, 1, 0:128],
                                                                   scalar=rzs[qt][0][:, 1:2], in1=ph[pis[qt]][:, 0:128],
                                                                   op0=ALU.mult, op1=ALU.add),
                            [rPS[ab[qt]], rzs[qt][1], rph[pis[qt]]], [rph[pis[qt]]])
                    for qt in range(2):
                        st_.append((t0 + qt, pis[qt]))
                    yield
                    ss, rss = st_alloc(2)
                    for i_, (t, pi) in enumerate(st_):
                        act(lambda e, pi=pi, i_=i_: e.activation(out=junk[:, 0:128], in_=ph[pi][:, 128:256], func=AF.Square,
                                                                 accum_out=ss[:, i_:i_ + 1]), [rph[pi], rss], [rjunk, rss])
                    yield
                    la_, rla_ = st_alloc(2)
                    rs, rrs = st_alloc(2)
                    act(lambda e: e.activation(out=la_, in_=ss, func=AF.Ln, scale=1.0 / 128, bias=cst[:, 0:1]), [rss, rcst], [rla_])
                    yield
                    act(lambda e: e.activation(out=rs, in_=la_, func=AF.Exp, scale=-0.5), [rla_], [rrs])
                    yield
                    c0 = 256 + h * 128
                    for i_, (t, pi) in enumerate(st_):
                        dve(lambda e, t=t, pi=pi, i_=i_: e.scalar_tensor_tensor(out=mix[t][:, c0:c0 + 128], in0=ph[pi][:, 128:256],
                                                                               scalar=rs[:, i_:i_ + 1], in1=SZ[:, t, c0:c0 + 128],
                                                                               op0=ALU.mult, op1=ALU.mult),
                            [rph[pi], rrs, rSZ[t]], [rmix[t][1 + h]])

            la = 2 if deep else 1
            if deep:
                trRR.items = [2]
            for j in range(la):
                s_phase(units[j])
            pend = list(extra) if extra else []
            for i, u in enumerate(units):
                if i + la < len(units):
                    s_phase(units[i + la])
                av_phase(u)
                step_all(pend)
                if filler:
                    step_all(filler)
                gen = post(u)
                try:
                    next(gen)
                    pend.append(gen)
                except StopIteration:
                    pass
            while pend:
                for g_ in list(pend):
                    try:
                        next(g_)
                    except StopIteration:
                        pend.remove(g_)
            trRR.items = [2, 3]
            for qt in range(2):
                t = t0 + qt
                mi = qt
                b, v = transposes([mix[t][:, k * 128:(k + 1) * 128] for k in range(6)], rmix[t])
                act(lambda e: e.copy(out=mixT[mi][:], in_=v[:, 0:6, :]), [rPS[b]], [rmixT[mi]])

        def outproj_tile(G, l, t, mi):
            xi = G["X"][t]
            banks = [inRR.next(), inRR.next()]
            for half in range(2):
                b = banks[half]
                for kc in range(8):
                    if kc < 2:
                        lt, rl = mixT[mi][:, kc, :], rmixT[mi]
                    elif kc < 4:
                        lt, rl = gT[:, kc - 2, t * 128:(t + 1) * 128], rgT[t]
                    else:
                        lt, rl = mixT[mi][:, kc - 2, :], rmixT[mi]
                    pe(lambda e, lt=lt, kc=kc: e.matmul(PS[b][:], lhsT=lt, rhs=Wo[l][:, kc, half * 512:(half + 1) * 512],
                                                        start=(kc == 0), stop=(kc == 7)), [rl, rWo[l]], [rPS[b]],
                       signal=(kc == 7))
                gi = scrRR.next()
                dve(lambda e, b=b, half=half, gi=gi: e.tensor_tensor(out=scr[gi][:], in0=PS[b][:],
                                                                    in1=gateB[G["c"]][:, half * 512:(half + 1) * 512], op=ALU.mult),
                    [rPS[b], rgate[G["c"]]], [rscr[gi]])
                dve(lambda e, half=half, gi=gi: e.tensor_tensor(out=X[xi][:, half * 512:(half + 1) * 512], in0=scr[gi][:],
                                                               in1=X[xi][:, half * 512:(half + 1) * 512], op=ALU.add),
                    [rscr[gi], rX[xi]], [rX[xi]])
            if l == DEPTH - 1:
                ss, rss = st_alloc()
                act(lambda e: e.activation(out=E[0][:, 0:4, :].rearrange("p a c -> p (a c)"), in_=X[xi][:], func=AF.Square, accum_out=ss), [rX[xi], rss], [rE[0], rss])
                rs, rrs = rstd_from_ss(ss, rss, D)
                for half in range(2):
                    yi = scrRR.next()
                    cs = slice(half * 512, (half + 1) * 512)
                    dve(lambda e, yi=yi, cs=cs: e.scalar_tensor_tensor(out=scr[yi][:], in0=X[xi][:, cs], scalar=rs, in1=Abc[:, cs],
                                                                      op0=ALU.mult, op1=ALU.mult), [rX[xi], rrs, rAbc], [rscr[yi]])
                    fw.dma("sp", G["ydst"][t][:, cs], scr[yi][:], reads=[rscr[yi]], sem=rscr[yi].sem)
                return None
            gen = norm_gen(G, l + 1, t)
            next(gen)
            return gen

        def prompt_group(g):
            G = {"nt": 4, "rope": False, "X": [0, 1, 2, 3], "kslot": [0, 1, 2, 3], "ht": [0, 1, 2, 3], "us": 0, "c": 0}
            G["xsrc"] = [xp[2 * g + t // 2, (t % 2) * 128:(t % 2 + 1) * 128, :] for t in range(4)]
            G["ydst"] = [y_p[2 * g + t // 2, (t % 2) * 128:(t % 2 + 1) * 128, :] for t in range(4)]
            rows = lambda t: slice((t % 2) * 128, (t % 2 + 1) * 128)
            G["o_dk"] = lambda l, t: o_dk[2 * g + t // 2, l, rows(t), :]
            G["o_dv"] = lambda l, t: o_dv[2 * g + t // 2, l, rows(t), :]
            G["o_ckv"] = lambda l, t: o_ckv[2 * g + t // 2, l, rows(t), :]
            G["o_kr"] = lambda l, t: o_kr[2 * g + t // 2, l, rows(t), :]
            return G

        PG = [prompt_group(0), prompt_group(1)]

        xsem2 = {}

        def prefetch_x(G, q="sp"):
            for t in range(G["nt"]):
                xi = G["X"][t]
                sm = rX[xi].sem
                if q == "pool":
                    if xi not in xsem2:
                        xsem2[xi] = fw.dma_sem()
                    sm = xsem2[xi]
                fw.dma(q, X[xi][:], G["xsrc"][t], writes=[rX[xi]], sem=sm)
            G["xpre"] = True

        def group_norm_gens(g):
            G = PG[g]
            gens = []
            for t in range(4):
                gen = norm_gen(G, 0, t)
                next(gen)
                gens.append(gen)
            return gens

        def run_prompt_group(g, fill_gen=None, first_extra=None):
            G = PG[g]
            carry = list(first_extra) if first_extra else []
            for l in range(DEPTH):
                st_reset()
                phase_gate(l, 0)
                stage_inproj(G, l, [0, 1, 2, 3, 4, 5, 6, 7], extra=carry, extra_delay=1)
                carry = []
                trigger_ag()
                filler = [fill_gen] if (l == 0 and fill_gen is not None) else None
                for sq in range(2):
                    conv_seq(G, l, sq)
                    attention_seq(G, l, sq, [2 * sq, 2 * sq + 1], extra=carry, filler=filler)
                    carry = []
                    if sq == 1 and filler:
                        while filler:
                            step_all(filler)
                    for qt in range(2):
                        gen = outproj_tile(G, l, 2 * sq + qt, qt)
                        if gen is not None:
                            carry.append(gen)
            assert not carry

        GS = {"nt": 2, "rope": True, "X": [4, 5], "kslot": [8, 9], "ht": [4, 5], "us": 2, "c": 1}
        GS["xsrc"] = [xs[t * 128:(t + 1) * 128, :] for t in range(2)]
        GS["ydst"] = [y_s[t * 128:(t + 1) * 128, :] for t in range(2)]
        rloc = [Res("kvloc%d" % l) for l in range(2)]
        rall = [Res("kvall%d" % l) for l in range(2)]
        for r in rloc:
            r.sem = fw.dma_sem()
        cc_sem = [fw.es.enter_context(nc.semaphore("cc_sem%d" % l)) for l in range(2)]

        fillRR = RR([0, 1])

        def sample_front_gen(l):
            yield from inproj_gen(GS, l, [0, 1, 2, 3], brr=fillRR)
            dve(lambda e: e.tensor_copy(out=ubnd[:, :, 0:1], in_=uT[:, 2, :, 1:2]), [ruT[4]], [rubnd])
            dve(lambda e: e.tensor_copy(out=ubnd[:, :, 1:2], in_=uT[:, 2, :, 256:257]), [ruT[5]], [rubnd])
            loc = kv_loc[l].ap()
            s = rloc[l].sem
            fw.dma("sp", loc[:, O_KT:O_KT + 1024].rearrange("p (h k) -> p h k", h=4), KT[:, :, 1024:1280],
                   reads=[rKT[8], rKT[9]], writes=[], sem=s)
            fw.dma("sp", loc[:, O_KD:O_KD + 1024].rearrange("p (h k) -> p h k", h=4), KdT[:, :, 1024:1280],
                   reads=[rKdT[8], rKdT[9]], writes=[], sem=s)
            fw.dma("sp", loc[:, O_VM:O_VM + 520], Vm[:, 8:10, :, :].rearrange("p s h d -> p (s h d)"),
                   reads=[rVm[8], rVm[9]], writes=[], sem=s)
            fw.dma("sp", loc[:, O_VD:O_VD + 1032], Vd[:, 8:10, :, :].rearrange("p s h d -> p (s h d)"),
                   reads=[rVd[8], rVd[9]], writes=[], sem=s)
            fw.dma("sp", loc[:, O_UB:O_UB + 4], ubnd[:].rearrange("p b c -> p (b c)"), reads=[rubnd], writes=[rloc[l]], sem=s)
            for r_ in (rKT[8], rKT[9], rKdT[8], rKdT[9], rVm[8], rVm[9], rVd[8], rVd[9], rubnd):
                r_.rd[s] = fw.cnt[s]
            pending_ag.append(l)

        pending_ag = []

        def trigger_ag():
            while pending_ag:
                l = pending_ag.pop(0)
                fw._waits("pool", [rloc[l]], [rall[l]])
                nc.gpsimd.collective_compute("AllGather", ALU.bypass, replica_groups=[[0, 1, 2, 3], [4, 5, 6, 7]],
                                             ins=[kv_loc[l].ap().opt()], outs=[kv_all[l].ap().opt()]).then_inc(cc_sem[l])

        def sample_ctx(l):
            for kt in range(2):
                rows = slice(kt * 128, (kt + 1) * 128)
                ci = scrRR.next()
                fw.dma("sp", scr[ci][:], dk_c[l, rows, :], writes=[rscr[ci]], sem=rscr[ci].sem)
                for _ in k_diff(scr[ci][:], [rscr[ci]], 4 + kt, None, None):
                    pass
                ci = scrRR.next()
                fw.dma("sp", scr[ci][:], dv_c[l, rows, :], writes=[rscr[ci]], sem=rscr[ci].sem)
                for _ in v_diff(scr[ci][:], [rscr[ci]], 4 + kt, None):
                    pass
                ci = scrRR.next()
                fw.dma("sp", scr[ci][:, 0:128], ckv_c[l, rows, :], writes=[rscr[ci]], sem=rscr[ci].sem)
                fw.dma("sp", scr[ci][:, 128:160], kr_c[l, rows, :], writes=[rscr[ci]], sem=rscr[ci].sem)
                for _ in k_mla(l, scr[ci][:, 0:128], scr[ci][:, 128:160], [rscr[ci]], 4 + kt, None, False, None, None):
                    pass

        def sample_back(l, extra=None):
            st_reset()
            phase_gate(l, 1)
            nc.sync.wait_ge(cc_sem[l], 1)
            al = kv_all[l].ap()
            for r in range(4):
                rw = al[r * 128:(r + 1) * 128, :]
                s0 = (0, 2, 6, 8)[r]
                fw.dma("sp", KT[:, :, s0 * 128:(s0 + 2) * 128], rw[:, O_KT:O_KT + 1024].rearrange("p (h k) -> p h k", h=4),
                       writes=[rKT[s0], rKT[s0 + 1]], sem=rKT[s0].sem)
                fw.dma("sp", KdT[:, :, s0 * 128:(s0 + 2) * 128], rw[:, O_KD:O_KD + 1024].rearrange("p (h k) -> p h k", h=4),
                       writes=[rKdT[s0], rKdT[s0 + 1]], sem=rKdT[s0].sem)
                fw.dma("sp", Vm[:, s0:s0 + 2, :, :].rearrange("p s h d -> p (s h d)"), rw[:, O_VM:O_VM + 520],
                       writes=[rVm[s0], rVm[s0 + 1]], sem=rVm[s0].sem)
                fw.dma("sp", Vd[:, s0:s0 + 2, :, :].rearrange("p s h d -> p (s h d)"), rw[:, O_VD:O_VD + 1032],
                       writes=[rVd[s0], rVd[s0 + 1]], sem=rVd[s0].sem)
                fw.dma("sp", uhal[:, r, :, :].rearrange("p b c -> p (b c)"), rw[:, O_UB:O_UB + 4], writes=[ruhal], sem=ruhal.sem)
            dve(lambda e: e.tensor_copy(out=uhf[:], in_=uhal[:]), [ruhal], [ruhf])
            for side in range(2):
                col = 1 - side
                dve(lambda e: e.tensor_scalar(out=uacc[:, :, side:side + 1], in0=uhf[:, 0, :, col:col + 1],
                                              scalar1=selh[:, side * 4:side * 4 + 1], scalar2=None, op0=ALU.mult),
                    [ruhf, rsel], [ruacc])
                for r in range(1, 4):
                    dve(lambda e, r=r: e.scalar_tensor_tensor(out=uacc[:, :, side:side + 1], in0=uhf[:, r, :, col:col + 1],
                                                              scalar=selh[:, side * 4 + r:side * 4 + r + 1],
                                                              in1=uacc[:, :, side:side + 1], op0=ALU.mult, op1=ALU.add),
                        [ruhf, rsel, ruacc], [ruacc])
            dve(lambda e: e.tensor_copy(out=uT[:, 2, :, 0:1], in_=uacc[:, :, 0:1]), [ruacc], [ruH[2]])
            dve(lambda e: e.tensor_copy(out=uT[:, 2, :, 257:258], in_=uacc[:, :, 1:2]), [ruacc], [ruH[2]])
            stage_inproj(GS, l, [4, 5, 6, 7], extra=extra, extra_delay=3)
            conv_seq(GS, l, 0)
            attention_seq(GS, l, 0, list(range(10)))
            gens = []
            for qt in range(2):
                gen = outproj_tile(GS, l, qt, qt)
                if gen is not None:
                    gens.append(gen)
            return gens

        def zero_prompt_halos():
            dve(lambda e: e.memset(uT[:, 0, :, 0:1], 0.0), [], [ruH[0]])
            dve(lambda e: e.memset(uT[:, 0, :, 257:258], 0.0), [], [ruH[0]])

        prefetch_x(GS)
        prefetch_x(PG[0])
        sample_ctx(0)
        gA = group_norm_gens(0)
        gS0 = []
        for t in range(2):
            g_ = norm_gen(GS, 0, t)
            next(g_)
            gS0.append(g_)
        finish_mod_a()
        finish_mod_b()
        for g_ in gA[0:2]:
            for _ in g_:
                pass
        run_prompt_group(0, fill_gen=sample_front_gen(0), first_extra=gA[2:4] + gS0)
        prefetch_x(PG[1], q="pool")
        gB = group_norm_gens(1)
        gS = sample_back(0, extra=gB)
        step_all(gS)
        sample_ctx(1)
        while gS:
            step_all(gS)
        run_prompt_group(1, fill_gen=sample_front_gen(1))
        sample_back(1)

        _DBG["sbuf_free"] = nc.sbuf_bytes_remaining
        fw.wait_all("sp", [k for k in fw.sems if k.startswith("dma")])
    return nc


def _rope_tables(pos, rot_dim):
    half = rot_dim // 2
    inv = (10000.0 ** (-(np.arange(0, half, 2, dtype=np.float32) / np.float32(half)))).astype(np.float32)
    row = (pos // 64).astype(np.float32)
    col = (pos % 64).astype(np.float32)
    ang_r = row[:, None] * inv
    ang_c = col[:, None] * inv
    ang = np.concatenate([ang_r, ang_r, ang_c, ang_c], axis=-1).astype(np.float32)
    cos = np.cos(ang).astype(np.float32)
    sin = np.sin(ang).astype(np.float32)
    q = half // 2
    sign = np.ones(rot_dim, np.float32)
    sign[0:q] = -1.0
    sign[half:half + q] = -1.0
    return cos, sin * sign


_COLS = None


def _perm_cols():
    offs = np.cumsum([0, 256, 128, 32, 256, 256, 256, 512, 512, 512, 1024])
    c_q, c_kv, k_r, cb, cc, cx, dq, dk, dv, z = [np.arange(offs[i], offs[i + 1]) for i in range(10)]
    return np.concatenate([dk, dv, c_kv, k_r, cc, cx, c_q, z[0:256], dq, cb, z[256:512], z[512:1024]])


_NC_CACHE = {}


def kernel(x_prompt, x_sample, c, cache_mla_ckv, cache_mla_krope, cache_diff_k, cache_diff_v,
           c_ctx, norm_g, ada_w, ada_b, w_in, mla_q_norm, w_uq, mla_kv_norm, w_ukv,
           conv_w, diff_lambda, diff_subln, w_out, final_norm):
    f = lambda a: np.ascontiguousarray(np.asarray(a, dtype=np.float32))
    x_prompt, x_sample, c, c_ctx = f(x_prompt), f(x_sample), f(c), f(c_ctx)
    cols = _perm_cols()
    shared = {
        "normgT": f(np.asarray(norm_g).reshape(2, 8, 128).transpose(0, 2, 1)),
        "final_norm": f(np.asarray(final_norm).reshape(1, D)),
        "w_in_p": f(np.asarray(w_in)[:, :, cols]),
        "qnT": f(np.asarray(mla_q_norm).reshape(2, 2, 128).transpose(0, 2, 1)),
        "w_uq": f(w_uq),
        "kv_norm": f(np.asarray(mla_kv_norm).reshape(2, 1, 128)),
        "w_ukv": f(w_ukv),
        "conv_wT": f(np.asarray(conv_w).reshape(2, 3, 2, 128).transpose(0, 3, 2, 1).reshape(2, 128, 6)),
        "diff_lambda": f(np.asarray(diff_lambda).reshape(2, 1, 256)),
        "diff_subln": f(np.asarray(diff_subln).reshape(2, 1, 128)),
        "w_out": f(w_out),
        "ident": np.eye(128, dtype=np.float32),
    }
    in_maps = []
    for i in range(NCORES):
        b, j = i // 4, i % 4
        pos = 256 * j + np.arange(256)
        cd, sd = _rope_tables(pos, 64)
        cm, sm = _rope_tables(pos, 32)
        rope_d = np.stack([cd, sd], axis=1).reshape(2, 128, 2, 64).transpose(1, 0, 2, 3)
        rope_m = np.stack([cm, sm], axis=1).reshape(2, 128, 2, 32).transpose(1, 0, 2, 3)
        sel = np.zeros((128, 8), np.float32)
        if j > 0:
            sel[:, j - 1] = 1.0
        if j < 3:
            sel[:, 4 + j + 1] = 1.0
        condT = np.stack([c_ctx.reshape(8, 128).T, c[b].reshape(8, 128).T], axis=-1)
        m = dict(shared)
        m.update({
            "xp": f(x_prompt[4 * i:4 * i + 4]),
            "xs": f(x_sample[b, 256 * j:256 * j + 256]),
            "condT": f(condT),
            "ckv_c": f(np.asarray(cache_mla_ckv)[b]),
            "kr_c": f(np.asarray(cache_mla_krope)[b]),
            "dk_c": f(np.asarray(cache_diff_k)[b].reshape(2, 256, 512)),
            "dv_c": f(np.asarray(cache_diff_v)[b].reshape(2, 256, 512)),
            "rope_d": f(rope_d), "rope_m": f(rope_m), "halo_sel": sel,
            "ada_w_s": f(np.asarray(ada_w)[:, :, 768 * j:768 * (j + 1)]),
            "ada_bT_s": f(np.asarray(ada_b)[:, 768 * j:768 * (j + 1)].reshape(2, 6, 128).transpose(0, 2, 1)),
        })
        in_maps.append(m)
    if "nc" not in _NC_CACHE:
        _NC_CACHE["nc"] = build_nc()
    res = run_bass_kernel_spmd(_NC_CACHE["nc"], in_maps, core_ids=list(range(NCORES)))
    R = res.results
    y_prompt = np.concatenate([R[i]["y_p"] for i in range(NCORES)], axis=0)
    y_sample = np.stack([np.concatenate([R[4 * b + j]["y_s"] for j in range(4)], axis=0) for b in range(2)], axis=0)
    s_ckv = np.concatenate([R[i]["o_ckv"] for i in range(NCORES)], axis=0)
    s_kr = np.concatenate([R[i]["o_kr"] for i in range(NCORES)], axis=0)
    s_dk = np.concatenate([R[i]["o_dk"] for i in range(NCORES)], axis=0).reshape(32, 2, 256, 4, 2, 64)
    s_dv = np.concatenate([R[i]["o_dv"] for i in range(NCORES)], axis=0).reshape(32, 2, 256, 4, 128)
    out = (y_prompt, y_sample, s_ckv, s_kr, s_dk, s_dv)
    return tuple(np.ascontiguousarray(o, dtype=np.float32) for o in out)
```

```python
import contextlib
import math
import numpy as np
import concourse.bass as bass
import concourse.mybir as mybir
from concourse.bass_utils import run_bass_kernel_spmd

F32 = mybir.dt.float32
BF16 = mybir.dt.bfloat16
AF = mybir.ActivationFunctionType
ALU = mybir.AluOpType
AX = mybir.AxisListType

EPS = 1e-6
NCORES = 8
DEPTH = 2
D = 1024
CH = [(0, 512), (512, 512), (1024, 160), (1184, 512), (1696, 512), (2208, 512), (2720, 512), (3232, 512)]
CPAD = 3604
O_KT, O_KD, O_VM, O_VD, O_UB = 0, 1024, 2048, 2568, 3600
_DBG = {}
MLA_SCALE = 96 ** -0.5
DIFF_SCALE = 64 ** -0.5


class Res:
    __slots__ = ("name", "lw", "rd", "excl", "sem")

    def __init__(self, name, excl=False):
        self.name = name
        self.lw = None
        self.rd = {}
        self.excl = excl
        self.sem = None


class FW:
    def __init__(self, nc):
        self.nc = nc
        self.es = contextlib.ExitStack()
        self.engs = {"pe": nc.tensor, "act": nc.scalar, "dve": nc.vector, "pool": nc.gpsimd, "sp": nc.sync}
        self.sems = {}
        self.cnt = {}
        self.known = {k: {} for k in self.engs}
        for k in ("pe", "act", "dve", "pool"):
            self.sems[k] = self.es.enter_context(nc.semaphore("sem_" + k))
            self.cnt[k] = 0
        self.ndma = 0

    def sbuf(self, name, shape, dt):
        return self.es.enter_context(self.nc.sbuf_tensor(name, list(shape), dt))

    def psum(self, name, shape, dt):
        return self.es.enter_context(self.nc.psum_tensor(name, list(shape), dt))

    def dma_sem(self):
        self.ndma += 1
        key = "dma%d" % self.ndma
        self.sems[key] = self.es.enter_context(self.nc.semaphore("sem_" + key))
        self.cnt[key] = 0
        return key

    def _waits(self, e, reads, writes, same_ok=False):
        deps = {}

        def need(kv):
            if kv is not None and deps.get(kv[0], 0) < kv[1]:
                deps[kv[0]] = kv[1]
        for r in reads:
            need(r.lw)
            if r.excl:
                for kv in r.rd.items():
                    need(kv)
        for w in writes:
            need(w.lw)
            for kv in w.rd.items():
                need(kv)
        eng = self.engs[e]
        for key, val in deps.items():
            if key == e and same_ok:
                continue
            if self.known[e].get(key, 0) >= val:
                continue
            eng.wait_ge(self.sems[key], val)
            self.known[e][key] = val

    def _record(self, reads, writes, key, ticket):
        for r in reads:
            if r.excl:
                r.lw = (key, ticket)
                r.rd = {}
            elif r.rd.get(key, 0) < ticket:
                r.rd[key] = ticket
        for w in writes:
            w.lw = (key, ticket)
            w.rd = {}

    def op(self, e, fn, reads=(), writes=(), signal=True):
        self._waits(e, reads, writes, same_ok=(e == "pe"))
        ins = fn(self.engs[e])
        if signal:
            self.cnt[e] += 1
            ins.then_inc(self.sems[e], 1)
            self._record(reads, writes, e, self.cnt[e])
        else:
            self._record(reads, writes, e, self.cnt[e] + 1)
        return ins

    def dma(self, q, out, in_, reads=(), writes=(), sem=None):
        self._waits(q, reads, writes)
        ins = self.engs[q].dma_start(out=out, in_=in_)
        self.cnt[sem] += 16
        ins.then_inc(self.sems[sem], 16)
        self._record(reads, writes, sem, self.cnt[sem])
        return ins

    def wait_all(self, e, keys):
        for k in keys:
            if self.cnt[k] > 0 and self.known[e].get(k, 0) < self.cnt[k]:
                self.engs[e].wait_ge(self.sems[k], self.cnt[k])
                self.known[e][k] = self.cnt[k]


class RR:
    def __init__(self, items):
        self.items = items
        self.i = 0

    def next(self):
        x = self.items[self.i % len(self.items)]
        self.i += 1
        return x


def build_nc():
    nc = bass.Bass("TRN2", target_bir_lowering=False)

    def din(name, shape, dt=F32):
        return nc.dram_tensor(name, list(shape), dt, kind="ExternalInput").ap()

    def dout(name, shape):
        return nc.dram_tensor(name, list(shape), F32, kind="ExternalOutput").ap()

    xp = din("xp", [4, 256, D])
    xs = din("xs", [256, D])
    condT = din("condT", [128, 8, 2])
    ckv_c = din("ckv_c", [2, 256, 128])
    kr_c = din("kr_c", [2, 256, 32])
    dk_c = din("dk_c", [2, 256, 512])
    dv_c = din("dv_c", [2, 256, 512])
    gT_d = din("normgT", [2, 128, 8])
    fn_d = din("final_norm", [1, D])
    ada_w = din("ada_w_s", [2, D, 768])
    ada_bT = din("ada_bT_s", [2, 128, 6])
    w_in = din("w_in_p", [2, D, 3744])
    qnT = din("qnT", [2, 128, 2])
    w_uq = din("w_uq", [2, 256, 384])
    kvn = din("kv_norm", [2, 1, 128])
    w_ukv = din("w_ukv", [2, 128, 512])
    cwT = din("conv_wT", [2, 128, 6])
    lam_d = din("diff_lambda", [2, 1, 256])
    sub_d = din("diff_subln", [2, 1, 128])
    w_out = din("w_out", [2, D, D])
    ident_d = din("ident", [128, 128])
    ropd_d = din("rope_d", [128, 2, 2, 64])
    ropm_d = din("rope_m", [128, 2, 2, 32])
    sel_d = din("halo_sel", [128, 8])

    y_p = dout("y_p", [4, 256, D])
    y_s = dout("y_s", [256, D])
    o_ckv = dout("o_ckv", [4, 2, 256, 128])
    o_kr = dout("o_kr", [4, 2, 256, 32])
    o_dk = dout("o_dk", [4, 2, 256, 512])
    o_dv = dout("o_dv", [4, 2, 256, 512])

    mod_loc = nc.dram_tensor("mod_loc", [128, 24], F32)
    mod_all = nc.dram_tensor("mod_all", [512, 24], F32)
    kv_loc = [nc.dram_tensor("kv_loc%d" % l, [128, CPAD], BF16) for l in range(2)]
    kv_all = [nc.dram_tensor("kv_all%d" % l, [512, CPAD], BF16) for l in range(2)]

    fw = FW(nc)
    with fw.es:
        S = fw.sbuf
        X = [S("X%d" % i, (128, D), F32) for i in range(6)]
        rX = [Res("X%d" % i) for i in range(6)]
        for r in rX:
            r.sem = fw.dma_sem()
        hT = S("hT", (128, 6, 8, 128), BF16)
        rhT = [[Res("hT%d_%d" % (i, k)) for k in range(8)] for i in range(6)]
        Wc = [S("Wc%d" % i, (128, 8, 512), BF16) for i in range(2)]
        rWc = [Res("Wc%d" % i) for i in range(2)]
        for r in rWc:
            r.sem = fw.dma_sem()
        Wo = [S("Wo%d" % l, (128, 8, D), BF16) for l in range(2)]
        rWo = [Res("Wo%d" % l) for l in range(2)]
        for r in rWo:
            r.sem = fw.dma_sem()
        Abc = S("Abc", (128, D), F32); rAbc = Res("Abc")
        gateB = [S("gateB%d" % i, (128, D), F32) for i in range(2)]; rgate = [Res("gateB%d" % i) for i in range(2)]
        ATt = S("ATt", (128, 2, 2, 8), F32); rAT = Res("ATt")
        rAbc.sem = fw.dma_sem()
        ph = [S("ph%d" % i, (128, 256), F32) for i in range(4)]
        rph = [Res("ph%d" % i) for i in range(4)]
        phRR = RR([0, 1, 2, 3])
        hb = [S("hb%d" % i, (128, D), BF16) for i in range(1)]
        rhb = [Res("hb%d" % i) for i in range(1)]
        hbRR = RR([0])
        junk = S("junk", (128, 256), BF16); rjunk = Res("junk")
        NS = 10
        KT = S("KT", (128, 4, NS * 128), BF16); rKT = [Res("KT%d" % i) for i in range(NS)]
        KdT = S("KdT", (128, 4, NS * 128), BF16); rKdT = [Res("KdT%d" % i) for i in range(NS)]
        Vm = S("Vm", (128, NS, 4, 65), BF16); rVm = [Res("Vm%d" % i) for i in range(NS)]
        Vd = S("Vd", (128, NS, 4, 129), BF16); rVd = [Res("Vd%d" % i) for i in range(NS)]
        for lst in (rKT, rKdT, rVm, rVd):
            for r in lst:
                r.sem = fw.dma_sem()
        QT = S("QT", (128, 4, 512), BF16); rQT = [Res("QT%d" % i) for i in range(4)]
        QdT = S("QdT", (128, 4, 512), BF16); rQdT = [Res("QdT%d" % i) for i in range(4)]
        E = [S("E%d" % i, (128, NS, 256), BF16) for i in range(2)]
        rE = [Res("E%d" % i) for i in range(2)]
        ERR = RR([0, 1])
        E.append(S("E3", (128, 2, 256), BF16)); rE.append(Res("E3"))
        ERR3 = RR([0, 1, 2])
        sRR3 = RR([4, 5, 3])
        uT = S("uT", (128, 3, 2, 258), BF16)
        ruT = [Res("uT%d" % i) for i in range(6)]
        ruH = [Res("uH%d" % i) for i in range(3)]
        gT = S("gT", (128, 2, 512), BF16)
        rgT = [Res("gT%d" % i) for i in range(4)]
        SZ = S("SZ", (128, 4, 768), BF16); rSZ = [Res("SZ%d" % i) for i in range(4)]
        mix = [S("mix%d" % i, (128, 768), BF16) for i in range(4)]
        rmix = [[Res("mix%d_%d" % (i, j)) for j in range(5)] for i in range(4)]
        mixT = [S("mixT%d" % i, (128, 6, 128), BF16) for i in range(2)]
        rmixT = [Res("mixT%d" % i) for i in range(2)]
        scr = [S("scr%d" % i, (128, 512), F32) for i in range(4)]
        rscr = [Res("scr%d" % i) for i in range(4)]
        for r in rscr:
            r.sem = fw.dma_sem()
        scrRR = RR([0, 1, 2, 3])
        stgc = [S("stgc%d" % i, (128, 160), F32) for i in range(2)]
        rstgc = [Res("stgc%d" % i) for i in range(2)]
        for r in rstgc:
            r.sem = fw.dma_sem()
        stgcRR = RR([0, 1])
        b512 = [S("b512_%d" % i, (128, 512), BF16) for i in range(2)]
        rb512 = [Res("b512_%d" % i) for i in range(2)]
        b512RR = RR([0, 1])
        b256 = [S("b256_%d" % i, (128, 256), BF16) for i in range(2)]
        rb256 = [Res("b256_%d" % i) for i in range(2)]
        b256RR = RR([0, 1])
        ckvb = [S("ckvb%d" % i, (128, 128), BF16) for i in range(2)]
        rckvb = [Res("ckvb%d" % i) for i in range(2)]
        ckvT = [S("ckvT%d" % i, (128, 128), BF16) for i in range(2)]
        rckvT = [Res("ckvT%d" % i) for i in range(2)]
        ckvRR = RR([0, 1])
        krb = [S("krb%d" % i, (128, 32), BF16) for i in range(2)]
        rkrb = [Res("krb%d" % i) for i in range(2)]
        cqT = [S("cqT%d" % i, (128, 2, 128), BF16) for i in range(2)]
        rcqT = [Res("cqT%d" % i) for i in range(2)]
        cqRR = RR([0, 1])
        Kf = [S("Kf%d" % i, (128, 4, 96), BF16) for i in range(2)]
        rKf = [Res("Kf%d" % i) for i in range(2)]
        KfRR = RR([0, 1])
        Qf = [S("Qf%d" % i, (128, 4, 96), BF16) for i in range(2)]
        rQf = [Res("Qf%d" % i) for i in range(2)]
        QfRR = RR([0, 1])
        ST = S("ST", (128, 512), F32)
        identf = S("identf", (128, 128), F32); ridf = Res("identf"); ridf.sem = fw.dma_sem()
        identb = S("identb", (128, 128), BF16); ridb = Res("identb")
        onesf = S("onesf", (128, 128), F32); rones = Res("onesf")
        cst = S("cst", (128, 4), F32); rcst = Res("cst")
        ropd = S("ropd", (128, 2, 2, 64), F32); rropd = Res("ropd"); rropd.sem = fw.dma_sem()
        ropm = S("ropm", (128, 2, 2, 32), F32); rropm = Res("ropm"); rropm.sem = fw.dma_sem()
        selh = S("selh", (128, 8), F32); rsel = Res("selh"); rsel.sem = fw.dma_sem()
        cT = S("cT", (128, 8, 2), F32); rcT = Res("cT"); rcT.sem = fw.dma_sem()
        scT = S("scT", (128, 8, 2), BF16); rscT = Res("scT")
        ctmp = S("ctmp", (128, 8, 2), F32); rctmp = Res("ctmp")
        modT = [S("modT%d" % l, (128, 24, 2), F32) for l in range(2)]; rmodT = [Res("modT%d" % l) for l in range(2)]
        abT = [S("abT%d" % l, (128, 6), F32) for l in range(2)]; rabT = [Res("abT%d" % l) for l in range(2)]
        gTn = [S("gTn%d" % l, (128, 8), F32) for l in range(2)]; rgTn = [Res("gTn%d" % l) for l in range(2)]
        qn = [S("qn%d" % l, (128, 2), F32) for l in range(2)]; rqn = [Res("qn%d" % l) for l in range(2)]
        wuqf = gateB[0][:, 0:768].rearrange("p (k n) -> p k n", k=2); rwuqf = rgate[0]; rgate[0].sem = fw.dma_sem()
        Wuq = [S("Wuq%d" % l, (128, 2, 384), BF16) for l in range(2)]; rWuq = [Res("Wuq%d" % l) for l in range(2)]
        Wukv = [S("Wukv%d" % l, (128, 512), BF16) for l in range(2)]; rWukv = [Res("Wukv%d" % l) for l in range(2)]
        kvnb = [S("kvnb%d" % l, (128, 128), F32) for l in range(2)]; rkvnb = [Res("kvnb%d" % l) for l in range(2)]
        subb = [S("subb%d" % l, (128, 128), F32) for l in range(2)]; rsubb = [Res("subb%d" % l) for l in range(2)]
        cw = [S("cw%d" % l, (128, 6), F32) for l in range(2)]; rcw = [Res("cw%d" % l) for l in range(2)]
        lamt = [S("lamt%d" % l, (128, 8), F32) for l in range(2)]; rlamt = [Res("lamt%d" % l) for l in range(2)]
        for lst in (rabT, rgTn, rqn, rWukv, rkvnb, rsubb, rcw):
            for r in lst:
                r.sem = fw.dma_sem()
        vT8 = S("vT8", (128, 8), F32); rvT8 = Res("vT8")
        modp = S("modp", (128, 2, 6, 2), F32); rmodp = Res("modp")
        moda = S("moda", (128, 4, 24), F32); rmoda = Res("moda"); rmoda.sem = fw.dma_sem()
        rmodloc = Res("modloc"); rmodloc.sem = fw.dma_sem()
        cc_mod = fw.es.enter_context(nc.semaphore("cc_mod"))
        uhal = S("uhal", (128, 4, 2, 2), BF16); ruhal = Res("uhal"); ruhal.sem = fw.dma_sem()
        uhf = S("uhf", (128, 4, 2, 2), F32); ruhf = Res("uhf")
        uacc = S("uacc", (128, 2, 2), F32); ruacc = Res("uacc")
        ubnd = S("ubnd", (128, 2, 2), BF16); rubnd = Res("ubnd")

        PS = [fw.psum("ps%d" % i, (128, 512), F32) for i in range(8)]
        rPS = [Res("ps%d" % i, excl=True) for i in range(8)]
        inRR = RR([0, 1])
        ipRR = RR([0, 1, 6, 7])
        trRR = RR([2, 3])
        sRR = RR([4, 5])

        def pst(b):
            return PS[b][:].bitcast(BF16).rearrange("p (a c) -> p a c", c=128)

        NBLK = 128
        rST = [Res("st%d" % i) for i in range(NBLK)]
        st_state = {"n": 0}

        def st_reset():
            pass

        def st_alloc(w=1):
            assert w <= 4
            i = st_state["n"] % NBLK
            st_state["n"] += 1
            return ST[:, 4 * i:4 * i + w], rST[i]

        def act(fn, reads, writes):
            return fw.op("act", fn, reads, writes)

        def dve(fn, reads, writes):
            return fw.op("dve", fn, reads, writes)

        def pe(fn, reads, writes, signal=True):
            return fw.op("pe", fn, reads, writes, signal)

        def rstd_from_ss(ss, rss, n, w=1):
            la, rla = st_alloc(w)
            rs, rrs = st_alloc(w)
            act(lambda e: e.activation(out=la, in_=ss, func=AF.Ln, scale=1.0 / n, bias=cst[:, 0:1]), [rss, rcst], [rla])
            act(lambda e: e.activation(out=rs, in_=la, func=AF.Exp, scale=-0.5), [rla], [rrs])
            return rs, rrs

        def sigmoid_from(ps_ap, rps, out_ap, rout, w):
            act(lambda e: e.activation(out=out_ap, in_=ps_ap, func=AF.Exp, scale=-1.0), [rps], [rout])
            act(lambda e: e.activation(out=out_ap, in_=out_ap, func=AF.Ln, bias=cst[:, 1:2]), [rout, rcst], [rout])
            act(lambda e: e.activation(out=out_ap, in_=out_ap, func=AF.Exp, scale=-1.0), [rout], [rout])

        def transposes(srcs, rsrc, nparts_out=128):
            b = trRR.next()
            v = pst(b)
            n = len(srcs)
            for i, s in enumerate(srcs):
                F = s.shape[-1]
                pe(lambda e, i=i, s=s, F=F: e.transpose(out=v[0:F, i, :], in_=s, identity=identb[:]),
                   list(rsrc) + [ridb], [rPS[b]], signal=(i == n - 1))
            return b, v

        def load_wo(l):
            fw.dma("pool", Wo[l][:], w_out[l].rearrange("(k p) n -> p k n", p=128), writes=[rWo[l]], sem=rWo[l].sem)

        fw.dma("sp", identf[:], ident_d, writes=[ridf], sem=ridf.sem)
        dve(lambda e: e.tensor_copy(out=identb[:], in_=identf[:]), [ridf], [ridb])
        dve(lambda e: e.memset(onesf[:], 1.0), [], [rones])
        dve(lambda e: e.memset(cst[:, 0:1], EPS), [], [rcst])
        dve(lambda e: e.memset(cst[:, 1:2], 1.0), [], [rcst])
        fw.dma("sp", ropd[:], ropd_d, writes=[rropd], sem=rropd.sem)
        fw.dma("sp", ropm[:], ropm_d, writes=[rropm], sem=rropm.sem)
        fw.dma("sp", selh[:], sel_d, writes=[rsel], sem=rsel.sem)
        fw.dma("sp", cT[:], condT, writes=[rcT], sem=rcT.sem)
        fw.op("pool", lambda e: e.memset(Vm[:], 1.0), [], rVm)
        fw.op("pool", lambda e: e.memset(Vd[:], 1.0), [], rVd)
        fw.op("pool", lambda e: e.memset(KT[:], 0.0), [], rKT)
        fw.op("pool", lambda e: e.memset(KdT[:], 0.0), [], rKdT)
        fw.op("pool", lambda e: e.memset(QT[:], 0.0), [], rQT)
        fw.op("pool", lambda e: e.memset(uT[:], 0.0), [], ruT + ruH)
        dve(lambda e: e.memset(ubnd[:], 0.0), [], [rubnd])
        st_reset()
        sigmoid_from(cT[:], rcT, ctmp[:], rctmp, 16)
        dve(lambda e: e.tensor_tensor(out=scT[:], in0=cT[:], in1=ctmp[:], op=ALU.mult), [rcT, rctmp], [rscT])

        for l in range(2):
            fw.dma("sp", abT[l][:], ada_bT[l], writes=[rabT[l]], sem=rabT[l].sem)
            fw.dma("sp", gTn[l][:], gT_d[l], writes=[rgTn[l]], sem=rgTn[l].sem)
            fw.dma("sp", qn[l][:], qnT[l], writes=[rqn[l]], sem=rqn[l].sem)
            fw.dma("sp", cw[l][:], cwT[l], writes=[rcw[l]], sem=rcw[l].sem)
            fw.dma("sp", kvnb[l][:], kvn[l].partition_broadcast(128).rearrange("p o n -> p (o n)"),
                   writes=[rkvnb[l]], sem=rkvnb[l].sem)
            fw.dma("sp", subb[l][:], sub_d[l].partition_broadcast(128).rearrange("p o n -> p (o n)"),
                   writes=[rsubb[l]], sem=rsubb[l].sem)
            li = 0.8 - 0.6 * math.exp(-0.3 * l)
            dve(lambda e, l=l, li=li: e.tensor_scalar(out=subb[l][:], in0=subb[l][:], scalar1=1.0 - li, scalar2=None,
                                                     op0=ALU.mult), [rsubb[l]], [rsubb[l]])
            fw.dma("pool", Wukv[l][:], w_ukv[l], writes=[rWukv[l]], sem=rWukv[l].sem)
            fw.dma("sp", wuqf, w_uq[l].rearrange("(k p) n -> p k n", p=128), writes=[rwuqf], sem=rwuqf.sem)
            for kc in range(2):
                dve(lambda e, l=l, kc=kc: e.tensor_scalar(out=Wuq[l][:, kc, :], in0=wuqf[:, kc, :],
                                                         scalar1=qn[l][:, kc:kc + 1], scalar2=None, op0=ALU.mult),
                    [rwuqf, rqn[l]], [rWuq[l]])
            fw.dma("sp", scr[1][:, 0:256], lam_d[l].partition_broadcast(128).rearrange("p o n -> p (o n)"),
                   writes=[rscr[1]], sem=rscr[1].sem)
            rlamb = rscr[1]
            lv = scr[1][:, 0:256].rearrange("p (a b d) -> p a b d", a=2, b=2)
            sc0 = scr[0]
            dve(lambda e: e.tensor_tensor(out=sc0[:, 0:128].rearrange("p (a d) -> p a d", a=2), in0=lv[:, :, 0, :],
                                          in1=lv[:, :, 1, :], op=ALU.mult), [rlamb], [rscr[0]])
            dve(lambda e, l=l: e.reduce_sum(out=lamt[l][:, 0:2], in_=sc0[:, 0:128].rearrange("p (a d) -> p a d", a=2),
                                            axis=AX.X), [rscr[0]], [rlamt[l]])
            act(lambda e, l=l: e.activation(out=lamt[l][:, 0:2], in_=lamt[l][:, 0:2], func=AF.Exp), [rlamt[l]], [rlamt[l]])
            dve(lambda e, l=l: e.tensor_tensor(out=lamt[l][:, 2:3], in0=lamt[l][:, 1:2], in1=lamt[l][:, 0:1],
                                               op=ALU.subtract), [rlamt[l]], [rlamt[l]])
            dve(lambda e, l=l, li=li: e.tensor_scalar(out=lamt[l][:, 3:4], in0=lamt[l][:, 2:3], scalar1=-li,
                                                     scalar2=None, op0=ALU.add), [rlamt[l]], [rlamt[l]])
            if l == 0:
                bmod = inRR.next()
            mps = PS[bmod][:, l * 12:(l + 1) * 12].rearrange("p (j c) -> p j c", c=2)
            for ch in range(2):
                wb = ch % 2
                fw.dma("pool", Wc[wb][:, :, 0:384], ada_w[l][:, ch * 384:(ch + 1) * 384].rearrange("(k p) n -> p k n", p=128),
                       writes=[rWc[wb]], sem=rWc[wb].sem)
                for jj in range(3):
                    j = ch * 3 + jj
                    for k in range(8):
                        pe(lambda e, k=k, jj=jj, j=j, wb=wb, mps=mps: e.matmul(
                            mps[:, j, :], lhsT=Wc[wb][:, k, jj * 128:(jj + 1) * 128], rhs=scT[:, k, :],
                            start=(k == 0), stop=(k == 7)), [rWc[wb], rscT], [rPS[bmod]], signal=(k == 7))
            dve(lambda e, l=l, mps=mps: e.tensor_tensor(out=modp[:, l, :, :], in0=mps,
                                                        in1=abT[l][:].unsqueeze(2).to_broadcast([128, 6, 2]), op=ALU.add),
                [rPS[bmod], rabT[l]], [rmodp])
        fw.dma("sp", mod_loc.ap(), modp[:].rearrange("p l j c -> p (l j c)"), reads=[rmodp], writes=[rmodloc], sem=rmodloc.sem)
        fw._waits("pool", [rmodloc], [])
        nc.gpsimd.collective_compute("AllGather", ALU.bypass, replica_groups=[[0, 1, 2, 3], [4, 5, 6, 7]],
                                     ins=[mod_loc.ap().opt()], outs=[mod_all.ap().opt()]).then_inc(cc_mod)
        load_wo(0)
        load_wo(1)
        def finish_mod_a():
            nc.sync.wait_ge(cc_mod, 1)
            fw.dma("sp", moda[:], mod_all.ap().rearrange("(r p) c -> p r c", p=128), writes=[rmoda], sem=rmoda.sem)
            for l in range(2):
                dve(lambda e, l=l: e.tensor_copy(out=modT[l][:].rearrange("p (r j) c -> p r j c", r=4),
                                                 in_=moda[:, :, l * 12:(l + 1) * 12].rearrange("p r (j c) -> p r j c", c=2)),
                    [rmoda], [rmodT[l]])


        def bcast_rows(vT_ap, rv, out_t, rout):
            for half in range(2):
                di = scrRR.next()
                dve(lambda e, di=di, half=half: e.tensor_tensor(
                    out=scr[di][:].rearrange("p (a c) -> p a c", a=4), in0=identf[:].unsqueeze(1).to_broadcast([128, 4, 128]),
                    in1=vT_ap[:, half * 4:half * 4 + 4].unsqueeze(2).to_broadcast([128, 4, 128]), op=ALU.mult),
                    [ridf, rv], [rscr[di]])
                b = inRR.next()
                pe(lambda e, di=di, b=b: e.matmul(PS[b][:], lhsT=onesf[:], rhs=scr[di][:],
                                                  start=True, stop=True), [rones, rscr[di]], [rPS[b]])
                act(lambda e, b=b, half=half: e.copy(out=out_t[:, half * 512:(half + 1) * 512], in_=PS[b][:]),
                    [rPS[b]], [rout])

        def finish_mod_b():
            fw.dma("sp", Abc[:], fn_d.partition_broadcast(128).rearrange("p o n -> p (o n)"), writes=[rAbc], sem=rAbc.sem)
            for l in range(2):
                for c in range(2):
                    dve(lambda e, l=l, c=c: e.scalar_tensor_tensor(out=ATt[:, l, c, :], in0=modT[l][:, 8:16, c], scalar=1.0,
                                                                   in1=gTn[l][:], op0=ALU.add, op1=ALU.mult),
                        [rmodT[l], rgTn[l]], [rAT])


        def phase_gate(l, c):
            bcast_rows(modT[l][:, 16:24, c], rmodT[l], gateB[c], rgate[c])

        wstate = {"i": 0}

        PASSES = [(0, list(range(8))), (0, [0, 1, 2, 3]), (1, list(range(8))), (0, [4, 5, 6, 7]),
                  (0, list(range(8))), (1, [0, 1, 2, 3]), (1, list(range(8))), (1, [4, 5, 6, 7])]
        pstate = {"i": 0, "pre": {}}

        def load_chunk(l, ci):
            wb = wstate["i"] % 2
            wstate["i"] += 1
            c0, cwid = CH[ci]
            fw.dma("pool", Wc[wb][:, :, 0:cwid], w_in[l][:, c0:c0 + cwid].rearrange("(k p) n -> p k n", p=128),
                   writes=[rWc[wb]], sem=rWc[wb].sem)
            return wb

        def inproj(wb, ci, ht, brr=None):
            cwid = CH[ci][1]
            b = (brr or ipRR).next()
            for k in range(8):
                pe(lambda e, k=k: e.matmul(PS[b][:, 0:cwid], lhsT=hT[:, ht, k, :], rhs=Wc[wb][:, k, 0:cwid],
                                           start=(k == 0), stop=(k == 7)), [rhT[ht][k], rWc[wb]], [rPS[b]], signal=(k == 7))
            return b

        def rope_apply(src, rsrc, G, dim, tab, rtab, tl, out_ap, rout, pre_scale=None):
            hq = dim // 4
            rd = list(rsrc) + [rtab]
            if pre_scale is not None:
                s0 = scrRR.next()
                sc_ = scr[s0][:, 0:G * dim].rearrange("p (g d) -> p g d", g=G)
                dve(lambda e: e.tensor_scalar(out=sc_, in0=src, scalar1=pre_scale[0], scalar2=None, op0=ALU.mult),
                    list(rsrc) + [pre_scale[1]], [rscr[s0]])
                src = sc_
                rd = [rscr[s0], rtab]
            s1 = scrRR.next(); s2 = scrRR.next()
            t1 = scr[s1][:, 0:G * dim].rearrange("p (g d) -> p g d", g=G)
            t2 = scr[s2][:, 0:G * dim].rearrange("p (g d) -> p g d", g=G)
            cosb = tab[:, tl, 0, :].unsqueeze(1).to_broadcast([128, G, dim])
            dve(lambda e: e.tensor_tensor(out=t1, in0=src, in1=cosb, op=ALU.mult), rd, [rscr[s1]])
            sv = src.rearrange("p g (a two q) -> p g a two q", two=2, q=hq)
            tv = t2.rearrange("p g (a two q) -> p g a two q", two=2, q=hq)
            sn = tab[:, tl, 1, :].rearrange("p (a two q) -> p a two q", two=2, q=hq)
            for two in range(2):
                o_ = tv[:, :, :, two, :]
                i_ = sv[:, :, :, 1 - two, :]
                s_ = sn[:, :, two, :].unsqueeze(1).to_broadcast([128, G, 2, hq])
                dve(lambda e, o_=o_, i_=i_, s_=s_: e.tensor_tensor(out=o_, in0=i_, in1=s_, op=ALU.mult), rd, [rscr[s2]])
            dve(lambda e: e.tensor_tensor(out=out_ap, in0=t1, in1=t2, op=ALU.add), [rscr[s1], rscr[s2]], [rout])

        def k_diff(src, rsrc, slot, rope_tl, out_dst):
            if out_dst is not None:
                si = scrRR.next()
                act(lambda e: e.copy(out=scr[si][:], in_=src), rsrc, [rscr[si]])
                fw.dma("sp", out_dst, scr[si][:], reads=[rscr[si]], sem=rscr[si].sem)
            bi = b512RR.next()
            if rope_tl is None:
                dve(lambda e: e.tensor_copy(out=b512[bi][:], in_=src), rsrc, [rb512[bi]])
            else:
                rope_apply(src.rearrange("p (g d) -> p g d", g=8), rsrc, 8, 64, ropd, rropd, rope_tl,
                           b512[bi][:].rearrange("p (g d) -> p g d", g=8), rb512[bi])
            yield
            b, v = transposes([b512[bi][:, h * 128:(h + 1) * 128] for h in range(4)], [rb512[bi]])
            act(lambda e: e.copy(out=KdT[:, :, slot * 128:(slot + 1) * 128], in_=v[:, 0:4, :]), [rPS[b]], [rKdT[slot]])

        def v_diff(src, rsrc, slot, out_dst):
            if out_dst is not None:
                si = scrRR.next()
                act(lambda e: e.copy(out=scr[si][:], in_=src), rsrc, [rscr[si]])
                fw.dma("sp", out_dst, scr[si][:], reads=[rscr[si]], sem=rscr[si].sem)
            dve(lambda e: e.tensor_copy(out=Vd[:, slot, :, 0:128], in_=src.rearrange("p (h d) -> p h d", h=4)),
                rsrc, [rVd[slot]])
            return
            yield

        def k_mla(l, ckv_src, kr_src, rsrc, slot, rope_tl, normalize, out_ckv, out_kr):
            ci = ckvRR.next()
            if normalize:
                ss, rss = st_alloc()
                act(lambda e: e.activation(out=junk[:, 0:128], in_=ckv_src, func=AF.Square, accum_out=ss),
                    list(rsrc) + [rss], [rjunk, rss])
                rs, rrs = rstd_from_ss(ss, rss, 128)
                gi = stgcRR.next()
                dve(lambda e: e.scalar_tensor_tensor(out=stgc[gi][:, 0:128], in0=ckv_src, scalar=rs, in1=kvnb[l][:],
                                                     op0=ALU.mult, op1=ALU.mult), list(rsrc) + [rrs, rkvnb[l]], [rstgc[gi]])
                if out_ckv is not None:
                    act(lambda e: e.copy(out=stgc[gi][:, 128:160], in_=kr_src), rsrc, [rstgc[gi]])
                    fw.dma("sp", out_ckv, stgc[gi][:, 0:128], reads=[rstgc[gi]], sem=rstgc[gi].sem)
                    fw.dma("sp", out_kr, stgc[gi][:, 128:160], reads=[rstgc[gi]], sem=rstgc[gi].sem)
                dve(lambda e: e.tensor_copy(out=ckvb[ci][:], in_=stgc[gi][:, 0:128]), [rstgc[gi]], [rckvb[ci]])
            else:
                dve(lambda e: e.tensor_copy(out=ckvb[ci][:], in_=ckv_src), rsrc, [rckvb[ci]])
            if rope_tl is None:
                dve(lambda e: e.tensor_copy(out=krb[ci][:], in_=kr_src), rsrc, [rkrb[ci]])
            else:
                rope_apply(kr_src.rearrange("p (g d) -> p g d", g=1), rsrc, 1, 32, ropm, rropm, rope_tl,
                           krb[ci][:].rearrange("p (g d) -> p g d", g=1), rkrb[ci])
            yield
            b, v = transposes([ckvb[ci][:]], [rckvb[ci]])
            act(lambda e: e.copy(out=ckvT[ci][:], in_=v[:, 0, :]), [rPS[b]], [rckvT[ci]])
            yield
            b2 = 4
            pe(lambda e: e.matmul(PS[b2][:], lhsT=ckvT[ci][:], rhs=Wukv[l][:], start=True, stop=True),
               [rckvT[ci], rWukv[l]], [rPS[b2]])
            pv = PS[b2][:].rearrange("p (h d) -> p h d", h=4)
            ki = KfRR.next()
            dve(lambda e: e.tensor_copy(out=Kf[ki][:, :, 0:64], in_=pv[:, :, 0:64]), [rPS[b2]], [rKf[ki]])
            act(lambda e: e.copy(out=Vm[:, slot, :, 0:64], in_=pv[:, :, 64:128]), [rPS[b2]], [rVm[slot]])
            dve(lambda e: e.tensor_copy(out=Kf[ki][:, :, 64:96], in_=krb[ci][:].unsqueeze(1).to_broadcast([128, 4, 32])),
                [rkrb[ci]], [rKf[ki]])
            yield
            b3, v3 = transposes([Kf[ki][:, h, :] for h in range(4)], [rKf[ki]])
            act(lambda e: e.copy(out=KT[0:96, :, slot * 128:(slot + 1) * 128], in_=v3[0:96, 0:4, :]), [rPS[b3]], [rKT[slot]])

        def stage_norm(G, l, t):
            for _ in norm_gen(G, l, t):
                pass

        def norm_gen(G, l, t):
            xi = G["X"][t]
            c = G["c"]
            if l == 0 and not G.get("xpre"):
                fw.dma("sp", X[xi][:], G["xsrc"][t], writes=[rX[xi]], sem=rX[xi].sem)
            ss, rss = st_alloc()
            act(lambda e: e.activation(out=E[0][:, 0:4, :].rearrange("p a c -> p (a c)"), in_=X[xi][:], func=AF.Square,
                                       accum_out=ss), [rX[xi], rss], [rE[0], rss])
            rs, rrs = rstd_from_ss(ss, rss, D)
            yield
            hi = hbRR.next()
            dve(lambda e: e.tensor_scalar(out=hb[hi][:], in0=X[xi][:], scalar1=rs, scalar2=None, op0=ALU.mult),
                [rX[xi], rrs], [rhb[hi]])
            b, v = transposes([hb[hi][:, k * 128:(k + 1) * 128] for k in range(8)], [rhb[hi]])
            ht = G["ht"][t]
            for k in range(8):
                if k % 2 == 0:
                    act(lambda e, k=k: e.activation(out=hT[:, ht, k, :], in_=v[:, k, :], func=AF.Identity,
                                                    scale=ATt[:, l, c, k:k + 1], bias=modT[l][:, k, c:c + 1]),
                        [rPS[b], rAT, rmodT[l]], [rhT[ht][k]])
                else:
                    dve(lambda e, k=k: e.tensor_scalar(out=hT[:, ht, k, :], in0=v[:, k, :], scalar1=ATt[:, l, c, k:k + 1],
                                                       scalar2=modT[l][:, k, c:c + 1], op0=ALU.mult, op1=ALU.add),
                        [rPS[b], rAT, rmodT[l]], [rhT[ht][k]])

        def consume_K(G, l, ci, t, b):
            rope = G["rope"]
            slot = G["kslot"][t]
            rps = [rPS[b]]
            if ci == 0:
                dst = None if rope else G["o_dk"](l, t)
                yield from k_diff(PS[b][:], rps, slot, (t if rope else None), dst)
            elif ci == 1:
                dst = None if rope else G["o_dv"](l, t)
                yield from v_diff(PS[b][:], rps, slot, dst)
            elif ci == 2:
                yield from k_mla(l, PS[b][:, 0:128], PS[b][:, 128:160], rps, slot, (t if rope else None), True,
                      None if rope else G["o_ckv"](l, t), None if rope else G["o_kr"](l, t))
            else:
                xi = scrRR.next()
                act(lambda e: e.copy(out=scr[xi][:, 0:256], in_=PS[b][:, 0:256]), rps, [rscr[xi]])
                ui = b256RR.next()
                dve(lambda e: e.tensor_tensor(out=b256[ui][:], in0=PS[b][:, 256:512], in1=scr[xi][:, 0:256], op=ALU.mult),
                    rps + [rscr[xi]], [rb256[ui]])
                yield
                bt, v = transposes([b256[ui][:, k * 128:(k + 1) * 128] for k in range(2)], [rb256[ui]])
                sq, hf = G["us"] + t // 2, t % 2
                act(lambda e: e.copy(out=uT[:, sq, :, 1 + hf * 128:1 + (hf + 1) * 128], in_=v[:, 0:2, :]),
                    [rPS[bt]], [ruT[2 * G["us"] + t]])

        def consume_Q(G, l, ci, t, b):
            rope = G["rope"]
            rps = [rPS[b]]
            if False:
                yield
            if ci == 4:
                ss, rss = st_alloc()
                act(lambda e: e.activation(out=junk[:, 0:256], in_=PS[b][:, 0:256], func=AF.Square, accum_out=ss),
                    rps + [rss], [rjunk, rss])
                la_, rla_ = st_alloc()
                rs, rrs = st_alloc()
                qi = b256RR.next()
                dve(lambda e: e.tensor_copy(out=b256[qi][:], in_=PS[b][:, 0:256]), rps, [rb256[qi]])
                si = scrRR.next()
                sigmoid_from(PS[b][:, 256:512], rPS[b], scr[si][:, 0:256], rscr[si], 256)
                dve(lambda e: e.tensor_tensor(out=SZ[:, t, 0:256], in0=PS[b][:, 256:512], in1=scr[si][:, 0:256],
                                              op=ALU.mult), rps + [rscr[si]], [rSZ[t]])
                yield
                act(lambda e: e.activation(out=la_, in_=ss, func=AF.Ln, scale=1.0 / 256, bias=cst[:, 0:1]), [rss, rcst], [rla_])
                bt, v = transposes([b256[qi][:, k * 128:(k + 1) * 128] for k in range(2)], [rb256[qi]])
                ci_ = cqRR.next()
                act(lambda e: e.copy(out=cqT[ci_][:], in_=v[:, 0:2, :]), [rPS[bt]], [rcqT[ci_]])
                yield
                act(lambda e: e.activation(out=rs, in_=la_, func=AF.Exp, scale=-0.5), [rla_], [rrs])
                b5 = 5
                for kc in range(2):
                    pe(lambda e, kc=kc: e.matmul(PS[b5][:, 0:384], lhsT=cqT[ci_][:, kc, :], rhs=Wuq[l][:, kc, :],
                                                 start=(kc == 0), stop=(kc == 1)), [rcqT[ci_], rWuq[l]], [rPS[b5]],
                       signal=(kc == 1))
                qv = PS[b5][:, 0:384].rearrange("p (h d) -> p h d", h=4)
                fi = QfRR.next()
                if not rope:
                    dve(lambda e: e.tensor_scalar(out=Qf[fi][:], in0=qv, scalar1=rs, scalar2=None, op0=ALU.mult),
                        [rPS[b5], rrs], [rQf[fi]])
                else:
                    dve(lambda e: e.tensor_scalar(out=Qf[fi][:, :, 0:64], in0=qv[:, :, 0:64], scalar1=rs,
                                                  scalar2=None, op0=ALU.mult), [rPS[b5], rrs], [rQf[fi]])
                    rope_apply(qv[:, :, 64:96], [rPS[b5]], 4, 32, ropm, rropm, t, Qf[fi][:, :, 64:96], rQf[fi],
                               pre_scale=(rs, rrs))
                yield
                b6, v6 = transposes([Qf[fi][:, h, :] for h in range(4)], [rQf[fi]])
                act(lambda e: e.copy(out=QT[0:96, :, t * 128:(t + 1) * 128], in_=v6[0:96, 0:4, :]), [rPS[b6]], [rQT[t]])
            elif ci == 5:
                bi = b512RR.next()
                if not rope:
                    dve(lambda e: e.tensor_copy(out=b512[bi][:], in_=PS[b][:]), rps, [rb512[bi]])
                else:
                    rope_apply(PS[b][:].rearrange("p (g d) -> p g d", g=8), rps, 8, 64, ropd, rropd, t,
                               b512[bi][:].rearrange("p (g d) -> p g d", g=8), rb512[bi])
                yield
                bt, v = transposes([b512[bi][:, h * 128:(h + 1) * 128] for h in range(4)], [rb512[bi]])
                act(lambda e: e.copy(out=QdT[:, :, t * 128:(t + 1) * 128], in_=v[:, 0:4, :]), [rPS[bt]], [rQdT[t]])
            elif ci == 6:
                si = scrRR.next()
                sigmoid_from(PS[b][:, 256:512], rPS[b], scr[si][:, 0:256], rscr[si], 256)
                dve(lambda e: e.tensor_tensor(out=scr[si][:, 0:256], in0=PS[b][:, 256:512], in1=scr[si][:, 0:256],
                                              op=ALU.mult), rps + [rscr[si]], [rscr[si]])
                gi = b256RR.next()
                dve(lambda e: e.tensor_tensor(out=b256[gi][:], in0=PS[b][:, 0:256], in1=scr[si][:, 0:256], op=ALU.mult),
                    rps + [rscr[si]], [rb256[gi]])
                yield
                bt, v = transposes([b256[gi][:, k * 128:(k + 1) * 128] for k in range(2)], [rb256[gi]])
                act(lambda e: e.copy(out=gT[:, :, t * 128:(t + 1) * 128], in_=v[:, 0:2, :]), [rPS[bt]], [rgT[t]])
            else:
                si = scrRR.next()
                sigmoid_from(PS[b][:], rPS[b], scr[si][:], rscr[si], 512)
                dve(lambda e: e.tensor_tensor(out=scr[si][:], in0=PS[b][:], in1=scr[si][:], op=ALU.mult),
                    rps + [rscr[si]], [rscr[si]])
                dve(lambda e: e.tensor_tensor(out=SZ[:, t, 256:768].rearrange("p (h d) -> p h d", h=4),
                                              in0=scr[si][:].rearrange("p (h d) -> p h d", h=4),
                                              in1=subb[l][:].unsqueeze(1).to_broadcast([128, 4, 128]), op=ALU.mult),
                    [rscr[si], rsubb[l]], [rSZ[t]])

        def step_all(lst):
            for g_ in list(lst):
                try:
                    next(g_)
                except StopIteration:
                    lst.remove(g_)

        def stage_inproj(G, l, chunks, extra=None, extra_delay=0):
            for _ in inproj_gen(G, l, chunks, extra=extra, extra_delay=extra_delay):
                pass

        def inproj_gen(G, l, chunks, extra=None, brr=None, extra_delay=0):
            nt = G["nt"]
            items = [(ci, t) for ci in chunks for t in range(nt)]
            assert PASSES[pstate["i"]] == (l, chunks), (pstate["i"], l, chunks)
            wbuf = dict(pstate["pre"])
            pstate["pre"] = {}
            if chunks[0] not in wbuf:
                wbuf[chunks[0]] = load_chunk(l, chunks[0])

            def issue(j):
                ci, t = items[j]
                if t == 0:
                    idx = chunks.index(ci)
                    if idx + 1 < len(chunks) and chunks[idx + 1] not in wbuf:
                        wbuf[chunks[idx + 1]] = load_chunk(l, chunks[idx + 1])
                return inproj(wbuf[ci], ci, G["ht"][t], brr)

            active = list(extra) if (extra and extra_delay == 0) else []
            for j in range(len(items)):
                b = issue(j)
                if extra and extra_delay > 0 and j >= extra_delay and (j - extra_delay) < len(extra):
                    active.append(extra[j - extra_delay])
                step_all(active)
                ci, t = items[j]
                gen = consume_K(G, l, ci, t, b) if ci < 4 else consume_Q(G, l, ci, t, b)
                try:
                    next(gen)
                    active.append(gen)
                except StopIteration:
                    pass
                yield
            while active:
                step_all(active)
                yield
            pstate["i"] += 1
            if pstate["i"] < len(PASSES):
                ln, chn = PASSES[pstate["i"]]
                for cj in chn[0:2]:
                    pstate["pre"][cj] = load_chunk(ln, cj)

        def stage_K(G, l):
            stage_inproj(G, l, [0, 1, 2, 3])

        def stage_Q(G, l):
            stage_inproj(G, l, [4, 5, 6, 7])

        def conv_seq(G, l, sq):
            t0 = 2 * sq
            for blk in range(2):
                yi = scrRR.next()
                us = G["us"] + sq
                u = uT[:, us, blk, :]
                rd = [ruT[2 * us], ruT[2 * us + 1], ruH[us], rcw[l]]
                dve(lambda e: e.tensor_scalar(out=scr[yi][:, 0:256], in0=u[:, 0:256], scalar1=cw[l][:, blk * 3:blk * 3 + 1],
                                              scalar2=None, op0=ALU.mult), rd, [rscr[yi]])
                for j in (1, 2):
                    dve(lambda e, j=j: e.scalar_tensor_tensor(out=scr[yi][:, 0:256], in0=u[:, j:j + 256],
                                                              scalar=cw[l][:, blk * 3 + j:blk * 3 + j + 1], in1=scr[yi][:, 0:256],
                                                              op0=ALU.mult, op1=ALU.add), rd + [rscr[yi]], [rscr[yi]])
                g_ = gT[:, blk, t0 * 128:(t0 + 2) * 128]
                dve(lambda e: e.tensor_tensor(out=g_, in0=scr[yi][:, 0:256], in1=g_, op=ALU.mult),
                    [rscr[yi], rgT[t0], rgT[t0 + 1]], [rgT[t0], rgT[t0 + 1]])

        def attention_seq(G, l, sq, slots, extra=None, filler=None):
            t0 = 2 * sq
            qc = slice(t0 * 128, t0 * 128 + 256)
            ns = len(slots)
            units = [("m", h, 0) for h in range(4)] + [("d", h, a) for h in range(4) for a in range(2)]
            ebuf = {}

            def accbanks(u):
                if u[0] == "m" or filler is not None:
                    return (6, 7)
                return (0, 1) if u[1] % 2 == 0 else (6, 7)

            deep = (filler is None and ns == 2)

            def s_phase(u):
                kind, h, a = u
                ei = (ERR3 if deep else ERR).next()
                ebuf[u] = ei
                for p0 in range(0, ns, 2):
                    b = (sRR3 if deep else sRR).next()
                    pair = slots[p0:p0 + 2]
                    for j, slot in enumerate(pair):
                        if kind == "m":
                            pe(lambda e, b=b, slot=slot, j=j: e.matmul(
                                PS[b][:, j * 256:(j + 1) * 256], lhsT=KT[0:96, h, slot * 128:(slot + 1) * 128],
                                rhs=QT[0:96, h, qc], start=True, stop=True),
                               [rKT[slot], rQT[t0], rQT[t0 + 1]], [rPS[b]], signal=(j == len(pair) - 1))
                            sc = MLA_SCALE
                        else:
                            pe(lambda e, b=b, slot=slot, j=j: e.matmul(
                                PS[b][:, j * 256:(j + 1) * 256], lhsT=KdT[a * 64:(a + 1) * 64, h, slot * 128:(slot + 1) * 128],
                                rhs=QdT[a * 64:(a + 1) * 64, h, qc], start=True, stop=True),
                               [rKdT[slot], rQdT[t0], rQdT[t0 + 1]], [rPS[b]], signal=(j == len(pair) - 1))
                            sc = DIFF_SCALE
                    w = 256 * len(pair)
                    act(lambda e, b=b, p0=p0, sc=sc, w=w: e.activation(
                        out=E[ei][:, p0:p0 + len(pair), :].rearrange("p s q -> p (s q)"), in_=PS[b][:, 0:w], func=AF.Exp,
                        scale=sc), [rPS[b]], [rE[ei]])

            def av_phase(u):
                kind, h, a = u
                ei = ebuf[u]
                ab = accbanks(u)
                for qt in range(2):
                    for si, slot in enumerate(slots):
                        if kind == "m":
                            pe(lambda e, qt=qt, si=si, slot=slot: e.matmul(
                                PS[ab[qt]][:, h * 65:(h + 1) * 65], lhsT=E[ei][:, si, qt * 128:(qt + 1) * 128],
                                rhs=Vm[:, slot, h, :], start=(si == 0), stop=(si == ns - 1)),
                               [rE[ei], rVm[slot]], [rPS[ab[qt]]], signal=(si == ns - 1))
                        else:
                            pe(lambda e, qt=qt, si=si, slot=slot: e.matmul(
                                PS[ab[qt]][:, a * 129:(a + 1) * 129], lhsT=E[ei][:, si, qt * 128:(qt + 1) * 128],
                                rhs=Vd[:, slot, h, :], start=(si == 0), stop=(si == ns - 1)),
                               [rE[ei], rVd[slot]], [rPS[ab[qt]]], signal=(si == ns - 1))

            def post(u):
                kind, h, a = u
                ab = accbanks(u)
                if False:
                    yield
                if kind == "m" and h == 3:
                    avs = [PS[ab[qt]][:, 0:260].rearrange("p (h d) -> p h d", h=4) for qt in range(2)]
                    rzs = [st_alloc(4) for qt in range(2)]
                    sis = [scrRR.next() for qt in range(2)]
                    for qt in range(2):
                        dve(lambda e, qt=qt: e.reciprocal(out=rzs[qt][0].unsqueeze(2), in_=avs[qt][:, :, 64:65]),
                            [rPS[ab[qt]]], [rzs[qt][1]])
                    for qt in range(2):
                        ov = scr[sis[qt]][:, 0:256].rearrange("p (h d) -> p h d", h=4)
                        dve(lambda e, qt=qt, ov=ov: e.tensor_tensor(out=ov, in0=avs[qt][:, :, 0:64],
                                                                   in1=rzs[qt][0].unsqueeze(2).to_broadcast([128, 4, 64]),
                                                                   op=ALU.mult), [rPS[ab[qt]], rzs[qt][1]], [rscr[sis[qt]]])
                    for qt in range(2):
                        t = t0 + qt
                        dve(lambda e, t=t, qt=qt: e.tensor_tensor(out=mix[t][:, 0:256], in0=scr[sis[qt]][:, 0:256],
                                                                 in1=SZ[:, t, 0:256], op=ALU.mult),
                            [rscr[sis[qt]], rSZ[t]], [rmix[t][0]])
                if kind == "d" and a == 1:
                    st_ = []
                    avs = [PS[ab[qt]][:, 0:258].rearrange("p (a d) -> p a d", a=2) for qt in range(2)]
                    rzs = [st_alloc(2) for qt in range(2)]
                    pis = [phRR.next() for qt in range(2)]
                    for qt in range(2):
                        dve(lambda e, qt=qt: e.reciprocal(out=rzs[qt][0].unsqueeze(2), in_=avs[qt][:, :, 128:129]),
                            [rPS[ab[qt]]], [rzs[qt][1]])
                    for qt in range(2):
                        dve(lambda e, qt=qt: e.tensor_tensor(out=rzs[qt][0][:, 1:2], in0=rzs[qt][0][:, 1:2], in1=lamt[l][:, 3:4],
                                                            op=ALU.mult), [rzs[qt][1], rlamt[l]], [rzs[qt][1]])
                    for qt in range(2):
                        dve(lambda e, qt=qt: e.tensor_scalar(out=ph[pis[qt]][:, 0:128], in0=avs[qt][:, 0, 0:128],
                                                            scalar1=rzs[qt][0][:, 0:1], scalar2=None, op0=ALU.mult),
                            [rPS[ab[qt]], rzs[qt][1]], [rph[pis[qt]]])
                    for qt in range(2):
                        dve(lambda e, qt=qt: e.scalar_tensor_tensor(out=ph[pis[qt]][:, 128:256], in0=avs[qt][:, 1, 0:128],
                                                                   scalar=rzs[qt][0][:, 1:2], in1=ph[pis[qt]][:, 0:128],
                                                                   op0=ALU.mult, op1=ALU.add),
                            [rPS[ab[qt]], rzs[qt][1], rph[pis[qt]]], [rph[pis[qt]]])
                    for qt in range(2):
                        st_.append((t0 + qt, pis[qt]))
                    yield
                    ss, rss = st_alloc(2)
                    for i_, (t, pi) in enumerate(st_):
                        act(lambda e, pi=pi, i_=i_: e.activation(out=junk[:, 0:128], in_=ph[pi][:, 128:256], func=AF.Square,
                                                                 accum_out=ss[:, i_:i_ + 1]), [rph[pi], rss], [rjunk, rss])
                    yield
                    la_, rla_ = st_alloc(2)
                    rs, rrs = st_alloc(2)
                    act(lambda e: e.activation(out=la_, in_=ss, func=AF.Ln, scale=1.0 / 128, bias=cst[:, 0:1]), [rss, rcst], [rla_])
                    yield
                    act(lambda e: e.activation(out=rs, in_=la_, func=AF.Exp, scale=-0.5), [rla_], [rrs])
                    yield
                    c0 = 256 + h * 128
                    for i_, (t, pi) in enumerate(st_):
                        dve(lambda e, t=t, pi=pi, i_=i_: e.scalar_tensor_tensor(out=mix[t][:, c0:c0 + 128], in0=ph[pi][:, 128:256],
                                                                               scalar=rs[:, i_:i_ + 1], in1=SZ[:, t, c0:c0 + 128],
                                                                               op0=ALU.mult, op1=ALU.mult),
                            [rph[pi], rrs, rSZ[t]], [rmix[t][1 + h]])

            la = 2 if deep else 1
            if deep:
                trRR.items = [2]
            for j in range(la):
                s_phase(units[j])
            pend = list(extra) if extra else []
            for i, u in enumerate(units):
                if i + la < len(units):
                    s_phase(units[i + la])
                av_phase(u)
                step_all(pend)
                if filler:
                    step_all(filler)
                gen = post(u)
                try:
                    next(gen)
                    pend.append(gen)
                except StopIteration:
                    pass
            while pend:
                for g_ in list(pend):
                    try:
                        next(g_)
                    except StopIteration:
                        pend.remove(g_)
            trRR.items = [2, 3]
            for qt in range(2):
                t = t0 + qt
                mi = qt
                b, v = transposes([mix[t][:, k * 128:(k + 1) * 128] for k in range(6)], rmix[t])
                act(lambda e: e.copy(out=mixT[mi][:], in_=v[:, 0:6, :]), [rPS[b]], [rmixT[mi]])

        def outproj_tile(G, l, t, mi):
            xi = G["X"][t]
            banks = [inRR.next(), inRR.next()]
            for half in range(2):
                b = banks[half]
                for kc in range(8):
                    if kc < 2:
                        lt, rl = mixT[mi][:, kc, :], rmixT[mi]
                    elif kc < 4:
                        lt, rl = gT[:, kc - 2, t * 128:(t + 1) * 128], rgT[t]
                    else:
                        lt, rl = mixT[mi][:, kc - 2, :], rmixT[mi]
                    pe(lambda e, lt=lt, kc=kc: e.matmul(PS[b][:], lhsT=lt, rhs=Wo[l][:, kc, half * 512:(half + 1) * 512],
                                                        start=(kc == 0), stop=(kc == 7)), [rl, rWo[l]], [rPS[b]],
                       signal=(kc == 7))
                gi = scrRR.next()
                dve(lambda e, b=b, half=half, gi=gi: e.tensor_tensor(out=scr[gi][:], in0=PS[b][:],
                                                                    in1=gateB[G["c"]][:, half * 512:(half + 1) * 512], op=ALU.mult),
                    [rPS[b], rgate[G["c"]]], [rscr[gi]])
                dve(lambda e, half=half, gi=gi: e.tensor_tensor(out=X[xi][:, half * 512:(half + 1) * 512], in0=scr[gi][:],
                                                               in1=X[xi][:, half * 512:(half + 1) * 512], op=ALU.add),
                    [rscr[gi], rX[xi]], [rX[xi]])
            if l == DEPTH - 1:
                ss, rss = st_alloc()
                act(lambda e: e.activation(out=E[0][:, 0:4, :].rearrange("p a c -> p (a c)"), in_=X[xi][:], func=AF.Square, accum_out=ss), [rX[xi], rss], [rE[0], rss])
                rs, rrs = rstd_from_ss(ss, rss, D)
                for half in range(2):
                    yi = scrRR.next()
                    cs = slice(half * 512, (half + 1) * 512)
                    dve(lambda e, yi=yi, cs=cs: e.scalar_tensor_tensor(out=scr[yi][:], in0=X[xi][:, cs], scalar=rs, in1=Abc[:, cs],
                                                                      op0=ALU.mult, op1=ALU.mult), [rX[xi], rrs, rAbc], [rscr[yi]])
                    fw.dma("sp", G["ydst"][t][:, cs], scr[yi][:], reads=[rscr[yi]], sem=rscr[yi].sem)
                return None
            gen = norm_gen(G, l + 1, t)
            next(gen)
            return gen

        def prompt_group(g):
            G = {"nt": 4, "rope": False, "X": [0, 1, 2, 3], "kslot": [0, 1, 2, 3], "ht": [0, 1, 2, 3], "us": 0, "c": 0}
            G["xsrc"] = [xp[2 * g + t // 2, (t % 2) * 128:(t % 2 + 1) * 128, :] for t in range(4)]
            G["ydst"] = [y_p[2 * g + t // 2, (t % 2) * 128:(t % 2 + 1) * 128, :] for t in range(4)]
            rows = lambda t: slice((t % 2) * 128, (t % 2 + 1) * 128)
            G["o_dk"] = lambda l, t: o_dk[2 * g + t // 2, l, rows(t), :]
            G["o_dv"] = lambda l, t: o_dv[2 * g + t // 2, l, rows(t), :]
            G["o_ckv"] = lambda l, t: o_ckv[2 * g + t // 2, l, rows(t), :]
            G["o_kr"] = lambda l, t: o_kr[2 * g + t // 2, l, rows(t), :]
            return G

        PG = [prompt_group(0), prompt_group(1)]

        xsem2 = {}

        def prefetch_x(G, q="sp"):
            for t in range(G["nt"]):
                xi = G["X"][t]
                sm = rX[xi].sem
                if q == "pool":
                    if xi not in xsem2:
                        xsem2[xi] = fw.dma_sem()
                    sm = xsem2[xi]
                fw.dma(q, X[xi][:], G["xsrc"][t], writes=[rX[xi]], sem=sm)
            G["xpre"] = True

        def group_norm_gens(g):
            G = PG[g]
            gens = []
            for t in range(4):
                gen = norm_gen(G, 0, t)
                next(gen)
                gens.append(gen)
            return gens

        def run_prompt_group(g, fill_gen=None, first_extra=None):
            G = PG[g]
            carry = list(first_extra) if first_extra else []
            for l in range(DEPTH):
                st_reset()
                phase_gate(l, 0)
                stage_inproj(G, l, [0, 1, 2, 3, 4, 5, 6, 7], extra=carry, extra_delay=1)
                carry = []
                trigger_ag()
                filler = [fill_gen] if (l == 0 and fill_gen is not None) else None
                for sq in range(2):
                    conv_seq(G, l, sq)
                    attention_seq(G, l, sq, [2 * sq, 2 * sq + 1], extra=carry, filler=filler)
                    carry = []
                    if sq == 1 and filler:
                        while filler:
                            step_all(filler)
                    for qt in range(2):
                        gen = outproj_tile(G, l, 2 * sq + qt, qt)
                        if gen is not None:
                            carry.append(gen)
            assert not carry

        GS = {"nt": 2, "rope": True, "X": [4, 5], "kslot": [8, 9], "ht": [4, 5], "us": 2, "c": 1}
        GS["xsrc"] = [xs[t * 128:(t + 1) * 128, :] for t in range(2)]
        GS["ydst"] = [y_s[t * 128:(t + 1) * 128, :] for t in range(2)]
        rloc = [Res("kvloc%d" % l) for l in range(2)]
        rall = [Res("kvall%d" % l) for l in range(2)]
        for r in rloc:
            r.sem = fw.dma_sem()
        cc_sem = [fw.es.enter_context(nc.semaphore("cc_sem%d" % l)) for l in range(2)]

        fillRR = RR([0, 1])

        def sample_front_gen(l):
            yield from inproj_gen(GS, l, [0, 1, 2, 3], brr=fillRR)
            dve(lambda e: e.tensor_copy(out=ubnd[:, :, 0:1], in_=uT[:, 2, :, 1:2]), [ruT[4]], [rubnd])
            dve(lambda e: e.tensor_copy(out=ubnd[:, :, 1:2], in_=uT[:, 2, :, 256:257]), [ruT[5]], [rubnd])
            loc = kv_loc[l].ap()
            s = rloc[l].sem
            fw.dma("sp", loc[:, O_KT:O_KT + 1024].rearrange("p (h k) -> p h k", h=4), KT[:, :, 1024:1280],
                   reads=[rKT[8], rKT[9]], writes=[], sem=s)
            fw.dma("sp", loc[:, O_KD:O_KD + 1024].rearrange("p (h k) -> p h k", h=4), KdT[:, :, 1024:1280],
                   reads=[rKdT[8], rKdT[9]], writes=[], sem=s)
            fw.dma("sp", loc[:, O_VM:O_VM + 520], Vm[:, 8:10, :, :].rearrange("p s h d -> p (s h d)"),
                   reads=[rVm[8], rVm[9]], writes=[], sem=s)
            fw.dma("sp", loc[:, O_VD:O_VD + 1032], Vd[:, 8:10, :, :].rearrange("p s h d -> p (s h d)"),
                   reads=[rVd[8], rVd[9]], writes=[], sem=s)
            fw.dma("sp", loc[:, O_UB:O_UB + 4], ubnd[:].rearrange("p b c -> p (b c)"), reads=[rubnd], writes=[rloc[l]], sem=s)
            for r_ in (rKT[8], rKT[9], rKdT[8], rKdT[9], rVm[8], rVm[9], rVd[8], rVd[9], rubnd):
                r_.rd[s] = fw.cnt[s]
            pending_ag.append(l)

        pending_ag = []

        def trigger_ag():
            while pending_ag:
                l = pending_ag.pop(0)
                fw._waits("pool", [rloc[l]], [rall[l]])
                nc.gpsimd.collective_compute("AllGather", ALU.bypass, replica_groups=[[0, 1, 2, 3], [4, 5, 6, 7]],
                                             ins=[kv_loc[l].ap().opt()], outs=[kv_all[l].ap().opt()]).then_inc(cc_sem[l])

        def sample_ctx(l):
            for kt in range(2):
                rows = slice(kt * 128, (kt + 1) * 128)
                ci = scrRR.next()
                fw.dma("sp", scr[ci][:], dk_c[l, rows, :], writes=[rscr[ci]], sem=rscr[ci].sem)
                for _ in k_diff(scr[ci][:], [rscr[ci]], 4 + kt, None, None):
                    pass
                ci = scrRR.next()
                fw.dma("sp", scr[ci][:], dv_c[l, rows, :], writes=[rscr[ci]], sem=rscr[ci].sem)
                for _ in v_diff(scr[ci][:], [rscr[ci]], 4 + kt, None):
                    pass
                ci = scrRR.next()
                fw.dma("sp", scr[ci][:, 0:128], ckv_c[l, rows, :], writes=[rscr[ci]], sem=rscr[ci].sem)
                fw.dma("sp", scr[ci][:, 128:160], kr_c[l, rows, :], writes=[rscr[ci]], sem=rscr[ci].sem)
                for _ in k_mla(l, scr[ci][:, 0:128], scr[ci][:, 128:160], [rscr[ci]], 4 + kt, None, False, None, None):
                    pass

        def sample_back(l, extra=None):
            st_reset()
            phase_gate(l, 1)
            nc.sync.wait_ge(cc_sem[l], 1)
            al = kv_all[l].ap()
            for r in range(4):
                rw = al[r * 128:(r + 1) * 128, :]
                s0 = (0, 2, 6, 8)[r]
                fw.dma("sp", KT[:, :, s0 * 128:(s0 + 2) * 128], rw[:, O_KT:O_KT + 1024].rearrange("p (h k) -> p h k", h=4),
                       writes=[rKT[s0], rKT[s0 + 1]], sem=rKT[s0].sem)
                fw.dma("sp", KdT[:, :, s0 * 128:(s0 + 2) * 128], rw[:, O_KD:O_KD + 1024].rearrange("p (h k) -> p h k", h=4),
                       writes=[rKdT[s0], rKdT[s0 + 1]], sem=rKdT[s0].sem)
                fw.dma("sp", Vm[:, s0:s0 + 2, :, :].rearrange("p s h d -> p (s h d)"), rw[:, O_VM:O_VM + 520],
                       writes=[rVm[s0], rVm[s0 + 1]], sem=rVm[s0].sem)
                fw.dma("sp", Vd[:, s0:s0 + 2, :, :].rearrange("p s h d -> p (s h d)"), rw[:, O_VD:O_VD + 1032],
                       writes=[rVd[s0], rVd[s0 + 1]], sem=rVd[s0].sem)
                fw.dma("sp", uhal[:, r, :, :].rearrange("p b c -> p (b c)"), rw[:, O_UB:O_UB + 4], writes=[ruhal], sem=ruhal.sem)
            dve(lambda e: e.tensor_copy(out=uhf[:], in_=uhal[:]), [ruhal], [ruhf])
            for side in range(2):
                col = 1 - side
                dve(lambda e: e.tensor_scalar(out=uacc[:, :, side:side + 1], in0=uhf[:, 0, :, col:col + 1],
                                              scalar1=selh[:, side * 4:side * 4 + 1], scalar2=None, op0=ALU.mult),
                    [ruhf, rsel], [ruacc])
                for r in range(1, 4):
                    dve(lambda e, r=r: e.scalar_tensor_tensor(out=uacc[:, :, side:side + 1], in0=uhf[:, r, :, col:col + 1],
                                                              scalar=selh[:, side * 4 + r:side * 4 + r + 1],
                                                              in1=uacc[:, :, side:side + 1], op0=ALU.mult, op1=ALU.add),
                        [ruhf, rsel, ruacc], [ruacc])
            dve(lambda e: e.tensor_copy(out=uT[:, 2, :, 0:1], in_=uacc[:, :, 0:1]), [ruacc], [ruH[2]])
            dve(lambda e: e.tensor_copy(out=uT[:, 2, :, 257:258], in_=uacc[:, :, 1:2]), [ruacc], [ruH[2]])
            stage_inproj(GS, l, [4, 5, 6, 7], extra=extra, extra_delay=3)
            conv_seq(GS, l, 0)
            attention_seq(GS, l, 0, list(range(10)))
            gens = []
            for qt in range(2):
                gen = outproj_tile(GS, l, qt, qt)
                if gen is not None:
                    gens.append(gen)
            return gens

        def zero_prompt_halos():
            dve(lambda e: e.memset(uT[:, 0, :, 0:1], 0.0), [], [ruH[0]])
            dve(lambda e: e.memset(uT[:, 0, :, 257:258], 0.0), [], [ruH[0]])

        prefetch_x(GS)
        prefetch_x(PG[0])
        sample_ctx(0)
        gA = group_norm_gens(0)
        gS0 = []
        for t in range(2):
            g_ = norm_gen(GS, 0, t)
            next(g_)
            gS0.append(g_)
        finish_mod_a()
        finish_mod_b()
        for g_ in gA[0:2]:
            for _ in g_:
                pass
        run_prompt_group(0, fill_gen=sample_front_gen(0), first_extra=gA[2:4] + gS0)
        prefetch_x(PG[1], q="pool")
        gB = group_norm_gens(1)
        gS = sample_back(0, extra=gB)
        step_all(gS)
        sample_ctx(1)
        while gS:
            step_all(gS)
        run_prompt_group(1, fill_gen=sample_front_gen(1))
        sample_back(1)

        _DBG["sbuf_free"] = nc.sbuf_bytes_remaining
        fw.wait_all("sp", [k for k in fw.sems if k.startswith("dma")])
    return nc


def _rope_tables(pos, rot_dim):
    half = rot_dim // 2
    inv = (10000.0 ** (-(np.arange(0, half, 2, dtype=np.float32) / np.float32(half)))).astype(np.float32)
    row = (pos // 64).astype(np.float32)
    col = (pos % 64).astype(np.float32)
    ang_r = row[:, None] * inv
    ang_c = col[:, None] * inv
    ang = np.concatenate([ang_r, ang_r, ang_c, ang_c], axis=-1).astype(np.float32)
    cos = np.cos(ang).astype(np.float32)
    sin = np.sin(ang).astype(np.float32)
    q = half // 2
    sign = np.ones(rot_dim, np.float32)
    sign[0:q] = -1.0
    sign[half:half + q] = -1.0
    return cos, sin * sign


_COLS = None


def _perm_cols():
    offs = np.cumsum([0, 256, 128, 32, 256, 256, 256, 512, 512, 512, 1024])
    c_q, c_kv, k_r, cb, cc, cx, dq, dk, dv, z = [np.arange(offs[i], offs[i + 1]) for i in range(10)]
    return np.concatenate([dk, dv, c_kv, k_r, cc, cx, c_q, z[0:256], dq, cb, z[256:512], z[512:1024]])


_NC_CACHE = {}


def kernel(x_prompt, x_sample, c, cache_mla_ckv, cache_mla_krope, cache_diff_k, cache_diff_v,
           c_ctx, norm_g, ada_w, ada_b, w_in, mla_q_norm, w_uq, mla_kv_norm, w_ukv,
           conv_w, diff_lambda, diff_subln, w_out, final_norm):
    f = lambda a: np.ascontiguousarray(np.asarray(a, dtype=np.float32))
    x_prompt, x_sample, c, c_ctx = f(x_prompt), f(x_sample), f(c), f(c_ctx)
    cols = _perm_cols()
    shared = {
        "normgT": f(np.asarray(norm_g).reshape(2, 8, 128).transpose(0, 2, 1)),
        "final_norm": f(np.asarray(final_norm).reshape(1, D)),
        "w_in_p": f(np.asarray(w_in)[:, :, cols]),
        "qnT": f(np.asarray(mla_q_norm).reshape(2, 2, 128).transpose(0, 2, 1)),
        "w_uq": f(w_uq),
        "kv_norm": f(np.asarray(mla_kv_norm).reshape(2, 1, 128)),
        "w_ukv": f(w_ukv),
        "conv_wT": f(np.asarray(conv_w).reshape(2, 3, 2, 128).transpose(0, 3, 2, 1).reshape(2, 128, 6)),
        "diff_lambda": f(np.asarray(diff_lambda).reshape(2, 1, 256)),
        "diff_subln": f(np.asarray(diff_subln).reshape(2, 1, 128)),
        "w_out": f(w_out),
        "ident": np.eye(128, dtype=np.float32),
    }
    in_maps = []
    for i in range(NCORES):
        b, j = i // 4, i % 4
        pos = 256 * j + np.arange(256)
        cd, sd = _rope_tables(pos, 64)
        cm, sm = _rope_tables(pos, 32)
        rope_d = np.stack([cd, sd], axis=1).reshape(2, 128, 2, 64).transpose(1, 0, 2, 3)
        rope_m = np.stack([cm, sm], axis=1).reshape(2, 128, 2, 32).transpose(1, 0, 2, 3)
        sel = np.zeros((128, 8), np.float32)
        if j > 0:
            sel[:, j - 1] = 1.0
        if j < 3:
            sel[:, 4 + j + 1] = 1.0
        condT = np.stack([c_ctx.reshape(8, 128).T, c[b].reshape(8, 128).T], axis=-1)
        m = dict(shared)
        m.update({
            "xp": f(x_prompt[4 * i:4 * i + 4]),
            "xs": f(x_sample[b, 256 * j:256 * j + 256]),
            "condT": f(condT),
            "ckv_c": f(np.asarray(cache_mla_ckv)[b]),
            "kr_c": f(np.asarray(cache_mla_krope)[b]),
            "dk_c": f(np.asarray(cache_diff_k)[b].reshape(2, 256, 512)),
            "dv_c": f(np.asarray(cache_diff_v)[b].reshape(2, 256, 512)),
            "rope_d": f(rope_d), "rope_m": f(rope_m), "halo_sel": sel,
            "ada_w_s": f(np.asarray(ada_w)[:, :, 768 * j:768 * (j + 1)]),
            "ada_bT_s": f(np.asarray(ada_b)[:, 768 * j:768 * (j + 1)].reshape(2, 6, 128).transpose(0, 2, 1)),
        })
        in_maps.append(m)
    if "nc" not in _NC_CACHE:
        _NC_CACHE["nc"] = build_nc()
    res = run_bass_kernel_spmd(_NC_CACHE["nc"], in_maps, core_ids=list(range(NCORES)))
    R = res.results
    y_prompt = np.concatenate([R[i]["y_p"] for i in range(NCORES)], axis=0)
    y_sample = np.stack([np.concatenate([R[4 * b + j]["y_s"] for j in range(4)], axis=0) for b in range(2)], axis=0)
    s_ckv = np.concatenate([R[i]["o_ckv"] for i in range(NCORES)], axis=0)
    s_kr = np.concatenate([R[i]["o_kr"] for i in range(NCORES)], axis=0)
    s_dk = np.concatenate([R[i]["o_dk"] for i in range(NCORES)], axis=0).reshape(32, 2, 256, 4, 2, 64)
    s_dv = np.concatenate([R[i]["o_dv"] for i in range(NCORES)], axis=0).reshape(32, 2, 256, 4, 128)
    out = (y_prompt, y_sample, s_ckv, s_kr, s_dk, s_dv)
    return tuple(np.ascontiguousarray(o, dtype=np.float32) for o in out)
```
